# Optimizing a Trainium2 kernel written in Bass

```python
import math
import jax
import jax.numpy as jnp
from jax import lax
import numpy as np

D_MODEL = 2048
BATCH = 32
SEQ = 256
DEPTH = 2
DEC_BATCH = 4
DEC_SEQ = 4096
PAST_LEN = 512

GRID_W = 64
BLOCK = 128
SSD_HEADS = 12
SSD_HEAD_DIM = 64
SSD_WIDTH = SSD_HEADS * SSD_HEAD_DIM
SSD_GROUPS = 2
SSD_D_STATE = 64
SSD_CONV_W = 3
SSD_CHUNK = 128
SSD_CONV_CH = SSD_WIDTH + 2 * SSD_GROUPS * SSD_D_STATE
ATT_HEADS = 12
ATT_KV_HEADS = 4
ATT_GROUP = ATT_HEADS // ATT_KV_HEADS
HEAD_DIM = 64
ATT_WIDTH = ATT_HEADS * HEAD_DIM
ATT_KV_WIDTH = ATT_KV_HEADS * HEAD_DIM
WINDOW = 128
ROPE_PER_AXIS = HEAD_DIM // 4
ROPE_BASE = 10000.0
S5_GROUPS = 32
S5_GROUP_CH = 16
S5_WIDTH = S5_GROUPS * S5_GROUP_CH
S5_P = 64
MIX_WIDTH = SSD_WIDTH + ATT_WIDTH + S5_WIDTH
D_FF = -(-(8 * D_MODEL) // (3 * 256)) * 256
OFF_Z = SSD_WIDTH
OFF_XBC = OFF_Z + SSD_CONV_CH
OFF_DT = OFF_XBC + 2 * SSD_HEADS
OFF_Q = OFF_DT + ATT_WIDTH
OFF_K = OFF_Q + ATT_KV_WIDTH
OFF_V = OFF_K + ATT_KV_WIDTH
D_IN = OFF_V + S5_WIDTH
IN_SPLITS = (OFF_Z, OFF_XBC, OFF_DT, OFF_Q, OFF_K, OFF_V)

kernel_name = 'hybrid_ssd_swa_s5_diffusion_step'


def rmsnorm(x, g, eps=1e-6):
    xf = x.astype(jnp.float32)
    y = xf * lax.rsqrt(jnp.mean(xf * xf, axis=-1, keepdims=True) + eps)
    return (y * g.astype(jnp.float32)).astype(x.dtype)


def dwconv_centred(x, w, b):
    k = w.shape[0]
    y = lax.conv_general_dilated(x, w[:, None, :].astype(x.dtype), window_strides=(1,),
                                 padding=[(k // 2, k // 2)],
                                 dimension_numbers=('NWC', 'WIO', 'NWC'),
                                 feature_group_count=w.shape[1])
    return y + b.astype(x.dtype)


def ssd_scan(x, dt, a_neg, bm, cm, h0):
    b, L, nh, hp = x.shape
    n = bm.shape[-1]
    q = SSD_CHUNK
    nc = L // q
    xc = (x * dt[..., None]).reshape(b, nc, q, nh, hp)
    bc = bm.reshape(b, nc, q, nh, n)
    cc = cm.reshape(b, nc, q, nh, n)
    cum = jnp.cumsum((dt * a_neg).reshape(b, nc, q, nh), axis=2)
    tri = jnp.tril(jnp.ones((q, q), dtype=bool))[None, None, :, :, None]
    seg = cum[:, :, :, None, :] - cum[:, :, None, :, :]
    decay = jnp.exp(jnp.where(tri, seg, -jnp.inf))
    scores = jnp.einsum('bclhn,bcshn->bclsh', cc, bc) * decay
    y_diag = jnp.einsum('bclsh,bcshp->bclhp', scores, xc)
    to_end = jnp.exp(cum[:, :, -1:, :] - cum)
    chunk_states = jnp.einsum('bcshn,bcsh,bcshp->bchpn', bc, to_end, xc)
    chunk_decay = jnp.exp(cum[:, :, -1, :])

    def step(h, inp):
        st, dec = inp
        return h * dec[:, :, None, None] + st, h

    h_final, h_start = lax.scan(step, h0, (jnp.moveaxis(chunk_states, 1, 0),
                                           jnp.moveaxis(chunk_decay, 1, 0)))
    h_start = jnp.moveaxis(h_start, 0, 1)
    y_off = jnp.einsum('bclhn,bchpn->bclhp', cc, h_start) * jnp.exp(cum)[..., None]
    return (y_diag + y_off).reshape(b, L, nh, hp), h_final


def ssd_mixer(z, xbc, dt_raw, P, l, h0_f, h0_b):
    b, L, _ = z.shape
    xbc = jax.nn.silu(dwconv_centred(xbc, P['ssd_conv_w'][l], P['ssd_conv_b'][l])).astype(jnp.float32)
    xs, bm, cm = jnp.split(xbc, [SSD_WIDTH, SSD_WIDTH + SSD_GROUPS * SSD_D_STATE], axis=-1)
    rep = SSD_HEADS // SSD_GROUPS
    xs = xs.reshape(b, L, SSD_HEADS, SSD_HEAD_DIM)
    bm = jnp.repeat(bm.reshape(b, L, SSD_GROUPS, SSD_D_STATE), rep, axis=2)
    cm = jnp.repeat(cm.reshape(b, L, SSD_GROUPS, SSD_D_STATE), rep, axis=2)
    dt = jax.nn.softplus(dt_raw.astype(jnp.float32).reshape(b, L, 2, SSD_HEADS)
                         + P['ssd_dt_bias'][l].astype(jnp.float32))
    a_neg = -jnp.exp(P['ssd_A_log'][l].astype(jnp.float32))
    y_f, h_f = ssd_scan(xs, dt[:, :, 0], a_neg[0], bm, cm, h0_f)
    y_b, h_b = ssd_scan(jnp.flip(xs, axis=1), jnp.flip(dt[:, :, 1], axis=1), a_neg[1],
                        jnp.flip(bm, axis=1), jnp.flip(cm, axis=1), h0_b)
    y = y_f + jnp.flip(y_b, axis=1) + P['ssd_D'][l].astype(jnp.float32)[:, None] * xs
    y = y.reshape(b, L, SSD_WIDTH) * jax.nn.silu(z.astype(jnp.float32))
    return rmsnorm(y, P['ssd_norm'][l]).astype(z.dtype), h_f, h_b


def axial_rope_tables(L):
    rows = L // GRID_W
    row = jnp.repeat(jnp.arange(rows, dtype=jnp.float32), GRID_W)
    col = jnp.tile(jnp.arange(GRID_W, dtype=jnp.float32), rows)
    inv = ROPE_BASE ** (-jnp.arange(ROPE_PER_AXIS, dtype=jnp.float32) / ROPE_PER_AXIS)
    ang = jnp.concatenate([row[:, None] * inv, col[:, None] * inv], axis=-1)
    return jnp.cos(ang), jnp.sin(ang)


def apply_rope(x, cos, sin):
    half = HEAD_DIM // 2
    xf = x.astype(jnp.float32)
    x1, x2 = xf[..., :half], xf[..., half:]
    c = cos[None, :, None, :]
    s = sin[None, :, None, :]
    return jnp.concatenate([x1 * c - x2 * s, x2 * c + x1 * s], axis=-1).astype(x.dtype)


def context_attention(q, k, v, sink):
    b, C, _, d = q.shape
    nq = C // BLOCK
    scale = HEAD_DIM ** -0.5
    qb = jnp.moveaxis(q.astype(jnp.float32).reshape(b, nq, BLOCK, ATT_KV_HEADS, ATT_GROUP, d), 1, 0)
    kf = k.astype(jnp.float32)
    vf = v.astype(jnp.float32)
    sk = jnp.broadcast_to(sink.astype(jnp.float32).reshape(ATT_KV_HEADS, ATT_GROUP)[None, :, :, None, None],
                          (b, ATT_KV_HEADS, ATT_GROUP, BLOCK, 1))

    def one(qblk):
        s = jnp.einsum('bqkgd,bckd->bkgqc', qblk, kf) * scale
        p = jax.nn.softmax(jnp.concatenate([sk, s], axis=-1), axis=-1)[..., 1:]
        return jnp.einsum('bkgqc,bckd->bqkgd', p, vf)

    o = lax.map(one, qb)
    return jnp.moveaxis(o, 0, 1).reshape(b, C, ATT_WIDTH).astype(q.dtype)


def latent_attention(q, k, v, ck, cv, sink):
    b, L, _, d = q.shape
    nb = L // BLOCK
    n_side = WINDOW // BLOCK
    span = BLOCK * (2 * n_side + 1)
    scale = HEAD_DIM ** -0.5
    qb = q.astype(jnp.float32).reshape(b, nb, BLOCK, ATT_KV_HEADS, ATT_GROUP, d)
    kp = jnp.pad(k.astype(jnp.float32), ((0, 0), (WINDOW, WINDOW), (0, 0), (0, 0)))
    vp = jnp.pad(v.astype(jnp.float32), ((0, 0), (WINDOW, WINDOW), (0, 0), (0, 0)))
    kwin = jnp.concatenate([kp[:, i * BLOCK:i * BLOCK + L].reshape(b, nb, BLOCK, ATT_KV_HEADS, d)
                            for i in range(2 * n_side + 1)], axis=2)
    vwin = jnp.concatenate([vp[:, i * BLOCK:i * BLOCK + L].reshape(b, nb, BLOCK, ATT_KV_HEADS, d)
                            for i in range(2 * n_side + 1)], axis=2)
    ckf = ck.astype(jnp.float32)
    cvf = cv.astype(jnp.float32)
    n_ctx = ckf.shape[1]
    sk = jnp.broadcast_to(sink.astype(jnp.float32).reshape(ATT_KV_HEADS, ATT_GROUP)[None, :, :, None, None],
                          (b, ATT_KV_HEADS, ATT_GROUP, BLOCK, 1))

    def one(args):
        qblk, kblk, vblk, n = args
        pos_q = n * BLOCK + jnp.arange(BLOCK)
        pos_k = n * BLOCK - WINDOW + jnp.arange(span)
        valid = (jnp.abs(pos_q[:, None] - pos_k[None, :]) <= WINDOW) & (pos_k >= 0) & (pos_k < L)
        s_band = jnp.where(valid, jnp.einsum('bqkgd,bskd->bkgqs', qblk, kblk) * scale, -jnp.inf)
        s_ctx = jnp.einsum('bqkgd,bckd->bkgqc', qblk, ckf) * scale
        p = jax.nn.softmax(jnp.concatenate([sk, s_ctx, s_band], axis=-1), axis=-1)
        return (jnp.einsum('bkgqc,bckd->bqkgd', p[..., 1:1 + n_ctx], cvf)
                + jnp.einsum('bkgqs,bskd->bqkgd', p[..., 1 + n_ctx:], vblk))

    o = lax.map(one, (jnp.moveaxis(qb, 1, 0), jnp.moveaxis(kwin, 1, 0),
                      jnp.moveaxis(vwin, 1, 0), jnp.arange(nb)))
    return jnp.moveaxis(o, 0, 1).reshape(b, L, ATT_WIDTH).astype(q.dtype)


def _linear_combine(e1, e2):
    a1, b1 = e1
    a2, b2 = e2
    return a1 * a2, a2 * b1 + b2


def s5_scan(bu, lam_bar, h0):
    bu = bu.at[:, 0].add(lam_bar[None] * h0)
    a = jnp.broadcast_to(lam_bar, bu.shape)
    _, hs = lax.associative_scan(_linear_combine, (a, bu), axis=1)
    return hs, hs[:, -1]


def s5_mixer(u, P, l, h0_f, h0_b):
    b, L, _ = u.shape
    uf = u.astype(jnp.float32)
    uc = uf.reshape(b, L, S5_GROUPS, S5_GROUP_CH).astype(jnp.complex64)
    b_in = lax.complex(P['s5_B_re'][l].astype(jnp.float32), P['s5_B_im'][l].astype(jnp.float32))
    y = P['s5_D'][l].astype(jnp.float32) * uf
    finals = []
    for d, h0 in enumerate((h0_f, h0_b)):
        lam = lax.complex(P['s5_A_re'][l, d].astype(jnp.float32), P['s5_A_im'][l, d].astype(jnp.float32))
        step = jnp.exp(P['s5_log_dt'][l, d].astype(jnp.float32))[:, None]
        lam_bar = jnp.exp(lam * step)
        b_bar = ((lam_bar - 1.0) / lam)[..., None] * b_in
        bu = jnp.einsum('blgc,gpc->blgp', uc, b_bar)
        if d == 1:
            bu = jnp.flip(bu, axis=1)
        hs, h_last = s5_scan(bu, lam_bar, h0)
        if d == 1:
            hs = jnp.flip(hs, axis=1)
        c_out = lax.complex(P['s5_C_re'][l, d].astype(jnp.float32), P['s5_C_im'][l, d].astype(jnp.float32))
        y = y + jnp.real(jnp.einsum('blgp,gcp->blgc', hs, c_out)).reshape(b, L, S5_WIDTH)
        finals.append(h_last)
    g = jax.nn.gelu(y)
    out = g * jax.nn.sigmoid(g @ P['s5_w_glu'][l].astype(jnp.float32) + P['s5_b_glu'][l].astype(jnp.float32))
    return out.astype(u.dtype), finals[0], finals[1]


def mixer_block(h, P, l, ctx):
    b, L, _ = h.shape
    proj = jnp.einsum('bld,de->ble', h, P['w_in'][l])
    z, xbc, dt_raw, q, k, v, u = jnp.split(proj, IN_SPLITS, axis=-1)
    q = q.reshape(b, L, ATT_HEADS, HEAD_DIM)
    k = k.reshape(b, L, ATT_KV_HEADS, HEAD_DIM)
    v = v.reshape(b, L, ATT_KV_HEADS, HEAD_DIM)
    sink = P['attn_sink'][l]
    if ctx is None:
        h0 = jnp.zeros((b, SSD_HEADS, SSD_HEAD_DIM, SSD_D_STATE), jnp.float32)
        s0 = jnp.zeros((b, S5_GROUPS, S5_P), jnp.complex64)
        o_att = context_attention(q, k, v, sink)
        y_ssd, h_f, h_b = ssd_mixer(z, xbc, dt_raw, P, l, h0, h0)
        y_s5, s_f, s_b = s5_mixer(u, P, l, s0, s0)
        new_ctx = (k, v, h_f, h_b, s_f, s_b)
    else:
        ck, cv, h0_f, h0_b, s0_f, s0_b = ctx
        cos, sin = axial_rope_tables(L)
        o_att = latent_attention(apply_rope(q, cos, sin), apply_rope(k, cos, sin), v, ck, cv, sink)
        y_ssd, _, _ = ssd_mixer(z, xbc, dt_raw, P, l, h0_f, h0_b)
        y_s5, _, _ = s5_mixer(u, P, l, s0_f, s0_b)
        new_ctx = None
    mix = jnp.concatenate([y_ssd, o_att, y_s5], axis=-1)
    return jnp.einsum('ble,ed->bld', mix, P['w_out'][l]), new_ctx


def trunk_layer(x, mod, P, l, ctx):
    sh1, sc1, g1, sh2, sc2, g2 = jnp.split(mod.astype(x.dtype), 6, axis=-1)
    h = rmsnorm(x, P['norm_mix_pre'][l]) * (1 + sc1) + sh1
    mix, new_ctx = mixer_block(h, P, l, ctx)
    x = x + g1 * rmsnorm(mix, P['norm_mix_post'][l])
    h = rmsnorm(x, P['norm_ffn_pre'][l]) * (1 + sc2) + sh2
    gate, up = jnp.split(jnp.einsum('bld,df->blf', h, P['w_ffn_in'][l]), 2, axis=-1)
    f = jnp.einsum('blf,fd->bld', jax.nn.silu(gate) * up, P['w_ffn_out'][l])
    x = x + g2 * rmsnorm(f, P['norm_ffn_post'][l])
    return x, new_ctx


def setup_inputs(seed: int = 0) -> dict:
    key = jax.random.key(seed)
    ks = iter(jax.random.split(key, 40))

    def nrm(shape, scale):
        return jax.random.normal(next(ks), shape, jnp.float32) * scale

    def unif(shape, lo, hi):
        return jax.random.uniform(next(ks), shape, jnp.float32, lo, hi)

    x_prompt = nrm((BATCH, SEQ, D_MODEL), 1.0)
    x_sample = nrm((DEC_BATCH, DEC_SEQ, D_MODEL), 1.0)
    c = nrm((DEC_BATCH, D_MODEL), 1.0)
    cache_k = nrm((DEC_BATCH, DEPTH, PAST_LEN, ATT_KV_HEADS, HEAD_DIM), 1.0)
    cache_v = nrm((DEC_BATCH, DEPTH, PAST_LEN, ATT_KV_HEADS, HEAD_DIM), 1.0)
    state_ssd = nrm((DEC_BATCH, DEPTH, 2, SSD_HEADS, SSD_HEAD_DIM, SSD_D_STATE), 0.5)
    state_s5_re = nrm((DEC_BATCH, DEPTH, 2, S5_GROUPS, S5_P), 0.5)
    state_s5_im = nrm((DEC_BATCH, DEPTH, 2, S5_GROUPS, S5_P), 0.5)
    c_ctx = nrm((D_MODEL,), 1.0)
    w_ada = nrm((DEPTH, D_MODEL, 6 * D_MODEL), 0.5 * D_MODEL ** -0.5)
    b_ada = nrm((DEPTH, 6 * D_MODEL), 0.02)
    norm_mix_pre = 1.0 + nrm((DEPTH, D_MODEL), 0.02)
    norm_mix_post = 1.0 + nrm((DEPTH, D_MODEL), 0.02)
    norm_ffn_pre = 1.0 + nrm((DEPTH, D_MODEL), 0.02)
    norm_ffn_post = 1.0 + nrm((DEPTH, D_MODEL), 0.02)
    w_in = nrm((DEPTH, D_MODEL, D_IN), D_MODEL ** -0.5)
    w_out = nrm((DEPTH, MIX_WIDTH, D_MODEL), MIX_WIDTH ** -0.5)
    ssd_conv_w = nrm((DEPTH, SSD_CONV_W, SSD_CONV_CH), SSD_CONV_W ** -0.5)
    ssd_conv_b = nrm((DEPTH, SSD_CONV_CH), 0.02)
    dt0 = jnp.exp(unif((DEPTH, 2, SSD_HEADS), math.log(1e-3), math.log(1e-1)))
    ssd_dt_bias = dt0 + jnp.log(-jnp.expm1(-dt0))
    ssd_A_log = jnp.log(unif((DEPTH, 2, SSD_HEADS), 1.0, 16.0))
    ssd_D = 1.0 + nrm((DEPTH, SSD_HEADS), 0.1)
    ssd_norm = 1.0 + nrm((DEPTH, SSD_WIDTH), 0.02)
    attn_sink = nrm((DEPTH, ATT_HEADS), 1.0)
    n_idx = jnp.arange(S5_P, dtype=jnp.float32)
    s5_A_re = -0.5 + nrm((DEPTH, 2, S5_GROUPS, S5_P), 0.01)
    s5_A_im = math.pi * n_idx + nrm((DEPTH, 2, S5_GROUPS, S5_P), 0.01)
    s5_log_dt = unif((DEPTH, 2, S5_GROUPS), math.log(1e-3), math.log(1e-1))
    s5_B_re = nrm((DEPTH, S5_GROUPS, S5_P, S5_GROUP_CH), (2 * S5_GROUP_CH) ** -0.5)
    s5_B_im = nrm((DEPTH, S5_GROUPS, S5_P, S5_GROUP_CH), (2 * S5_GROUP_CH) ** -0.5)
    s5_C_re = nrm((DEPTH, 2, S5_GROUPS, S5_GROUP_CH, S5_P), (2 * S5_P) ** -0.5)
    s5_C_im = nrm((DEPTH, 2, S5_GROUPS, S5_GROUP_CH, S5_P), (2 * S5_P) ** -0.5)
    s5_D = nrm((DEPTH, S5_WIDTH), 1.0)
    s5_w_glu = nrm((DEPTH, S5_WIDTH, S5_WIDTH), S5_WIDTH ** -0.5)
    s5_b_glu = nrm((DEPTH, S5_WIDTH), 0.02)
    w_ffn_in = nrm((DEPTH, D_MODEL, 2 * D_FF), D_MODEL ** -0.5)
    w_ffn_out = nrm((DEPTH, D_FF, D_MODEL), D_FF ** -0.5)
    return {'x_prompt': x_prompt, 'x_sample': x_sample, 'c': c,
            'cache_k': cache_k, 'cache_v': cache_v, 'state_ssd': state_ssd,
            'state_s5_re': state_s5_re, 'state_s5_im': state_s5_im, 'c_ctx': c_ctx,
            'w_ada': w_ada, 'b_ada': b_ada,
            'norm_mix_pre': norm_mix_pre, 'norm_mix_post': norm_mix_post,
            'norm_ffn_pre': norm_ffn_pre, 'norm_ffn_post': norm_ffn_post,
            'w_in': w_in, 'w_out': w_out,
            'ssd_conv_w': ssd_conv_w, 'ssd_conv_b': ssd_conv_b, 'ssd_dt_bias': ssd_dt_bias,
            'ssd_A_log': ssd_A_log, 'ssd_D': ssd_D, 'ssd_norm': ssd_norm,
            'attn_sink': attn_sink,
            's5_A_re': s5_A_re, 's5_A_im': s5_A_im, 's5_log_dt': s5_log_dt,
            's5_B_re': s5_B_re, 's5_B_im': s5_B_im, 's5_C_re': s5_C_re, 's5_C_im': s5_C_im,
            's5_D': s5_D, 's5_w_glu': s5_w_glu, 's5_b_glu': s5_b_glu,
            'w_ffn_in': w_ffn_in, 'w_ffn_out': w_ffn_out}


def reference(x_prompt, x_sample, c, cache_k, cache_v, state_ssd, state_s5_re, state_s5_im, c_ctx,
              w_ada, b_ada, norm_mix_pre, norm_mix_post, norm_ffn_pre, norm_ffn_post,
              w_in, w_out, ssd_conv_w, ssd_conv_b, ssd_dt_bias, ssd_A_log, ssd_D, ssd_norm,
              attn_sink, s5_A_re, s5_A_im, s5_log_dt, s5_B_re, s5_B_im, s5_C_re, s5_C_im,
              s5_D, s5_w_glu, s5_b_glu, w_ffn_in, w_ffn_out):
    P = {'norm_mix_pre': norm_mix_pre, 'norm_mix_post': norm_mix_post,
         'norm_ffn_pre': norm_ffn_pre, 'norm_ffn_post': norm_ffn_post,
         'w_in': w_in, 'w_out': w_out,
         'ssd_conv_w': ssd_conv_w, 'ssd_conv_b': ssd_conv_b, 'ssd_dt_bias': ssd_dt_bias,
         'ssd_A_log': ssd_A_log, 'ssd_D': ssd_D, 'ssd_norm': ssd_norm,
         'attn_sink': attn_sink,
         's5_A_re': s5_A_re, 's5_A_im': s5_A_im, 's5_log_dt': s5_log_dt,
         's5_B_re': s5_B_re, 's5_B_im': s5_B_im, 's5_C_re': s5_C_re, 's5_C_im': s5_C_im,
         's5_D': s5_D, 's5_w_glu': s5_w_glu, 's5_b_glu': s5_b_glu,
         'w_ffn_in': w_ffn_in, 'w_ffn_out': w_ffn_out}
    out_dtype = x_prompt.dtype

    xp = x_prompt
    ks, vs, ssd_st, s5_re, s5_im = [], [], [], [], []
    for l in range(DEPTH):
        mod = (jax.nn.silu(c_ctx) @ w_ada[l] + b_ada[l])[None, None, :]
        xp, (k, v, h_f, h_b, s_f, s_b) = trunk_layer(xp, mod, P, l, None)
        ks.append(k)
        vs.append(v)
        ssd_st.append(jnp.stack([h_f, h_b], axis=1))
        s5_re.append(jnp.stack([jnp.real(s_f), jnp.real(s_b)], axis=1))
        s5_im.append(jnp.stack([jnp.imag(s_f), jnp.imag(s_b)], axis=1))
    new_cache_k = jnp.stack(ks, axis=1)
    new_cache_v = jnp.stack(vs, axis=1)
    new_state_ssd = jnp.stack(ssd_st, axis=1).astype(out_dtype)
    new_state_s5_re = jnp.stack(s5_re, axis=1).astype(out_dtype)
    new_state_s5_im = jnp.stack(s5_im, axis=1).astype(out_dtype)

    xs = x_sample
    for l in range(DEPTH):
        mod = (jax.nn.silu(c) @ w_ada[l] + b_ada[l])[:, None, :]
        ctx = (cache_k[:, l], cache_v[:, l],
               state_ssd[:, l, 0].astype(jnp.float32), state_ssd[:, l, 1].astype(jnp.float32),
               lax.complex(state_s5_re[:, l, 0].astype(jnp.float32), state_s5_im[:, l, 0].astype(jnp.float32)),
               lax.complex(state_s5_re[:, l, 1].astype(jnp.float32), state_s5_im[:, l, 1].astype(jnp.float32)))
        xs, _ = trunk_layer(xs, mod, P, l, ctx)

    return (xp, xs, new_cache_k, new_cache_v, new_state_ssd, new_state_s5_re, new_state_s5_im)
```

```python
import contextlib
import math
import numpy as np
import concourse.bass as bass
import concourse.mybir as mybir
from concourse.bass_utils import run_bass_kernel_spmd

F32 = mybir.dt.float32
BF16 = mybir.dt.bfloat16
I32 = mybir.dt.int32
ALU = mybir.AluOpType
AF = mybir.ActivationFunctionType
AX = mybir.AxisListType

NPR, LP, LS = 4, 256, 4096
NT = NPR * LP + LS
LH = LS // 2
NTD = NPR * LP + LH
D = 2048
DIN = 3608
DFF = 5632
NEG = -30000.0
TWO_PI = 2.0 * math.pi


class Tok:
    __slots__ = ("sem", "val", "eng")

    def __init__(self, sem, val=None, eng=None):
        self.sem = sem
        self.val = val
        self.eng = eng


class Trk:
    __slots__ = ("w", "r")

    def __init__(self):
        self.w = None
        self.r = []


class Op:
    __slots__ = ("fn", "waits", "signal", "sem", "inc")

    def __init__(self, fn, waits):
        self.fn = fn
        self.waits = waits
        self.signal = False
        self.sem = None
        self.inc = 1


ENGS = ("pe", "act", "dve", "pool", "sp")


class Sched:
    def __init__(self, nc, n_dma_sems=32):
        self.nc = nc
        self.stack = contextlib.ExitStack()
        self.ops = {e: [] for e in ENGS}
        self.sem = {e: self.stack.enter_context(nc.semaphore("s_" + e)) for e in ENGS}
        self.cnt = {e: 0 for e in ENGS}
        self.pending = {e: [] for e in ENGS}
        self.lastc = {e: None for e in ENGS}
        self.known = {e: {} for e in ENGS}
        self.dsems = [self.stack.enter_context(nc.semaphore("s_d%d" % i)) for i in range(n_dma_sems)]
        self.dcnt = [0] * n_dma_sems
        self.dlast = [None] * n_dma_sems
        self.dnext = 0
        self.dnext_sw = 0
        self.nalloc = 0
        self.bgsems = [self.stack.enter_context(nc.semaphore("s_bg%d" % i)) for i in range(28)]
        self.bgnext = 0
        self.csem = self.stack.enter_context(nc.semaphore("s_cc"))
        self.ccnt = 0
        self.ctoks = []

    def sb(self, shape, dtype, name=None):
        self.nalloc += 1
        return self.stack.enter_context(self.nc.sbuf_tensor(name or ("sb%d" % self.nalloc), list(shape), dtype))

    def ps(self, shape, dtype, name=None):
        self.nalloc += 1
        return self.stack.enter_context(self.nc.psum_tensor(name or ("ps%d" % self.nalloc), list(shape), dtype))

    def _resolve_eng(self, e):
        if not self.pending[e]:
            return
        last = self.lastc[e]
        assert last is not None and not last.signal
        last.signal = True
        last.sem = self.sem[e]
        self.cnt[e] += 1
        v = self.cnt[e]
        for t in self.pending[e]:
            t.val = v
        self.pending[e] = []

    def _need(self, eng, toks):
        kn = self.known[eng]
        best = {}
        for t in toks:
            if t is None:
                continue
            if t.val is None:
                self._resolve_eng(t.eng)
            key = id(t.sem)
            if kn.get(key, 0) >= t.val:
                continue
            kn[key] = t.val
            best[key] = (t.sem, t.val)
        return list(best.values())

    def _deps(self, rd, wr, skip_pe=False):
        toks = []
        for t in rd:
            toks.append(t.w)
        for t in wr:
            toks.append(t.w)
            toks.extend(t.r)
        if skip_pe:
            toks = [t for t in toks if t is not None and t.eng != "pe"]
        return toks

    def _commit(self, tok, rd, wr):
        for t in rd:
            t.r = [x for x in t.r if not (x.eng is not None and x.eng == tok.eng)] + [tok]
        for t in wr:
            t.w = tok
            t.r = []

    def op(self, eng, fn, rd=(), wr=(), pe_acc=False):
        toks = self._deps(rd, wr, skip_pe=(eng == "pe" and pe_acc))
        waits = self._need(eng, toks)
        o = Op(fn, waits)
        self.ops[eng].append(o)
        self.lastc[eng] = o
        tok = Tok(self.sem[eng], None, eng)
        self.pending[eng].append(tok)
        self._commit(tok, rd, wr)
        return tok

    def dma(self, out, in_, rd=(), wr=(), eng="sp", **kw):
        self._resolve_eng(eng)
        nh = len(self.dsems) // 2
        if eng == "pool":
            i = nh + self.dnext_sw
            self.dnext_sw = (self.dnext_sw + 1) % (len(self.dsems) - nh)
        else:
            i = self.dnext
            self.dnext = (self.dnext + 1) % nh
        toks = self._deps(rd, wr)
        toks.append(self.dlast[i])
        waits = self._need(eng, toks)
        o = Op(lambda e: e.dma_start(out=out, in_=in_, **kw), waits)
        o.signal = True
        o.sem = self.dsems[i]
        o.inc = 16
        self.ops[eng].append(o)
        self.dcnt[i] += 16
        tok = Tok(self.dsems[i], self.dcnt[i], None)
        self.dlast[i] = tok
        self._commit(tok, rd, wr)
        return tok

    def dma_bg(self, out, in_, eng="pool", **kw):
        self._resolve_eng(eng)
        sem = self.bgsems[self.bgnext]
        self.bgnext += 1
        o = Op(lambda e: e.dma_start(out=out, in_=in_, **kw), [])
        o.signal = True
        o.sem = sem
        o.inc = 16
        self.ops[eng].append(o)
        return Tok(sem, 16, None)

    def coll(self, kind, op, groups, src, dst):
        eng = "pool"
        self._resolve_eng(eng)
        o = Op(lambda e: e.collective_compute(kind, op, replica_groups=groups, ins=[src], outs=[dst]), [])
        o.signal = True
        o.sem = self.csem
        o.inc = None
        self.ops[eng].append(o)
        self.ccnt += 1
        tok = Tok(self.csem, self.ccnt, None)
        self.ctoks.append(tok)
        return tok

    def wait_toks(self, toks):
        for e in ENGS:
            waits = self._need(e, toks)
            if waits:
                self.ops[e].append(Op(None, waits))

    def barrier(self):
        for e in ENGS:
            self._resolve_eng(e)
        for e in ENGS:
            toks = [Tok(self.sem[o], self.cnt[o], o) for o in ENGS if o != e and self.cnt[o] > 0]
            toks += [t for t in self.dlast if t is not None]
            toks += self.ctoks[-16:]
            waits = self._need(e, toks)
            if waits:
                self.ops[e].append(Op(None, waits))

    def finish(self):
        self.barrier()
        nc = self.nc

        def run(eng_obj, ops):
            for o in ops:
                for s, v in o.waits:
                    eng_obj.wait_ge(s, v)
                if o.fn is None:
                    continue
                ins = o.fn(eng_obj)
                if o.signal:
                    if o.inc is None:
                        ins.then_inc(o.sem)
                    else:
                        ins.then_inc(o.sem, o.inc)

        with nc.Block() as block:
            @block.tensor
            def _(e):
                run(e, self.ops["pe"])

            @block.scalar
            def _(e):
                run(e, self.ops["act"])

            @block.vector
            def _(e):
                run(e, self.ops["dve"])

            @block.gpsimd
            def _(e):
                run(e, self.ops["pool"])

            @block.sync
            def _(e):
                run(e, self.ops["sp"])
        self.stack.close()


class Arena:
    def __init__(self, t, nwords):
        self.t = t
        self.n = nwords
        self.p = 0

    def reset(self):
        self.p = 0

    def f32(self, n):
        a = self.t[:, self.p:self.p + n]
        self.p += n
        assert self.p <= self.n, ("arena overflow", self.p)
        return a

    def bf(self, n):
        w = (n + 1) // 2
        a = self.t[:, self.p:self.p + w].bitcast(BF16)
        self.p += w
        assert self.p <= self.n, ("arena overflow", self.p)
        return a[:, 0:n]

    def i32(self, n):
        return self.f32(n).bitcast(I32)


class Rot:
    def __init__(self, aps):
        self.items = [(a, Trk()) for a in aps]
        self.i = 0

    def next(self):
        it = self.items[self.i]
        self.i = (self.i + 1) % len(self.items)
        return it


MIX = ("conv", "ssd", "attn", "s5")
DEBUG = False
NLAYERS = 2


def build():
    nc = bass.Bass("TRN2", target_bir_lowering=False)

    def din(name, shape, dt=F32):
        return nc.dram_tensor(name, list(shape), dt, kind="ExternalInput").ap()

    def dout(name, shape, dt=F32):
        return nc.dram_tensor(name, list(shape), dt, kind="ExternalOutput").ap()

    def dscr(name, shape, dt=F32):
        kind = "ExternalOutput" if (DEBUG and name in ("mixT", "projT", "tokm")) else "Internal"
        return nc.dram_tensor(name, list(shape), dt, kind=kind).ap()

    xin = din("xin", [NTD, D])
    hmask = din("hmask", [128, 2])
    cvec = din("cvec", [2, D])
    ckin = din("ckin", [2, 512, 256])
    cvin = din("cvin", [2, 512, 256])
    sssd = din("sssd", [2, 2, 12, 64, 64])
    ss5re = din("ss5re", [2, 2, 2048])
    ss5im = din("ss5im", [2, 2, 2048])
    w_ada = din("w_ada", [2, D, 6 * D])
    b_ada = din("b_ada", [2, 6 * D])
    n_mix_pre = din("norm_mix_pre", [2, D])
    n_mix_post = din("norm_mix_post", [2, D])
    n_ffn_pre = din("norm_ffn_pre", [2, D])
    n_ffn_post = din("norm_ffn_post", [2, D])
    w_in = din("w_in", [2, D, DIN])
    w_out = din("w_out", [2, D, D])
    conv_w = din("ssd_conv_w", [2, 3, 1024])
    conv_b = din("ssd_conv_b", [2, 1024])
    dt_bias = din("ssd_dt_bias", [2, 24])
    a_log = din("ssd_A_log", [2, 24])
    ssd_D = din("ssd_D", [2, 12])
    ssd_norm = din("ssd_norm", [2, 768])
    attn_sink = din("attn_sink", [2, 12])
    s5_A_re = din("s5_A_re", [2, 2, 2048])
    s5_A_im = din("s5_A_im", [2, 2, 2048])
    s5_log_dt = din("s5_log_dt", [2, 2, 32])
    s5_B_re = din("s5_B_re", [2, 2048, 16])
    s5_B_im = din("s5_B_im", [2, 2048, 16])
    s5_C_re = din("s5_C_re", [2, 2, 32, 16, 64])
    s5_C_im = din("s5_C_im", [2, 2, 32, 16, 64])
    s5_D = din("s5_D", [2, 512])
    s5_w_glu = din("s5_w_glu", [2, 512, 512])
    s5_b_glu = din("s5_b_glu", [2, 512])
    w_ffn_in = din("w_ffn_in", [2, D, 2 * DFF])
    w_ffn_out = din("w_ffn_out", [2, DFF, D])
    yout = dout("yout", [NTD, D])
    ock = dout("ock", [NPR, 2, LP, 256])
    ocv = dout("ocv", [NPR, 2, LP, 256])
    ossd = dout("ossd", [NPR, 2, 2, 12, 64, 64])
    os5re = dout("os5re", [NPR, 2, 2, 2048])
    os5im = dout("os5im", [NPR, 2, 2, 2048])
    projT = dscr("projT", [3328, NT], BF16)
    tokm = dscr("tokm", [NT, 536])
    xcT = dscr("xcT", [1024, NT], BF16)
    xtok = dscr("xtok", [NT, 896], BF16)
    yfT = dscr("yfT", [768, NT])
    NJ = NT // 4
    u4 = dscr("u4", [4, 512, NJ], BF16)
    s5x4 = [dscr("s5f4", [4, 512, NJ]), dscr("s5b4", [4, 512, NJ])]
    mixT = dscr("mixT", [2048, NT], BF16)
    x1 = dscr("x1", [NTD, D])
    xmid = dscr("xmid", [NTD, D])
    projT_h = dscr("projT_h", [2816, LH], BF16)
    mixA_h = dscr("mixA_h", [768, LH], BF16)
    tokm_h = dscr("tokm_h", [LH, 536])
    u4_h = dscr("u4_h", [4 * 512, LH // 4], BF16)
    gp = [dscr("g_proj%d" % i, [2 * (min(2816, r0 + 512) - r0), LH], BF16) for i, r0 in enumerate(range(0, 2816, 512))]
    gt = [dscr("g_tokm%d" % i, [2 * 512, 536]) for i in range(LH // 512)]
    gu = [dscr("g_u4%d" % i, [2 * 1024, LH // 4], BF16) for i in range(2)]
    wbs = [dict(ada=dscr("wb_ada%d" % i, [D, 6 * D], BF16), win=dscr("wb_in%d" % i, [D, DIN], BF16),
                wout=dscr("wb_out%d" % i, [D, D], BF16), f1=dscr("wb_f1%d" % i, [D, 2 * DFF], BF16),
                f2=dscr("wb_f2%d" % i, [DFF, D], BF16)) for i in range(2)]
    WB = {}

    S = Sched(nc)
    ARW = 36000
    arena_t = S.sb([128, ARW], F32, "arena")
    ar = Arena(arena_t, ARW)
    banks = [S.ps([128, 512], F32, "bank%d" % i) for i in range(8)]

    def E(eng, fn, rd=(), wr=(), pe_acc=False):
        return S.op(eng, fn, rd=rd, wr=wr, pe_acc=pe_acc)

    def mm(out, lhsT, rhs, start, stop, rd, wr, skip=False):
        return E("pe", lambda e: e.matmul(out, lhsT=lhsT, rhs=rhs, start=start, stop=stop, skip_group_check=skip),
                 rd=rd, wr=wr, pe_acc=(not start))

    def tr(out, in_, ident, rd, wr):
        return E("pe", lambda e: e.transpose(out, in_, ident), rd=rd, wr=wr)

    def act(out, in_, func, rd, wr, scale=1.0, bias=0.0, accum=None, eng="act"):
        if accum is None:
            return E(eng, lambda e: e.activation(out=out, in_=in_, func=func, scale=scale, bias=bias), rd=rd, wr=wr)
        return E(eng, lambda e: e.activation(out=out, in_=in_, func=func, scale=scale, bias=bias, accum_out=accum),
                 rd=rd, wr=wr)

    def tt(eng, out, a, b, op, rd, wr):
        return E(eng, lambda e: e.tensor_tensor(out, a, b, op), rd=rd, wr=wr)

    def ts(eng, out, a, s1, s2, op0, op1, rd, wr):
        return E(eng, lambda e: e.tensor_scalar(out, a, s1, s2, op0, op1), rd=rd, wr=wr)

    def stt(out, a, sc, b, op0, op1, rd, wr):
        return E("dve", lambda e: e.scalar_tensor_tensor(out, a, sc, b, op0, op1), rd=rd, wr=wr)

    def cp(eng, out, a, rd, wr):
        if eng == "act":
            return E("act", lambda e: e.copy(out, a), rd=rd, wr=wr)
        return E(eng, lambda e: e.tensor_copy(out, a), rd=rd, wr=wr)

    def memset(eng, out, v, wr):
        return E(eng, lambda e: e.memset(out, v), wr=wr)

    ident_f = S.sb([128, 128], F32, "ident_f")
    ident_b = S.sb([128, 128], BF16, "ident_b")
    ones_f = S.sb([128, 128], F32, "ones_f")
    ones_b = S.sb([128, 128], BF16, "ones_b")
    U_f = S.sb([128, 128], F32, "U_f")
    L_f = S.sb([128, 128], F32, "L_f")
    nm_fwd = S.sb([128, 128], F32, "nm_fwd")
    nm_bwd = S.sb([128, 128], F32, "nm_bwd")
    perm_b = S.sb([128, 128], BF16, "perm_b")
    negones_f = S.sb([128, 128], F32, "negones_f")
    nm3 = [S.sb([128, 384], F32, "nm3_%d" % i) for i in range(2)]
    nm3b = [S.sb([128, 384], BF16, "nm3b_%d" % i) for i in range(2)]
    nmb = [S.sb([128, 128], BF16, "nmb_%d" % i) for i in range(2)]
    modc = S.sb([128, 4, 16, 2], F32, "modc")
    gates = S.sb([128, 2, 2, D], F32, "gates")
    tC = Trk()
    t_modc = Trk()
    t_gates = Trk()
    iot = S.sb([128, 128], F32, "iot")
    E("pool", lambda e: e.iota(iot[:], pattern=[[1, 128]], base=0, channel_multiplier=-1,
                               allow_small_or_imprecise_dtypes=True), wr=[tC])
    ts("dve", ident_f[:], iot[:], 0.0, None, ALU.is_equal, ALU.bypass, [tC], [tC])
    cp("dve", ident_b[:], ident_f[:], [tC], [tC])
    memset("dve", ones_f[:], 1.0, [tC])
    memset("dve", ones_b[:], 1.0, [tC])
    memset("dve", negones_f[:], -1.0, [tC])
    ts("dve", U_f[:], iot[:], 0.0, None, ALU.is_ge, ALU.bypass, [tC], [tC])
    ts("dve", L_f[:], iot[:], 0.0, None, ALU.is_le, ALU.bypass, [tC], [tC])
    ts("dve", nm_fwd[:], iot[:], 0.0, NEG, ALU.is_lt, ALU.mult, [tC], [tC])
    ts("dve", nm_bwd[:], iot[:], 0.0, NEG, ALU.is_gt, ALU.mult, [tC], [tC])
    for i in range(3):
        cp("dve", nm3[0][:, i * 128:(i + 1) * 128], nm_fwd[:], [tC], [tC])
        cp("dve", nm3[1][:, i * 128:(i + 1) * 128], nm_bwd[:], [tC], [tC])
        cp("dve", nm3b[0][:, i * 128:(i + 1) * 128], nm_fwd[:], [tC], [tC])
        cp("dve", nm3b[1][:, i * 128:(i + 1) * 128], nm_bwd[:], [tC], [tC])
    cp("dve", nmb[0][:], nm_fwd[:], [tC], [tC])
    cp("dve", nmb[1][:], nm_bwd[:], [tC], [tC])
    pa = S.sb([128, 128], F32, "pa")
    pidx = S.sb([128, 1], I32, "pidx")
    pf = S.sb([128, 4], F32, "pf")
    E("pool", lambda e: e.iota(pidx[:], pattern=[[0, 1]], base=0, channel_multiplier=1), wr=[tC])
    pidx2 = S.sb([128, 1], I32, "pidx2")
    E("dve", lambda e: e.tensor_single_scalar(pidx2[:], pidx[:], 32, ALU.bitwise_and), rd=[tC], wr=[tC])
    cp("dve", pf[:, 0:1], pidx2[:], [tC], [tC])
    ts("dve", pf[:, 1:2], pf[:, 0:1], -2.0, 32.0, ALU.mult, ALU.add, [tC], [tC])
    ts("dve", pa[:], iot[:], pf[:, 1:2], None, ALU.is_equal, ALU.bypass, [tC], [tC])
    cp("dve", perm_b[:], pa[:], [tC], [tC])

    def convert(dst, src, rows, cols):
        toks = []
        step = 1024
        for r0 in range(0, rows, step):
            r1 = min(rows, r0 + step)
            toks.append(S.dma_bg(dst[r0:r1, :], src[r0:r1, :], eng="pool"))
        return toks

    conv_toks = []
    for l_ in range(NLAYERS):
        conv_toks.append(dict(
            ada=convert(wbs[l_]["ada"], w_ada[l_], D, 6 * D),
            win=convert(wbs[l_]["win"], w_in[l_], D, DIN),
            wout=convert(wbs[l_]["wout"], w_out[l_], D, D),
            f1=convert(wbs[l_]["f1"], w_ffn_in[l_], D, 2 * DFF),
            f2=convert(wbs[l_]["f2"], w_ffn_out[l_], DFF, D)))

    def need_w(l, name):
        S.wait_toks(conv_toks[l][name])
        WB[name] = wbs[l][name]

    PAIRS = [[0, 1], [2, 3], [4, 5], [6, 7]]

    def exchange_jobs():
        jobs = []
        for i, r0 in enumerate(range(0, 2816, 512)):
            r1 = min(2816, r0 + 512)
            jobs.append((projT_h[r0:r1, :], gp[i], ("p", r0, r1)))
        for i, t0 in enumerate(range(0, LH, 512)):
            jobs.append((tokm_h[t0:t0 + 512, :], gt[i], ("t", t0, t0 + 512)))
        for i in range(2):
            jobs.append((u4_h[i * 1024:(i + 1) * 1024, :], gu[i], ("u", i, None)))
        return jobs

    def exchange_issue():
        for (src, dst, _) in exchange_jobs():
            S.coll("AllGather", ALU.bypass, PAIRS, src, dst)

    def exchange_finish():
        for (src, dst, (kind, a0, a1)) in exchange_jobs():
            for h in range(2):
                c0 = NPR * LP + h * LH
                if kind == "p":
                    n = a1 - a0
                    S.dma(projT[a0:a1, c0:c0 + LH], dst[h * n:(h + 1) * n, :])
                elif kind == "t":
                    S.dma(tokm[c0 + a0:c0 + a1, :], dst[h * 512:(h + 1) * 512, :])
                else:
                    for ss in range(2):
                        s_ = a0 * 2 + ss
                        S.dma(u4[s_, :, c0 // 4: c0 // 4 + LH // 4],
                              dst[h * 1024 + ss * 512: h * 1024 + (ss + 1) * 512, :])
        S.barrier()

    def layer(l):
        need_w(l, "ada")
        x_src = xin if l == 0 else xmid
        x_dst = xmid if l == 0 else yout
        phase_mod(l)
        S.barrier()
        need_w(l, "win")
        for t3 in range(3):
            phase_A(l, t3, x_src)
            S.barrier()
        exchange_issue()
        S.barrier()
        exchange_finish()
        if "conv" in MIX:
            phase_conv(l)
            S.barrier()
        if "ssd" in MIX:
            phase_ssd(l)
            S.barrier()
        if "attn" in MIX:
            phase_attn(l)
            S.barrier()
        if "s5" in MIX:
            for d_ in range(2):
                phase_s5h(l, d_)
                S.barrier()
            phase_s5fin(l)
            S.barrier()
        need_w(l, "wout")
        for t3 in range(3):
            phase_C(l, t3, x_src)
            S.barrier()
        need_w(l, "f1")
        need_w(l, "f2")
        for t10 in range(6):
            phase_F1(l, t10)
            S.barrier()
            phase_F2(l, t10, x_dst)
            S.barrier()

    def phase_mod(l):
        ar.reset()
        cT = ar.f32(32).rearrange("p (k r) -> p k r", r=2)
        sT = ar.bf(32).rearrange("p (k r) -> p k r", r=2)
        t_c = Trk()
        for r in range(2):
            S.dma(cT[:, :, r], cvec[r].rearrange("(k p) -> p k", p=128), wr=[t_c], allow_slow_non_contiguous=True)
        act(sT, cT, AF.Silu, [t_c], [t_c])
        npre = ar.f32(32).rearrange("p (v k) -> p v k", v=2)
        t_np = Trk()
        S.dma(npre[:, 0, :], n_mix_pre[l].rearrange("(k p) -> p k", p=128), wr=[t_np], allow_slow_non_contiguous=True)
        S.dma(npre[:, 1, :], n_ffn_pre[l].rearrange("(k p) -> p k", p=128), wr=[t_np], allow_slow_non_contiguous=True)
        npost = ar.f32(2 * D).rearrange("p (v d) -> p v d", v=2)
        t_npo = Trk()
        S.dma(npost[:, 0, :], n_mix_post[l].partition_broadcast(128), wr=[t_npo])
        S.dma(npost[:, 1, :], n_ffn_post[l].partition_broadcast(128), wr=[t_npo])
        sel = ar.f32(256).rearrange("p (r m) -> p r m", r=2)
        esel = ar.f32(2)
        t_sel = Trk()
        for r in range(2):
            ts("dve", sel[:, r, :], ones_f[:], ident_f[:, r:r + 1], None, ALU.mult, ALU.bypass, [tC], [t_sel])
        cp("dve", esel, ident_f[:, 0:2], [tC], [t_sel])
        wrot = Rot([ar.bf(16 * 512).rearrange("p (k n) -> p k n", k=16) for _ in range(2)])
        brot = Rot([ar.f32(512) for _ in range(2)])
        mrot = Rot([ar.f32(512) for _ in range(2)])
        t_b0, t_b1, t_b2 = Trk(), Trk(), Trk()
        colsb = ar.f32(8)
        t_colsb = Trk()
        for blk in range(24):
            vec = blk // 4
            wt, t_w = wrot.next()
            S.dma(wt, WB["ada"][:, blk * 512:(blk + 1) * 512].rearrange("(k p) n -> p k n", p=128), wr=[t_w])
            bt, t_b = brot.next()
            S.dma(bt[0:2, :], b_ada[l, blk * 512:(blk + 1) * 512].partition_broadcast(2), wr=[t_b])
            for k in range(16):
                mm(banks[0][0:2, :], sT[:, k, :], wt[:, k, :], k == 0, k == 15, [t_c, t_w], [t_b0])
            mt, t_m = mrot.next()
            tt("dve", mt[0:2, :], banks[0][0:2, :], bt[0:2, :], ALU.add, [t_b0, t_b], [t_m])
            if vec in (2, 5):
                which = 0 if vec == 2 else 1
                for r in range(2):
                    mm(banks[1][:, :], sel[0:2, r, :], mt[0:2, :], True, True, [t_sel, t_m], [t_b1])
                    tt("dve", gates[:, which, r, (blk % 4) * 512:(blk % 4 + 1) * 512], banks[1][:, :],
                       npost[:, which, (blk % 4) * 512:(blk % 4 + 1) * 512], ALU.mult, [t_b1, t_npo], [t_gates])
            else:
                vi = {0: 1, 1: 0, 3: 3, 4: 2}[vec]
                for j in range(4):
                    for r in range(2):
                        mm(banks[2][:, j * 2 + r:j * 2 + r + 1], mt[0:2, j * 128:(j + 1) * 128], esel[0:2, r:r + 1],
                           True, True, [t_m, t_sel], [t_b2])
                kc0 = (blk % 4) * 4
                cp("dve", modc[:, vi, kc0:kc0 + 4, :], banks[2][:, 0:8].rearrange("p (j r) -> p j r", r=2),
                   [t_b2], [t_modc])
        for r in range(2):
            for (vi, v) in ((0, 0), (2, 1)):
                stt(modc[:, vi, :, r], modc[:, vi, :, r], 1.0, npre[:, v, :], ALU.add, ALU.mult,
                    [t_modc, t_np], [t_modc])

    def prenorm(xsrc, tok0, T, gi, vA, vB, hT, t_h):
        nsub = T // 128
        xrot = Rot([ar.f32(D) for _ in range(2)])
        nrot = Rot([ar.bf(D) for _ in range(2)])
        junk = ar.bf(D)
        t_junk = Trk()
        ssr = Rot([ar.f32(4) for _ in range(2)])
        t_tp = [Trk(), Trk()]
        for sub in range(nsub):
            xt, t_x = xrot.next()
            S.dma(xt, xsrc[tok0 + sub * 128: tok0 + (sub + 1) * 128, :], wr=[t_x])
            ss, t_ss = ssr.next()
            act(junk, xt, AF.Square, [t_x], [t_junk, t_ss], accum=ss[:, 0:1])
            ts("dve", ss[:, 1:2], ss[:, 0:1], 1.0 / D, 1e-6, ALU.mult, ALU.add, [t_ss], [t_ss])
            act(ss[:, 2:3], ss[:, 1:2], AF.Sqrt, [t_ss], [t_ss])
            E("dve", lambda e, o=ss[:, 3:4], i=ss[:, 2:3]: e.reciprocal(o, i), rd=[t_ss], wr=[t_ss])
            xn, t_n = nrot.next()
            ts("dve", xn, xt, ss[:, 3:4], None, ALU.mult, ALU.bypass, [t_x, t_ss], [t_n])
            for q4 in range(4):
                pb = banks[6 + (q4 % 2)][:, 0:256].bitcast(BF16).rearrange("p (a b) -> p a b", a=4)
                tp = t_tp[q4 % 2]
                for a in range(4):
                    kc = q4 * 4 + a
                    tr(pb[:, a, :], xn[:, kc * 128:(kc + 1) * 128], ident_b[:], [t_n, tC], [tp])
                cp("act" if q4 % 2 == 0 else "dve", hT[:, q4 * 4:(q4 + 1) * 4, sub * 128:(sub + 1) * 128], pb,
                   [tp], [t_h])
        for kc in range(16):
            act(hT[:, kc, :], hT[:, kc, :], AF.Identity, [t_h, t_modc], [t_h],
                scale=modc[:, vA, kc, gi:gi + 1], bias=modc[:, vB, kc, gi:gi + 1])

    FM_BLOCKS = [(0, 512, 0), (512, 256, 512), (768, 512, 768), (1280, 512, 1280),
                 (1816, 512, 1792), (2328, 256, 2304), (2584, 256, 2560), (3096, 512, 2816)]

    def phase_A(l, t5, xsrc):
        ar.reset()
        T = 1024
        tok0 = t5 * T
        gi = 0 if t5 == 0 else 1
        if t5 == 0:
            pj_dst, tk_dst, dcol = projT, tokm, 0
            u4v = u4
        else:
            pj_dst, tk_dst, dcol = projT_h, tokm_h, (t5 - 1) * T
            u4v = u4_h.rearrange("(s f) j -> s f j", s=4)
        hT = ar.bf(16 * T).rearrange("p (k t) -> p k t", k=16)
        t_h = Trk()
        prenorm(xsrc, tok0, T, gi, 0, 1, hT, t_h)
        wtm = ar.bf(16 * 536).rearrange("p (k n) -> p k n", k=16)
        t_wtm = Trk()
        for (c0, n, d0) in ((1792, 24, 0), (2584, 256, 24), (2840, 256, 280)):
            S.dma(wtm[:, :, d0:d0 + n], WB["win"][:, c0:c0 + n].rearrange("(k p) n -> p k n", p=128), wr=[t_wtm])
        trot = Rot([ar.f32(536) for _ in range(2)])
        t_pa, t_pb = [Trk(), Trk()], [Trk(), Trk()]
        for sub in range(T // 128):
            pa_, pb_ = banks[sub % 2], banks[2 + sub % 2]
            ta, tb = t_pa[sub % 2], t_pb[sub % 2]
            for k in range(16):
                mm(pa_[:, :], hT[:, k, sub * 128:(sub + 1) * 128], wtm[:, k, 24:536], k == 0, k == 15,
                   [t_h, t_wtm], [ta])
            for k in range(16):
                mm(pb_[:, 0:24], hT[:, k, sub * 128:(sub + 1) * 128], wtm[:, k, 0:24], k == 0, k == 15,
                   [t_h, t_wtm], [tb])
            st, t_st = trot.next()
            cp("act", st[:, 24:536], pa_[:, :], [ta], [t_st])
            cp("dve", st[:, 0:24], pb_[:, 0:24], [tb], [t_st])
            g0 = dcol + sub * 128
            S.dma(tk_dst[g0:g0 + 128, :], st, rd=[t_st])
            if t5 == 0:
                sq, so = sub // 2, (sub % 2) * 128
                S.dma(ock[sq, l, so:so + 128, :], st[:, 24:280], rd=[t_st])
                S.dma(ocv[sq, l, so:so + 128, :], st[:, 280:536], rd=[t_st])
        wrot = Rot([ar.bf(16 * 512).rearrange("p (k n) -> p k n", k=16) for _ in range(2)])
        srot = Rot([ar.bf(T) for _ in range(2)])
        t_pm = [Trk() for _ in range(2)]
        cnt = 0
        for (c0, n, d0) in FM_BLOCKS:
            wt, t_w = wrot.next()
            S.dma(wt[:, :, 0:n], WB["win"][:, c0:c0 + n].rearrange("(k p) n -> p k n", p=128), wr=[t_w])
            if d0 == 2816:
                hT4 = hT.rearrange("p k (j s) -> p k s j", s=4)
                for mt in range(4):
                    for s_ in range(4):
                        stg, t_sg = srot.next()
                        pm, tpm = banks[4 + cnt % 2], t_pm[cnt % 2]
                        for k in range(16):
                            mm(pm[:, 0:256], wt[:, k, mt * 128:(mt + 1) * 128], hT4[:, k, s_, :],
                               k == 0, k == 15, [t_w, t_h], [tpm])
                        cp("act" if cnt % 2 == 0 else "dve", stg[:, 0:256], pm[:, 0:256], [tpm], [t_sg])
                        cnt += 1
                        S.dma(u4v[s_, mt * 128:(mt + 1) * 128, dcol // 4: dcol // 4 + 256], stg[:, 0:256], rd=[t_sg])
                continue
            for mt in range(n // 128):
                stg, t_sg = srot.next()
                for nn in range(T // 512):
                    pm, tpm = banks[4 + cnt % 2], t_pm[cnt % 2]
                    for k in range(16):
                        mm(pm[:, :], wt[:, k, mt * 128:(mt + 1) * 128], hT[:, k, nn * 512:(nn + 1) * 512],
                           k == 0, k == 15, [t_w, t_h], [tpm])
                    cp("act" if cnt % 2 == 0 else "dve", stg[:, nn * 512:(nn + 1) * 512], pm[:, :], [tpm], [t_sg])
                    cnt += 1
                S.dma(pj_dst[d0 + mt * 128: d0 + (mt + 1) * 128, dcol:dcol + T], stg, rd=[t_sg])

    SEQS = [(i * LP, LP, False, i) for i in range(NPR)] + [(NPR * LP, LS, True, None)]

    def phase_conv(l):
        ar.reset()
        cw = ar.f32(24).rearrange("p (j k) -> p j k", k=3)
        cb = ar.f32(8)
        t_cw = Trk()
        for k in range(3):
            S.dma(cw[:, :, k], conv_w[l, k].rearrange("(j p) -> p j", p=128), wr=[t_cw], allow_slow_non_contiguous=True)
        S.dma(cb, conv_b[l].rearrange("(j p) -> p j", p=128), wr=[t_cw], allow_slow_non_contiguous=True)
        xrot = Rot([ar.bf(8 * 130).rearrange("p (j t) -> p j t", j=8) for _ in range(2)])
        acc = ar.f32(1024).rearrange("p (j t) -> p j t", j=8)
        acc2 = ar.f32(1024).rearrange("p (j t) -> p j t", j=8)
        t_acc, t_acc2 = Trk(), Trk()
        crot = Rot([ar.bf(1024).rearrange("p (j t) -> p j t", j=8) for _ in range(2)])
        srot = Rot([ar.bf(896) for _ in range(2)])
        t_ps = [Trk(), Trk()]
        ci = 0
        for (off, L, lat, si) in SEQS:
            for c in range(L // 128):
                t0 = c * 128
                g0 = off + t0
                xi, t_xi = xrot.next()
                lo, hi = 0, 130
                if t0 == 0:
                    memset("pool", xi[:, :, 0:1], 0.0, [t_xi])
                    lo = 1
                if t0 + 128 == L:
                    memset("pool", xi[:, :, 129:130], 0.0, [t_xi])
                    hi = 129
                S.dma(xi[:, :, lo:hi], projT[768:1792, g0 - 1 + lo: g0 - 1 + hi].rearrange("(j p) t -> p j t", p=128),
                      wr=[t_xi])
                wb = lambda k: cw[:, :, k:k + 1].to_broadcast([128, 8, 128])
                tt("dve", acc, xi[:, :, 1:129], wb(1), ALU.mult, [t_xi, t_cw], [t_acc])
                tt("pool", acc2, xi[:, :, 0:128], wb(0), ALU.mult, [t_xi, t_cw], [t_acc2])
                tt("dve", acc, acc, acc2, ALU.add, [t_acc2], [t_acc])
                tt("pool", acc2, xi[:, :, 2:130], wb(2), ALU.mult, [t_xi, t_cw], [t_acc2])
                tt("dve", acc, acc, acc2, ALU.add, [t_acc2], [t_acc])
                tt("dve", acc, acc, cb.unsqueeze(2).to_broadcast([128, 8, 128]), ALU.add, [t_cw], [t_acc])
                xc, t_xc = crot.next()
                act(xc, acc, AF.Silu, [t_acc], [t_xc])
                S.dma(xcT[:, g0:g0 + 128].rearrange("(j p) t -> p j t", p=128), xc, rd=[t_xc])
                pb = banks[ci % 2][:, 0:448].bitcast(BF16)
                tp = t_ps[ci % 2]
                for j in range(7):
                    tr(pb[:, j * 128:(j + 1) * 128], xc[:, j, :], ident_b[:], [t_xc, tC], [tp])
                st, t_st = srot.next()
                cp("act", st, pb, [tp], [t_st])
                S.dma(xtok[g0:g0 + 128, :], st, rd=[t_st])
                ci += 1

    def phase_ssd(l):
        ar.reset()
        dtb = ar.f32(24)
        aneg = ar.f32(24)
        Dc = ar.f32(6)
        nw = ar.f32(6)
        t_par = Trk()
        S.dma(dtb, dt_bias[l].partition_broadcast(128), wr=[t_par])
        S.dma(aneg, a_log[l].partition_broadcast(128), wr=[t_par])
        act(aneg, aneg, AF.Exp, [t_par], [t_par])
        ts("dve", aneg, aneg, -1.0, None, ALU.mult, ALU.bypass, [t_par], [t_par])
        dv = ssd_D[l].rearrange("(j t) -> t j", t=2)
        for hh in range(2):
            S.dma(Dc[hh * 64:(hh + 1) * 64, :], dv[hh].partition_broadcast(64), wr=[t_par], allow_slow_non_contiguous=True)
        S.dma(nw, ssd_norm[l].rearrange("(j p) -> p j", p=128), wr=[t_par], allow_slow_non_contiguous=True)
        ST = ar.f32(6 * 64).rearrange("p (h q) -> p h q", h=6)
        STb = ar.bf(6 * 64).rearrange("p (h q) -> p h q", h=6)
        t_ST = Trk()
        stio = ar.f32(12 * 64).rearrange("p (h n) -> p h n", h=12)
        t_stio = Trk()
        dtr = Rot([ar.f32(24) for _ in range(2)])
        xkr = Rot([ar.bf(896) for _ in range(2)])
        bcr = Rot([ar.bf(256).rearrange("p (w t) -> p w t", w=2) for _ in range(2)])
        sm = Rot([ar.f32(96) for _ in range(2)])
        xdtr = Rot([ar.bf(1536).rearrange("p (w h q) -> p w h q", w=2, h=12) for _ in range(2)])
        GTr = Rot([ar.f32(256).rearrange("p (g t) -> p g t", g=2) for _ in range(2)])
        rhr = Rot([ar.f32(384).rearrange("p (i l) -> p i l", i=3) for _ in range(3)])
        e2r = Rot([ar.f32(384) for _ in range(2)])
        dcr = Rot([ar.f32(384) for _ in range(2)])
        scr = Rot([ar.bf(384).rearrange("p (i l) -> p i l", i=3) for _ in range(3)])
        csr = Rot([ar.bf(384).rearrange("p (i l) -> p i l", i=3) for _ in range(2)])
        yst = Rot([ar.f32(768).rearrange("p (j t) -> p j t", j=6) for _ in range(2)])
        yfl = ar.f32(768).rearrange("p (j t) -> p j t", j=6)
        xsl = ar.bf(768).rearrange("p (j t) -> p j t", j=6)
        zl = ar.bf(768).rearrange("p (j t) -> p j t", j=6)
        zs = ar.f32(768).rearrange("p (j t) -> p j t", j=6)
        ysq = ar.bf(768).rearrange("p (j t) -> p j t", j=6)
        rsd = ar.f32(128)
        yo = ar.bf(768).rearrange("p (j t) -> p j t", j=6)
        t_fin = Trk()
        t_yo = Trk()
        pending_fin = []
        tb = {k: Trk() for k in ("c", "tot", "gt", "cr0", "cr1", "y0", "y1", "pst", "ss")}
        tb["tr"] = tb["ss"]
        for d in range(2):
            Tm = U_f if d == 0 else L_f
            nmk = nm_fwd if d == 0 else nm_bwd
            for (off, L, lat, si) in SEQS:
                if lat:
                    S.dma(stio[0:64, :, :], sssd[l, d].rearrange("h p n -> p h n"), wr=[t_stio])
                    for h in range(12):
                        g = h // 6
                        E("pe", lambda e, o=banks[7][g * 64:(g + 1) * 64, (h % 6) * 64:(h % 6 + 1) * 64],
                          i=stio[0:64, h, :]: e.matmul(o, lhsT=i, rhs=ident_f[0:64, 0:64], start=True, stop=True),
                          rd=[t_stio, tC], wr=[tb["tr"]])
                        if h % 6 == 5:
                            cp("dve", ST[g * 64:(g + 1) * 64, :, :],
                               banks[7][g * 64:(g + 1) * 64, 0:384].rearrange("p (h q) -> p h q", h=6), [tb["tr"]], [t_ST])
                else:
                    memset("dve", ST, 0.0, [t_ST])
                cp("act", STb, ST, [t_ST], [t_ST])
                nch = L // 128
                order = range(nch) if d == 0 else range(nch - 1, -1, -1)
                for c in order:
                    g0 = off + c * 128
                    dtt, t_dt = dtr.next()
                    S.dma(dtt, tokm[g0:g0 + 128, 0:24], wr=[t_dt])
                    xk, t_xk = xkr.next()
                    S.dma(xk, xtok[g0:g0 + 128, :], wr=[t_xk])
                    bc, t_bc = bcr.next()
                    S.dma(bc, xcT[768:1024, g0:g0 + 128].rearrange("(w p) t -> p w t", p=128), wr=[t_bc])
                    s_, t_s = sm.next()
                    dt_ = s_[:, 0:24]
                    da = s_[:, 24:48]
                    cneg = s_[:, 48:60]
                    w_ = s_[:, 60:72]
                    edec = s_[:, 72:84]
                    tt("dve", dt_, dtt, dtb, ALU.add, [t_dt, t_par], [t_s])
                    act(dt_, dt_, AF.Exp, [t_s], [t_s])
                    act(dt_, dt_, AF.Ln, [t_s], [t_s], bias=1.0)
                    tt("dve", da, dt_, aneg, ALU.mult, [t_s, t_par], [t_s])
                    dad = da[:, d * 12:(d + 1) * 12]
                    mm(banks[0][:, 0:12], Tm[:], dad, True, True, [tC, t_s], [tb["c"]])
                    mm(banks[0][:, 16:28], ones_f[:], dad, True, True, [tC, t_s], [tb["tot"]])
                    ts("dve", cneg, banks[0][:, 0:12], -1.0, None, ALU.mult, ALU.bypass, [tb["c"]], [t_s])
                    tt("dve", w_, banks[0][:, 16:28], cneg, ALU.add, [tb["tot"], t_s], [t_s])
                    act(w_, w_, AF.Exp, [t_s], [t_s])
                    act(edec, banks[0][:, 16:28], AF.Exp, [tb["tot"]], [t_s])
                    xd, t_xd = xdtr.next()
                    xs3 = xk[:, 0:768].rearrange("p (h q) -> p h q", h=12)
                    tt("dve", xd[:, 0, :, :], xs3, dt_[:, d * 12:(d + 1) * 12].unsqueeze(2).to_broadcast([128, 12, 64]),
                       ALU.mult, [t_xk, t_s], [t_xd])
                    GT, t_GT = GTr.next()
                    for g in range(2):
                        mm(banks[1][:, g * 128:(g + 1) * 128], bc[g * 64:(g + 1) * 64, 0, :], bc[g * 64:(g + 1) * 64, 1, :],
                           True, True, [t_bc], [tb["gt"]])
                    cp("act", GT, banks[1][:, 0:256].rearrange("p (g t) -> p g t", g=2), [tb["gt"]], [t_GT])
                    ys, t_ys = yst.next()
                    def stageA(hb):
                        g = hb // 2
                        gs = slice(g * 64, (g + 1) * 64)
                        hd0 = d * 12 + hb * 3
                        rh, t_rh = rhr.next()
                        tt("pool", rh, Tm[:].unsqueeze(1).to_broadcast([128, 3, 128]),
                           da[:, hd0:hd0 + 3].unsqueeze(2).to_broadcast([128, 3, 128]), ALU.mult, [tC, t_s], [t_rh])
                        crb, tcr = banks[2 + hb % 2], tb["cr%d" % (hb % 2)]
                        mm(crb[:, 0:384], ones_f[:], rh.rearrange("p i l -> p (i l)"), True, True, [tC, t_rh], [tcr])
                        e2, t_e2 = e2r.next()
                        act(e2[gs, :], crb[gs, 0:384], AF.Exp, [tcr], [t_e2])
                        for i in range(3):
                            mm(crb[:, i * 128:(i + 1) * 128], rh[:, i, :], negones_f[:], False, False, [t_rh, tC], [tcr], skip=True)
                        mm(crb[:, 0:384], ident_b[:], nm3b[d][:], False, True, [tC], [tcr], skip=True)
                        dc, t_dc = dcr.next()
                        act(dc, crb[:, 0:384], AF.Exp, [tcr], [t_dc])
                        sc, t_sc = scr.next()
                        tt("dve", sc, dc.rearrange("p (i l) -> p i l", i=3), GT[:, g:g + 1, :].to_broadcast([128, 3, 128]),
                           ALU.mult, [t_dc, t_GT], [t_sc])
                        cs, t_cs = csr.next()
                        tt("pool", cs[gs, :, :], e2[gs, :].rearrange("p (i l) -> p i l", i=3),
                           bc[gs, 1:2, :].to_broadcast([64, 3, 128]), ALU.mult, [t_bc, t_e2], [t_cs])
                        return (hb, sc, t_sc, cs, t_cs)

                    def stageB(cxb):
                        hb, sc, t_sc, cs, t_cs = cxb
                        g = hb // 2
                        gs = slice(g * 64, (g + 1) * 64)
                        for i in range(3):
                            h = hb * 3 + i
                            yb, tyb = banks[4 + (h // 2) % 2], tb["y%d" % ((h // 2) % 2)]
                            yo_ = yb[(h % 2) * 64:(h % 2 + 1) * 64, 0:128]
                            mm(yo_, xd[:, 0, h, :], sc[:, i, :], True, False, [t_xd, t_sc], [tyb])
                            mm(yo_, STb[gs, h % 6, :], cs[gs, i, :], False, True, [t_ST, t_cs], [tyb])
                            mm(banks[6][gs, (h % 6) * 64:(h % 6 + 1) * 64], xk[:, 768 + g * 64: 768 + (g + 1) * 64],
                               xd[:, 1, h, :], True, True, [t_xk, t_xd], [tb["pst"]])
                            stt(ST[gs, h % 6, :], ST[gs, h % 6, :], edec[gs, h:h + 1],
                                banks[6][gs, (h % 6) * 64:(h % 6 + 1) * 64], ALU.mult, ALU.add,
                                [t_s, tb["pst"]], [t_ST])
                            if h % 6 == 5:
                                cp("act", STb[gs, :, :], ST[gs, :, :], [t_ST], [t_ST])
                            if h % 2 == 1:
                                cp("act", ys[:, h // 2, :], yb[:, 0:128], [tyb], [t_ys])

                    prevb = stageA(0)
                    tt("pool", xd[:, 1, :, :], xd[:, 0, :, :], w_.unsqueeze(2).to_broadcast([128, 12, 64]), ALU.mult,
                       [t_s], [t_xd])
                    while pending_fin:
                        pending_fin.pop(0)()
                    for hb in range(1, 4):
                        nxt = stageA(hb)
                        stageB(prevb)
                        prevb = nxt
                    stageB(prevb)
                    if d == 0:
                        S.dma(yfT[:, g0:g0 + 128].rearrange("(j p) t -> p j t", p=128), ys, rd=[t_ys])
                    else:
                        def fin_chunk(g0=g0, ys=ys, t_ys=t_ys):
                            S.dma(yfl, yfT[:, g0:g0 + 128].rearrange("(j p) t -> p j t", p=128), wr=[t_fin])
                            S.dma(xsl, xcT[0:768, g0:g0 + 128].rearrange("(j p) t -> p j t", p=128), wr=[t_fin])
                            S.dma(zl, projT[0:768, g0:g0 + 128].rearrange("(j p) t -> p j t", p=128), wr=[t_fin])
                            tt("dve", ys, ys, yfl, ALU.add, [t_fin], [t_ys])
                            tt("pool", yfl, xsl, Dc.unsqueeze(2).to_broadcast([128, 6, 128]), ALU.mult, [t_par], [t_fin])
                            tt("dve", ys, ys, yfl, ALU.add, [t_fin], [t_ys])
                            act(zs, zl, AF.Silu, [t_fin], [t_fin])
                            tt("dve", ys, ys, zs, ALU.mult, [t_fin], [t_ys])
                            tt("pool", ysq, ys, ys, ALU.mult, [t_ys], [t_fin])
                            for j in range(6):
                                mm(banks[7][:, 0:128], ones_b[:], ysq[:, j, :], j == 0, j == 5, [tC, t_fin], [tb["ss"]])
                            ts("dve", rsd, banks[7][:, 0:128], 1.0 / 768, 1e-6, ALU.mult, ALU.add, [tb["ss"]], [t_fin])
                            act(rsd, rsd, AF.Sqrt, [t_fin], [t_fin])
                            E("dve", lambda e: e.reciprocal(rsd, rsd), rd=[t_fin], wr=[t_fin])
                            tt("dve", ys, ys, rsd.unsqueeze(1).to_broadcast([128, 6, 128]), ALU.mult, [t_fin], [t_ys])
                            tt("dve", yo, ys, nw.unsqueeze(2).to_broadcast([128, 6, 128]), ALU.mult, [t_ys, t_par], [t_yo])
                            S.dma(mixT[0:768, g0:g0 + 128].rearrange("(j p) t -> p j t", p=128), yo, rd=[t_yo])
                        pending_fin.append(fin_chunk)
                while pending_fin:
                    pending_fin.pop(0)()
                if not lat:
                    for h in range(12):
                        g = h // 6
                        E("pe", lambda e, o=banks[7][0:64, (h % 6) * 64:(h % 6 + 1) * 64],
                          i=ST[g * 64:(g + 1) * 64, h % 6, :], idn=ident_f[g * 64:(g + 1) * 64, g * 64:(g + 1) * 64]: e.transpose(o, i, idn),
                          rd=[t_ST, tC], wr=[tb["tr"]])
                        if h % 6 == 5:
                            cp("dve", stio[0:64, g * 6:(g + 1) * 6, :],
                               banks[7][0:64, 0:384].rearrange("p (h q) -> p h q", h=6), [tb["tr"]], [t_stio])
                    S.dma(ossd[si, l, d].rearrange("h p n -> p h n"), stio[0:64, :, :], rd=[t_stio])

    def phase_attn(l):
        ar.reset()
        esink = ar.f32(12)
        t_es = Trk()
        S.dma(esink[0:64, :], attn_sink[l].partition_broadcast(64), wr=[t_es])
        act(esink[0:64, :], esink[0:64, :], AF.Exp, [t_es], [t_es])
        t1 = ar.f32(4096)
        t2 = ar.f32(4096)
        t3 = ar.f32(4096)
        pc = ar.f32(8)
        pci = ar.i32(4)
        t_rt = Trk()
        E("pool", lambda e: e.iota(t1[0:64, :].rearrange("p (r c) -> p r c", r=64), pattern=[[1, 64], [0, 64]], base=0,
                                   channel_multiplier=0, allow_small_or_imprecise_dtypes=True), wr=[t_rt])
        E("pool", lambda e: e.iota(t2[0:64, :].rearrange("p (r c) -> p r c", r=64), pattern=[[0, 64], [1, 64]], base=0,
                                   channel_multiplier=0, allow_small_or_imprecise_dtypes=True), wr=[t_rt])
        E("dve", lambda e: e.tensor_single_scalar(pci[0:64, 0:1], pidx[0:64, :], 16, ALU.bitwise_and), rd=[tC], wr=[t_rt])
        E("dve", lambda e: e.tensor_single_scalar(pci[0:64, 1:2], pidx[0:64, :], 15, ALU.bitwise_and), rd=[tC], wr=[t_rt])
        cp("dve", pc[0:64, 0:2], pci[0:64, 0:2], [t_rt], [t_rt])
        ts("dve", pc[0:64, 2:3], pc[0:64, 0:1], 0.0, None, ALU.is_equal, ALU.bypass, [t_rt], [t_rt])
        act(pc[0:64, 3:4], pc[0:64, 1:2], AF.Exp, [t_rt], [t_rt], scale=-math.log(10000.0) / 16.0)
        ts("dve", pc[0:64, 4:5], pf[0:64, 0:1], 1.0 / 16.0, -1.0, ALU.mult, ALU.add, [tC], [t_rt])
        tt("dve", t1[0:64, :], t1[0:64, :], t2[0:64, :], ALU.subtract, [t_rt], [t_rt])
        stt(t1[0:64, :], t1[0:64, :], pc[0:64, 2:3], t2[0:64, :], ALU.mult, ALU.add, [t_rt], [t_rt])
        ts("dve", t1[0:64, :], t1[0:64, :], pc[0:64, 3:4], None, ALU.mult, ALU.bypass, [t_rt], [t_rt])
        t2i = t2.bitcast(I32)
        ts("dve", t2i[0:64, :], t1[0:64, :], 1.0 / TWO_PI, None, ALU.mult, ALU.bypass, [t_rt], [t_rt])
        cp("dve", t3[0:64, :], t2i[0:64, :], [t_rt], [t_rt])
        stt(t1[0:64, :], t3[0:64, :], -TWO_PI, t1[0:64, :], ALU.mult, ALU.add, [t_rt], [t_rt])
        ts("dve", t1[0:64, :], t1[0:64, :], math.pi, -math.pi, ALU.min, ALU.max, [t_rt], [t_rt])
        act(t2[0:64, :], t1[0:64, :], AF.Sin, [t_rt], [t_rt])
        act(t3[0:64, :], t1[0:64, :], AF.Abs, [t_rt], [t_rt])
        act(t3[0:64, :], t3[0:64, :], AF.Sin, [t_rt], [t_rt], scale=-1.0, bias=math.pi / 2)
        ts("dve", t2[0:64, :], t2[0:64, :], pc[0:64, 4:5], None, ALU.mult, ALU.bypass, [t_rt], [t_rt])
        cosT, sinT = t3, t2
        kT = ar.bf(4096)
        qT = ar.bf(4096)
        t_kT, t_qT = Trk(), Trk()
        vf = ar.f32(2048).rearrange("p (t q) -> p t q", q=64)
        vb = ar.bf(2048).rearrange("p (t q) -> p t q", q=64)
        t_v = Trk()
        ckf = ar.f32(256).rearrange("p (t q) -> p t q", q=64)
        ckT = ar.bf(512)
        cvf = ar.f32(256).rearrange("p (t q) -> p t q", q=64)
        cvb = ar.bf(256).rearrange("p (t q) -> p t q", q=64)
        t_ck = Trk()
        etr = Rot([ar.bf(512) for _ in range(4)])
        rtmp = Rot([ar.f32(512) for _ in range(2)])
        dnr = Rot([ar.f32(512) for _ in range(2)])
        otr = Rot([ar.bf(512) for _ in range(2)])
        tbk = [Trk() for _ in range(8)]

        def rope(xT, t_x, ncol, cT=None, sT=None):
            cT = cosT if cT is None else cT
            sT = sinT if sT is None else sT
            for b0 in range(0, ncol, 512):
                sl = slice(b0, b0 + 512)
                mm(banks[7][0:64, :], perm_b[0:64, 0:64], xT[0:64, sl], True, True, [tC, t_x], [tbk[7]])
                rt, t_r = rtmp.next()
                tt("dve", rt[0:64, :], banks[7][0:64, :], sT[0:64, sl], ALU.mult, [tbk[7], t_rt], [t_r])
                tt("pool", xT[0:64, sl], xT[0:64, sl], cT[0:64, sl], ALU.mult, [t_rt], [t_x])
                tt("dve", xT[0:64, sl], xT[0:64, sl], rt[0:64, :], ALU.add, [t_r], [t_x])

        hm = ar.f32(2)
        S.dma(hm, hmask, wr=[t_rt])
        cos_h = ar.f32(LH)
        sin_h = ar.f32(LH)
        for (dst_, src_) in ((cos_h, cosT), (sin_h, sinT)):
            ts("pool", dst_[0:64, :], src_[0:64, 0:LH], hm[0:64, 0:1], None, ALU.mult, ALU.bypass, [t_rt], [t_rt])
            stt(dst_[0:64, :], src_[0:64, LH:2 * LH], hm[0:64, 1:2], dst_[0:64, :], ALU.mult, ALU.add, [t_rt], [t_rt])
        kTw = ar.bf(18 * 128)
        vbw = ar.bf(18 * 64).rearrange("p (t q) -> p t q", q=64)
        t_kw, t_vw = Trk(), Trk()

        def finish_o(nps, dps, tn, td, h, n, col0, dst=None):
            dn, t_dn = dnr.next()
            ts("dve", dn[0:64, 0:n], dps[0:64, 0:n], esink[0:64, h:h + 1], None, ALU.add, ALU.bypass, [td, t_es], [t_dn])
            E("dve", lambda e: e.reciprocal(dn[0:64, 0:n], dn[0:64, 0:n]), rd=[t_dn], wr=[t_dn])
            ot, t_ot = otr.next()
            tt("dve", ot[0:64, 0:n], nps[0:64, 0:n], dn[0:64, 0:n], ALU.mult, [tn, t_dn], [t_ot])
            if dst is None:
                S.dma(mixT[768 + h * 64: 768 + (h + 1) * 64, col0:col0 + n], ot[0:64, 0:n], rd=[t_ot])
            else:
                S.dma(dst[h * 64:(h + 1) * 64, col0:col0 + n], ot[0:64, 0:n], rd=[t_ot])

        cnt = 0
        for (off, L, lat, si) in SEQS:
            nt = L // 128
            for j in range(4):
                S.dma(kT[0:64, 0:L], projT[2560 + j * 64: 2560 + (j + 1) * 64, off:off + L], wr=[t_kT])
                S.dma(vf[:, 0:nt, :], tokm[off:off + L, 280 + j * 64: 280 + (j + 1) * 64].rearrange("(t p) q -> p t q", p=128),
                      wr=[t_v])
                cp("act", vb[:, 0:nt, :], vf[:, 0:nt, :], [t_v], [t_v])
                if lat:
                    rope(kT, t_kT, L)
                    S.dma(ckf, ckin[l, :, j * 64:(j + 1) * 64].rearrange("(t p) q -> p t q", p=128), wr=[t_ck])
                    S.dma(cvf, cvin[l, :, j * 64:(j + 1) * 64].rearrange("(t p) q -> p t q", p=128), wr=[t_ck])
                    for t_ in range(4):
                        E("pe", lambda e, o=banks[7][0:64, t_ * 128:(t_ + 1) * 128], i=ckf[:, t_, :]: e.transpose(o, i, ident_f[:]),
                          rd=[t_ck, tC], wr=[tbk[7]])
                    cp("dve", ckT[0:64, :], banks[7][0:64, :], [tbk[7]], [t_ck])
                    cp("act", cvb, cvf, [t_ck], [t_ck])
                    h0c, h1c = hm[0:64, 0:1], hm[0:64, 1:2]
                    ts("pool", kTw[0:64, 0:128], kT[0:64, LH - 128:LH], h1c, None, ALU.mult, ALU.bypass, [t_kT, t_rt], [t_kw])
                    ts("pool", kTw[0:64, 128:128 + LH], kT[0:64, 0:LH], h0c, None, ALU.mult, ALU.bypass, [t_kT, t_rt], [t_kw])
                    stt(kTw[0:64, 128:128 + LH], kT[0:64, LH:2 * LH], h1c, kTw[0:64, 128:128 + LH], ALU.mult, ALU.add,
                        [t_kT, t_rt], [t_kw])
                    ts("pool", kTw[0:64, 128 + LH:256 + LH], kT[0:64, LH:LH + 128], h0c, None, ALU.mult, ALU.bypass, [t_kT, t_rt], [t_kw])
                    ts("pool", vbw[:, 0, :], vb[:, 15, :], hm[:, 1:2], None, ALU.mult, ALU.bypass, [t_v, t_rt], [t_vw])
                    ts("pool", vbw[:, 1:17, :], vb[:, 0:16, :], hm[:, 0:1], None, ALU.mult, ALU.bypass, [t_v, t_rt], [t_vw])
                    stt(vbw[:, 1:17, :], vb[:, 16:32, :], hm[:, 1:2], vbw[:, 1:17, :], ALU.mult, ALU.add, [t_v, t_rt], [t_vw])
                    ts("pool", vbw[:, 17, :], vb[:, 16, :], hm[:, 0:1], None, ALU.mult, ALU.bypass, [t_v, t_rt], [t_vw])
                for gq in range(3):
                    h = j * 3 + gq
                    if lat:
                        S.dma(qT[0:64, 0:LH], projT_h[1792 + h * 64: 1792 + (h + 1) * 64, 0:LH], wr=[t_qT])
                    else:
                        S.dma(qT[0:64, 0:L], projT[1792 + h * 64: 1792 + (h + 1) * 64, off:off + L], wr=[t_qT])
                    if not lat:
                        nps, dps, tn, td = banks[4], banks[5], tbk[4], tbk[5]
                        for kt in range(2):
                            sp_, tsp = banks[cnt % 2], tbk[cnt % 2]
                            cnt += 1
                            mm(sp_[:, 0:256], kT[0:64, kt * 128:(kt + 1) * 128], qT[0:64, 0:256], True, True, [t_kT, t_qT], [tsp])
                            et, t_et = etr.next()
                            act(et[:, 0:256], sp_[:, 0:256], AF.Exp, [tsp], [t_et], scale=0.125)
                            mm(nps[0:64, 0:256], vb[:, kt, :], et[:, 0:256], kt == 0, kt == 1, [t_v, t_et], [tn])
                            mm(dps[0:64, 0:256], ones_b[:, 0:64], et[:, 0:256], kt == 0, kt == 1, [tC, t_et], [td])
                        finish_o(nps, dps, tn, td, h, 256, off)
                    else:
                        rope(qT, t_qT, LH, cos_h, sin_h)
                        for qb in range(LH // 512):
                            pi_ = qb % 2
                            nps, dps, tn, td = banks[4 + pi_], banks[2 + pi_], tbk[4 + pi_], tbk[2 + pi_]
                            qs_ = slice(qb * 512, (qb + 1) * 512)
                            tasks = [("ctx", kt) for kt in range(4)] + [("win", qs) for qs in range(4)]

                            def score(task):
                                nonlocal cnt
                                kind, idx = task
                                if kind == "ctx":
                                    sp_, tsp = banks[cnt % 2], tbk[cnt % 2]
                                    cnt += 1
                                    mm(sp_[:, :], ckT[0:64, idx * 128:(idx + 1) * 128], qT[0:64, qs_], True, True, [t_ck, t_qT], [tsp])
                                    et, t_et = etr.next()
                                    act(et, sp_[:, :], AF.Exp, [tsp], [t_et], scale=0.125)
                                    return (kind, idx, et, t_et, None)
                                n_ = qb * 4 + idx
                                kts = [n_ + dk for dk in (-1, 0, 1)]
                                sp_, tsp = banks[6 + cnt % 2], tbk[6 + cnt % 2]
                                cnt += 1
                                for ii, kt in enumerate(kts):
                                    dk = kt - n_
                                    cols = slice(ii * 128, (ii + 1) * 128)
                                    mm(sp_[:, cols], kTw[0:64, (kt + 1) * 128:(kt + 2) * 128], qT[0:64, n_ * 128:(n_ + 1) * 128],
                                       True, dk == 0, [t_kw, t_qT], [tsp])
                                    if dk != 0:
                                        mm(sp_[:, cols], ident_b[:], (nmb[1] if dk == -1 else nmb[0])[:], False, True, [tC], [tsp])
                                et, t_et = etr.next()
                                nk = len(kts) * 128
                                act(et[:, 0:nk], sp_[:, 0:nk], AF.Exp, [tsp], [t_et], scale=0.125)
                                if n_ == 0:
                                    ts("dve", et[:, 0:128], et[:, 0:128], hm[:, 1:2], None, ALU.mult, ALU.bypass, [t_rt], [t_et])
                                if n_ == LH // 128 - 1:
                                    ts("dve", et[:, 256:384], et[:, 256:384], hm[:, 0:1], None, ALU.mult, ALU.bypass, [t_rt], [t_et])
                                return (kind, idx, et, t_et, kts)

                            def pv(res):
                                kind, idx, et, t_et, kts = res
                                if kind == "ctx":
                                    mm(nps[0:64, :], cvb[:, idx, :], et, idx == 0, False, [t_ck, t_et], [tn])
                                    mm(dps[0:64, :], ones_b[:, 0:64], et, idx == 0, False, [tC, t_et], [td])
                                    return
                                for ii, kt in enumerate(kts):
                                    last = (idx == 3 and ii == len(kts) - 1)
                                    mm(nps[0:64, idx * 128:(idx + 1) * 128], vbw[:, kt + 1, :], et[:, ii * 128:(ii + 1) * 128], False, last,
                                       [t_vw, t_et], [tn])
                                    mm(dps[0:64, idx * 128:(idx + 1) * 128], ones_b[:, 0:64], et[:, ii * 128:(ii + 1) * 128], False, last,
                                       [tC, t_et], [td])

                            prev_r = score(tasks[0])
                            for tk_ in tasks[1:]:
                                nxt_r = score(tk_)
                                pv(prev_r)
                                prev_r = nxt_r
                            pv(prev_r)
                            finish_o(nps, dps, tn, td, h, 512, qb * 512, dst=mixA_h)

    def phase_s5h(l, d):
        ar.reset()
        Q = 128
        t_su = Trk()
        W = [t_su]
        tbk = [Trk() for _ in range(8)]
        ta = ar.f32(16 * 64).rearrange("p (k t) -> p k t", k=16)
        tb_ = ar.f32(16 * 64).rearrange("p (k t) -> p k t", k=16)
        Bre = ar.f32(256).rearrange("p (k c) -> p k c", k=16)
        Bim = ar.f32(256).rearrange("p (k c) -> p k c", k=16)
        b1 = ar.f32(256).rearrange("p (k c) -> p k c", k=16)
        b2 = ar.f32(256).rearrange("p (k c) -> p k c", k=16)
        b3 = ar.f32(256).rearrange("p (k c) -> p k c", k=16)
        b4 = ar.f32(256).rearrange("p (k c) -> p k c", k=16)
        cn = ar.f32(4 * 128).rearrange("p (a q) -> p a q", a=4)
        pads = [ar.bf(16 * 128).rearrange("p (k c) -> p k c", k=16) for _ in range(2)]
        CTf = [ar.f32(16 * 32).rearrange("p (k c) -> p k c", k=16) for _ in range(2)]
        CTb = [ar.bf(16 * 32).rearrange("p (k c) -> p k c", k=16) for _ in range(2)]
        Kall = ar.bf(16 * 4 * 32).rearrange("p (k e c) -> p k e c", k=16, e=4)
        sc_ = ar.f32(16 * 24).rearrange("p (v k) -> p v k", v=24)
        PW = ar.f32(5 * 2 * 16).rearrange("p (e c k) -> p e c k", e=5, c=2)
        kqi = ar.i32(16)
        are, aim, ldt = sc_[:, 0, :], sc_[:, 1, :], sc_[:, 2, :]
        S.dma(are, s5_A_re[l, d].rearrange("(k q) -> q k", q=128), wr=W, allow_slow_non_contiguous=True)
        S.dma(aim, s5_A_im[l, d].rearrange("(k q) -> q k", q=128), wr=W, allow_slow_non_contiguous=True)
        lv = s5_log_dt[l, d].rearrange("(k t) -> t k", t=2)
        for g2 in range(2):
            S.dma(ldt[g2 * 64:(g2 + 1) * 64, :], lv[g2].partition_broadcast(64), wr=W, allow_slow_non_contiguous=True)
        step, a_, th, r_, kq_ = sc_[:, 3, :], sc_[:, 4, :], sc_[:, 5, :], sc_[:, 6, :], sc_[:, 7, :]
        c1, s1, lr, li = sc_[:, 8, :], sc_[:, 9, :], sc_[:, 10, :], sc_[:, 11, :]
        den, kr, ki, tmp1, tmp2 = sc_[:, 12, :], sc_[:, 13, :], sc_[:, 14, :], sc_[:, 15, :], sc_[:, 16, :]
        Cm, Sm, tmp3 = sc_[:, 17, :], sc_[:, 18, :], sc_[:, 19, :]
        r4, c4, s4, rr = sc_[:, 20, :], sc_[:, 21, :], sc_[:, 22, :], sc_[:, 23, :]
        act(step, ldt, AF.Exp, W, W)
        tt("dve", a_, are, step, ALU.mult, W, W)
        tt("dve", th, aim, step, ALU.mult, W, W)
        act(r_, a_, AF.Exp, W, W)
        ts("dve", kqi, th, 1.0 / TWO_PI, None, ALU.mult, ALU.bypass, W, W)
        cp("dve", kq_, kqi, W, W)
        stt(tmp1, kq_, -TWO_PI, th, ALU.mult, ALU.add, W, W)
        ts("dve", tmp1, tmp1, math.pi, -math.pi, ALU.min, ALU.max, W, W)
        act(s1, tmp1, AF.Sin, W, W)
        act(tmp2, tmp1, AF.Abs, W, W)
        act(c1, tmp2, AF.Sin, W, W, scale=-1.0, bias=math.pi / 2)
        tt("dve", lr, r_, c1, ALU.mult, W, W)
        tt("dve", li, r_, s1, ALU.mult, W, W)
        tt("dve", den, are, are, ALU.mult, W, W)
        tt("dve", tmp1, aim, aim, ALU.mult, W, W)
        tt("dve", den, den, tmp1, ALU.add, W, W)
        E("dve", lambda e, o=den: e.reciprocal(o, o), rd=W, wr=W)
        ts("dve", tmp1, lr, -1.0, None, ALU.add, ALU.bypass, W, W)
        tt("dve", kr, tmp1, are, ALU.mult, W, W)
        tt("dve", tmp2, li, aim, ALU.mult, W, W)
        tt("dve", kr, kr, tmp2, ALU.add, W, W)
        tt("dve", kr, kr, den, ALU.mult, W, W)
        tt("dve", ki, li, are, ALU.mult, W, W)
        tt("dve", tmp2, tmp1, aim, ALU.mult, W, W)
        tt("dve", ki, ki, tmp2, ALU.subtract, W, W)
        tt("dve", ki, ki, den, ALU.mult, W, W)

        def cmul16(o_re, o_im, a_re, a_im, b_re, b_im):
            tt("dve", tmp1, a_re, b_re, ALU.mult, W, W)
            tt("dve", tmp2, a_im, b_im, ALU.mult, W, W)
            tt("dve", tmp3, a_re, b_im, ALU.mult, W, W)
            tt("dve", rr, a_im, b_re, ALU.mult, W, W)
            tt("dve", o_re, tmp1, tmp2, ALU.subtract, W, W)
            tt("dve", o_im, tmp3, rr, ALU.add, W, W)

        memset("dve", PW[:, 0, 0, :], 1.0, W)
        memset("dve", PW[:, 0, 1, :], 0.0, W)
        cp("dve", PW[:, 1, 0, :], lr, W, W)
        cp("dve", PW[:, 1, 1, :], li, W, W)
        cmul16(PW[:, 2, 0, :], PW[:, 2, 1, :], lr, li, lr, li)
        cmul16(PW[:, 3, 0, :], PW[:, 3, 1, :], PW[:, 2, 0, :], PW[:, 2, 1, :], lr, li)
        cmul16(PW[:, 4, 0, :], PW[:, 4, 1, :], PW[:, 2, 0, :], PW[:, 2, 1, :], PW[:, 2, 0, :], PW[:, 2, 1, :])
        tt("dve", r4, r_, r_, ALU.mult, W, W)
        tt("dve", r4, r4, r4, ALU.mult, W, W)
        E("dve", lambda e: e.reciprocal(rr, r4), rd=W, wr=W)
        tt("dve", c4, PW[:, 4, 0, :], rr, ALU.mult, W, W)
        tt("dve", s4, PW[:, 4, 1, :], rr, ALU.mult, W, W)
        cosT = ar.f32(16 * Q).rearrange("p (k t) -> p k t", k=16)
        sinT = ar.f32(16 * Q).rearrange("p (k t) -> p k t", k=16)
        dec = ar.f32(16 * Q).rearrange("p (k t) -> p k t", k=16)
        dec64 = ar.f32(16 * 64).rearrange("p (k t) -> p k t", k=16)
        memset("dve", cosT[:, :, 0:1], 1.0, W)
        memset("dve", sinT[:, :, 0:1], 0.0, W)
        cp("dve", Cm, c4, W, W)
        cp("dve", Sm, s4, W, W)
        m = 1
        while m < Q:
            Cb = Cm.unsqueeze(2).to_broadcast([128, 16, m])
            Sb = Sm.unsqueeze(2).to_broadcast([128, 16, m])
            tt("dve", ta[:, :, 0:m], cosT[:, :, 0:m], Cb, ALU.mult, W, W)
            tt("dve", tb_[:, :, 0:m], sinT[:, :, 0:m], Sb, ALU.mult, W, W)
            tt("dve", cosT[:, :, m:2 * m], ta[:, :, 0:m], tb_[:, :, 0:m], ALU.subtract, W, W)
            tt("dve", ta[:, :, 0:m], sinT[:, :, 0:m], Cb, ALU.mult, W, W)
            tt("dve", tb_[:, :, 0:m], cosT[:, :, 0:m], Sb, ALU.mult, W, W)
            tt("dve", sinT[:, :, m:2 * m], ta[:, :, 0:m], tb_[:, :, 0:m], ALU.add, W, W)
            tt("dve", tmp1, Cm, Cm, ALU.mult, W, W)
            tt("dve", tmp2, Sm, Sm, ALU.mult, W, W)
            tt("dve", tmp3, Cm, Sm, ALU.mult, W, W)
            tt("dve", Cm, tmp1, tmp2, ALU.subtract, W, W)
            ts("dve", Sm, tmp3, 2.0, None, ALU.mult, ALU.bypass, W, W)
            m *= 2
        cp("dve", dec, r4.unsqueeze(2).to_broadcast([128, 16, Q]), W, W)
        cp("dve", dec64, r4.unsqueeze(2).to_broadcast([128, 16, 64]), W, W)
        if d == 0:
            memset("dve", dec[:, :, 0:1], 0.0, W)
            memset("dve", dec64[:, :, 0:1], 0.0, W)
        else:
            taf = ta.rearrange("p k t -> p (k t)")
            for T_ in (cosT, sinT):
                for k in range(16):
                    rev = bass.AP(T_.tensor, T_[:, k, Q - 1:Q].offset, [[T_.ap[0][0], 128], [-1, Q]])
                    cp("dve", taf[:, 0:Q], rev, W, W)
                    cp("dve", T_[:, k, :], taf[:, 0:Q], W, W)
            memset("dve", dec[:, :, Q - 1:Q], 0.0, W)
            memset("dve", dec64[:, :, 63:64], 0.0, W)
        S.dma(Bre, s5_B_re[l].rearrange("(k q) c -> q k c", q=128), wr=W)
        S.dma(Bim, s5_B_im[l].rearrange("(k q) c -> q k c", q=128), wr=W)
        krb = kr.unsqueeze(2).to_broadcast([128, 16, 16])
        kib = ki.unsqueeze(2).to_broadcast([128, 16, 16])
        tt("dve", b1, Bre, krb, ALU.mult, W, W)
        tt("dve", b3, Bim, kib, ALU.mult, W, W)
        tt("dve", b1, b1, b3, ALU.subtract, W, W)
        tt("dve", b2, Bim, krb, ALU.mult, W, W)
        tt("dve", b3, Bre, kib, ALU.mult, W, W)
        tt("dve", b2, b2, b3, ALU.add, W, W)
        for pz in pads:
            memset("dve", pz, 0.0, W)
        for s_ in range(4):
            e_ = (3 - s_) if d == 0 else s_
            pr_ = PW[:, e_, 0, :].unsqueeze(2).to_broadcast([128, 16, 16])
            pi_ = PW[:, e_, 1, :].unsqueeze(2).to_broadcast([128, 16, 16])
            tt("dve", b3, b1, pr_, ALU.mult, W, W)
            tt("dve", b4, b2, pi_, ALU.mult, W, W)
            for g2 in range(2):
                hs = slice(g2 * 64, (g2 + 1) * 64)
                tt("dve", pads[0][hs, :, s_ * 32 + g2 * 16: s_ * 32 + (g2 + 1) * 16], b3[hs], b4[hs], ALU.subtract, W, W)
            tt("dve", b3, b2, pr_, ALU.mult, W, W)
            tt("dve", b4, b1, pi_, ALU.mult, W, W)
            for g2 in range(2):
                hs = slice(g2 * 64, (g2 + 1) * 64)
                tt("dve", pads[1][hs, :, s_ * 32 + g2 * 16: s_ * 32 + (g2 + 1) * 16], b3[hs], b4[hs], ALU.add, W, W)
        WT = []
        for pz in pads:
            wt_ = ar.bf(16 * 128).rearrange("p (k q) -> p k q", k=16)
            for k4 in range(0, 16, 4):
                pb = banks[7][:, 0:256].bitcast(BF16)
                for kk in range(4):
                    tr(pb[:, kk * 128:(kk + 1) * 128], pz[:, k4 + kk, :], ident_b[:], W + [tC], [tbk[7]])
                cp("dve", wt_[:, k4:k4 + 4, :], pb.rearrange("p (k q) -> p k q", k=4), [tbk[7]], W)
            WT.append(wt_)
        for ci_, csrc in enumerate((s5_C_re, s5_C_im)):
            memset("dve", CTf[ci_], 0.0, W)
            cv_ = csrc[l, d].rearrange("g c p -> (g c) p").rearrange("(a q) p -> q a p", q=128)
            S.dma(cn[:, :, 0:64], cv_, wr=W)
            S.dma(cn[:, :, 64:128], cv_, wr=W)
            for a in range(4):
                E("pe", lambda e, o=banks[6][:, 0:128], i=cn[:, a, :]: e.transpose(o, i, ident_f[:]), rd=W + [tC], wr=[tbk[6]])
                for g2 in range(2):
                    src = banks[6][g2 * 64:(g2 + 1) * 64, 0:128].rearrange("p (a2 gg c) -> p a2 gg c", a2=4, gg=2)[:, :, g2, :]
                    dst = CTf[ci_][g2 * 64:(g2 + 1) * 64, a * 4:(a + 1) * 4, g2 * 16:(g2 + 1) * 16]
                    cp("dve", dst, src, [tbk[6]], W)
        cp("dve", CTb[0], CTf[0], W, W)
        ts("dve", CTb[1], CTf[1], -1.0, None, ALU.mult, ALU.bypass, W, W)
        CH = [ar.bf(16 * 128).rearrange("p (k q) -> p k q", k=16) for _ in range(2)]
        cta = ta.rearrange("p k t -> p (k t)")[:, 0:512].rearrange("p (k c) -> p k c", k=16)
        ctb = tb_.rearrange("p k t -> p (k t)")[:, 0:512].rearrange("p (k c) -> p k c", k=16)
        for tau in range(4):
            f_ = (tau + 1) if d == 0 else (4 - tau)
            pr_ = PW[:, f_, 0, :].unsqueeze(2).to_broadcast([128, 16, 32])
            pi_ = PW[:, f_, 1, :].unsqueeze(2).to_broadcast([128, 16, 32])
            tt("dve", cta, CTf[0], pr_, ALU.mult, W, W)
            tt("dve", ctb, CTf[1], pi_, ALU.mult, W, W)
            tt("dve", CH[0][:, :, tau * 32:(tau + 1) * 32], cta, ctb, ALU.subtract, W, W)
            tt("dve", cta, CTf[0], pi_, ALU.mult, W, W)
            tt("dve", ctb, CTf[1], pr_, ALU.mult, W, W)
            tt("dve", cta, cta, ctb, ALU.add, W, W)
            ts("dve", CH[1][:, :, tau * 32:(tau + 1) * 32], cta, -1.0, None, ALU.mult, ALU.bypass, W, W)
        for k4 in range(0, 16, 4):
            for kk in range(4):
                k = k4 + kk
                for e_ in range(4):
                    s_blk = (3 - e_) if d == 0 else e_
                    o_ = banks[5][0:32, kk * 128 + e_ * 32: kk * 128 + (e_ + 1) * 32]
                    mm(o_, pads[0][:, k, s_blk * 32:(s_blk + 1) * 32], CTb[0][:, k, :], True, False, W, [tbk[5]])
                    mm(o_, pads[1][:, k, s_blk * 32:(s_blk + 1) * 32], CTb[1][:, k, :], False, True, W, [tbk[5]])
            cp("dve", Kall[0:32, k4:k4 + 4, :, :], banks[5][0:32, :].rearrange("p (k e c) -> p k e c", k=4, e=4), [tbk[5]], W)
        TOE = ar.bf(16 * 128).rearrange("p (k q) -> p k q", k=16)
        t_toe = Trk()
        memset("dve", TOE, 0.0, [t_toe])
        for s_ in range(4):
            for tau in range(4):
                dl = (tau - s_) if d == 0 else (s_ - tau)
                if dl < 0:
                    continue
                S.dma(TOE[s_ * 32:(s_ + 1) * 32, :, tau * 32:(tau + 1) * 32], Kall[0:32, :, dl, :], rd=W, wr=[t_toe])
        urot = Rot([ar.bf(16 * Q).rearrange("p (k t) -> p k t", k=16) for _ in range(2)])
        ginb = Rot([ar.f32(2 * 4 * Q) for _ in range(2)])
        goutb = Rot([ar.f32(2 * 4 * Q) for _ in range(2)])
        tmpA = Rot([ar.f32(4 * Q) for _ in range(2)])
        Hfb = Rot([ar.f32(2 * 4 * Q) for _ in range(2)])
        Hbb = Rot([ar.bf(2 * 4 * (Q + 2)) for _ in range(2)])
        ysr = Rot([ar.f32(4 * Q) for _ in range(2)])
        Hin = ar.f32(32).rearrange("p (c k) -> p c k", c=2)
        carry = ar.f32(64).rearrange("p (c k) -> p c k", c=4)
        ct = ar.f32(64).rearrange("p (c k) -> p c k", c=4)
        t_car = Trk()
        dst4 = s5x4[d]
        bodyno = [0]

        def stage1(j0, n, a, u_, t_u):
            bn = bodyno[0]
            bodyno[0] += 1
            ks = slice(a * 4, (a + 1) * 4)
            bre, bim = banks[(bn % 2) * 2], banks[(bn % 2) * 2 + 1]
            tre, tim = tbk[(bn % 2) * 2], tbk[(bn % 2) * 2 + 1]
            for kq in range(4):
                k = a * 4 + kq
                mm(bre[:, kq * Q:kq * Q + n], WT[0][:, k, :], u_[:, k, 0:n], True, True, W + [t_u], [tre])
                mm(bim[:, kq * Q:kq * Q + n], WT[1][:, k, :], u_[:, k, 0:n], True, True, W + [t_u], [tim])
            br = bre[:, :].rearrange("p (k t) -> p k t", k=4)[:, :, 0:n]
            bi = bim[:, :].rearrange("p (k t) -> p k t", k=4)[:, :, 0:n]
            tsl = slice(0, n) if d == 0 else slice(Q - n, Q)
            cT_, sT_ = cosT[:, ks, tsl], sinT[:, ks, tsl]
            gb, t_gi = ginb.next()
            gi_ = gb[:, 0:2 * 4 * n].rearrange("p (c k t) -> p c k t", c=2, k=4)
            tAb, t_tA = tmpA.next()
            tA = tAb[:, 0:4 * n].rearrange("p (k t) -> p k t", k=4)
            tt("dve", gi_[:, 0, :, :], br, cT_, ALU.mult, [tre] + W, [t_gi])
            tt("dve", tA, bi, sT_, ALU.mult, [tim] + W, [t_tA])
            tt("pool", gi_[:, 0, :, :], gi_[:, 0, :, :], tA, ALU.add, [t_tA], [t_gi])
            tAb, t_tA = tmpA.next()
            tA = tAb[:, 0:4 * n].rearrange("p (k t) -> p k t", k=4)
            tt("dve", gi_[:, 1, :, :], bi, cT_, ALU.mult, [tim] + W, [t_gi])
            tt("dve", tA, br, sT_, ALU.mult, [tre] + W, [t_tA])
            tt("pool", gi_[:, 1, :, :], gi_[:, 1, :, :], tA, ALU.subtract, [t_tA], [t_gi])
            return dict(j0=j0, n=n, a=a, ks=ks, gi=gi_, t_gi=t_gi, bn=bn, cT=cT_, sT=sT_, u=u_, t_u=t_u)

        def stage2(cx):
            j0, n, a, ks, gi_, t_gi, bn, cT_, sT_, u_, t_u = (cx[k] for k in ("j0", "n", "a", "ks", "gi", "t_gi", "bn", "cT", "sT", "u", "t_u"))
            first = 0 if d == 0 else n - 1
            lastp = n - 1 if d == 0 else 0
            tt("dve", ct[:, 0, ks], Hin[:, 0, ks], c4[:, ks], ALU.mult, W + [t_car], [t_car])
            tt("dve", ct[:, 1, ks], Hin[:, 1, ks], s4[:, ks], ALU.mult, W, [t_car])
            tt("dve", ct[:, 2, ks], Hin[:, 0, ks], s4[:, ks], ALU.mult, W, [t_car])
            tt("dve", ct[:, 3, ks], Hin[:, 1, ks], c4[:, ks], ALU.mult, W, [t_car])
            tt("dve", carry[:, 0, ks], ct[:, 0, ks], ct[:, 1, ks], ALU.subtract, [t_car], [t_car])
            tt("dve", carry[:, 1, ks], ct[:, 2, ks], ct[:, 3, ks], ALU.add, [t_car], [t_car])
            for cc in range(2):
                tt("dve", carry[:, 2 + cc, ks], carry[:, cc, ks], r4[:, ks], ALU.mult, W, [t_car])
                tt("dve", gi_[:, cc, :, first], gi_[:, cc, :, first], carry[:, 2 + cc, ks], ALU.add, [t_car], [t_gi])
            gob, t_go = goutb.next()
            go = gob[:, 0:2 * 4 * n].rearrange("p (c k t) -> p c k t", c=2, k=4)
            dsrc = (dec if n == Q else dec64)[:, ks, :]
            for cc in range(2):
                fin = gi_[:, cc, :, :].rearrange("p k t -> p (k t)")
                fo = go[:, cc, :, :].rearrange("p k t -> p (k t)")
                fd = dsrc.rearrange("p k t -> p (k t)")
                if d == 1:
                    def rv(x):
                        return bass.AP(x.tensor, x[:, 4 * n - 1:4 * n].offset, [[x.ap[0][0], 128], [-1, 4 * n]])
                    fin, fo, fd = rv(fin), rv(fo), rv(fd)
                E("dve", lambda e, o=fo, a0=fd, a1=fin: e.tensor_tensor_scan(o, a0, a1, 0.0, ALU.mult, ALU.add),
                  rd=[t_gi] + W, wr=[t_go])
            hfb_, t_hf = Hfb.next()
            Hf = hfb_[:, 0:2 * 4 * n].rearrange("p (c k t) -> p c k t", c=2, k=4)
            tAb, t_tA = tmpA.next()
            tA = tAb[:, 0:4 * n].rearrange("p (k t) -> p k t", k=4)
            tt("dve", Hf[:, 0, :, :], go[:, 0, :, :], cT_, ALU.mult, [t_go] + W, [t_hf])
            tt("pool", tA, go[:, 1, :, :], sT_, ALU.mult, [t_go] + W, [t_tA])
            tt("dve", Hf[:, 0, :, :], Hf[:, 0, :, :], tA, ALU.subtract, [t_tA], [t_hf])
            tAb, t_tA = tmpA.next()
            tA = tAb[:, 0:4 * n].rearrange("p (k t) -> p k t", k=4)
            tt("dve", Hf[:, 1, :, :], go[:, 1, :, :], cT_, ALU.mult, [t_go] + W, [t_hf])
            tt("pool", tA, go[:, 0, :, :], sT_, ALU.mult, [t_go] + W, [t_tA])
            tt("dve", Hf[:, 1, :, :], Hf[:, 1, :, :], tA, ALU.add, [t_tA], [t_hf])
            hbb_, t_hb = Hbb.next()
            Hb = hbb_[:, 0:2 * 4 * (n + 2)].rearrange("p (c k t) -> p c k t", c=2, k=4)
            bcol = 0 if d == 0 else n + 1
            cp("act", Hb[:, :, :, 1:n + 1], Hf, [t_hf], [t_hb])
            cp("dve", Hb[:, :, :, bcol], Hin[:, :, ks], [t_car], [t_hb])
            cp("dve", Hin[:, :, ks], Hf[:, :, :, lastp], [t_hf], [t_car])
            sh = slice(0, n) if d == 0 else slice(2, n + 2)
            yb, tyb = banks[4 + bn % 2], tbk[4 + bn % 2]
            for kq in range(4):
                k = a * 4 + kq
                o_ = yb[:, kq * Q:kq * Q + n]
                mm(o_, CH[0][:, k, :], Hb[:, 0, kq, sh], True, False, W + [t_hb], [tyb])
                mm(o_, CH[1][:, k, :], Hb[:, 1, kq, sh], False, False, W + [t_hb], [tyb])
                mm(o_, TOE[:, k, :], u_[:, k, 0:n], False, True, [t_toe, t_u], [tyb])
            ysb, t_ys = ysr.next()
            ys = ysb[:, 0:4 * n].rearrange("p (k t) -> p k t", k=4)
            cp("act", ys, yb[:, :].rearrange("p (k t) -> p k t", k=4)[:, :, 0:n], [tyb], [t_ys])
            for tau in range(4):
                S.dma(dst4[tau, a * 128:(a + 1) * 128, j0:j0 + n].rearrange("(k i) j -> i k j", i=32),
                      ys[tau * 32:(tau + 1) * 32, :, :], rd=[t_ys])

        for (off, L, lat, si) in SEQS:
            if lat:
                S.dma(Hin[:, 0, :], ss5re[l, d].rearrange("(k q) -> q k", q=128), wr=[t_car], allow_slow_non_contiguous=True)
                S.dma(Hin[:, 1, :], ss5im[l, d].rearrange("(k q) -> q k", q=128), wr=[t_car], allow_slow_non_contiguous=True)
            else:
                memset("dve", Hin, 0.0, [t_car])
            LJ = L // 4
            n = min(Q, LJ)
            nch = LJ // n
            order = range(nch) if d == 0 else range(nch - 1, -1, -1)
            prev = None
            for c in order:
                j0 = off // 4 + c * n
                u_, t_u = urot.next()
                for s_ in range(4):
                    S.dma(u_[s_ * 32:(s_ + 1) * 32, :, 0:n], u4[s_, :, j0:j0 + n].rearrange("(k i) j -> i k j", i=32), wr=[t_u])
                for a in range(4):
                    cx = stage1(j0, n, a, u_, t_u)
                    if prev is not None:
                        stage2(prev)
                    prev = cx
            stage2(prev)
            if not lat:
                S.dma(os5re[si, l, d].rearrange("(k q) -> q k", q=128), Hin[:, 0, :], rd=[t_car], allow_slow_non_contiguous=True)
                S.dma(os5im[si, l, d].rearrange("(k q) -> q k", q=128), Hin[:, 1, :], rd=[t_car], allow_slow_non_contiguous=True)

    def phase_s5fin(l):
        ar.reset()
        Wg = ar.bf(4 * 512).rearrange("p (a o) -> p a o", a=4)
        bg = ar.f32(4)
        Dc = ar.f32(4)
        t_w = Trk()
        S.dma(Wg, s5_w_glu[l].rearrange("(a p) o -> p a o", p=128), wr=[t_w], eng="pool")
        S.dma(bg, s5_b_glu[l].rearrange("(a p) -> p a", p=128), wr=[t_w], allow_slow_non_contiguous=True)
        S.dma(Dc, s5_D[l].rearrange("(a p) -> p a", p=128), wr=[t_w], allow_slow_non_contiguous=True)
        N = 512
        JB = 128
        fr = Rot([ar.f32(4 * N).rearrange("p (a t) -> p a t", a=4) for _ in range(2)])
        br_ = Rot([ar.f32(4 * N).rearrange("p (a t) -> p a t", a=4) for _ in range(2)])
        ur = Rot([ar.bf(4 * N).rearrange("p (a t) -> p a t", a=4) for _ in range(2)])
        y2 = ar.f32(4 * N).rearrange("p (a t) -> p a t", a=4)
        gb = Rot([ar.bf(4 * N).rearrange("p (a t) -> p a t", a=4) for _ in range(2)])
        sg = ar.f32(N)
        ob = Rot([ar.bf(4 * N).rearrange("p (a t) -> p a t", a=4) for _ in range(2)])
        t_y2, t_sg = Trk(), Trk()
        tbk = [Trk(), Trk()]
        cnt = 0
        for c in range(NJ // JB):
            j0 = c * JB
            g0 = 4 * j0
            f_, t_f = fr.next()
            b_, t_b = br_.next()
            u_, t_u = ur.next()
            for a in range(4):
                rs = slice(a * 128, (a + 1) * 128)
                S.dma(f_[:, a, :].rearrange("p (s j) -> p s j", s=4), s5x4[0][:, rs, j0:j0 + JB].rearrange("s p j -> p s j"), wr=[t_f])
                S.dma(b_[:, a, :].rearrange("p (s j) -> p s j", s=4), s5x4[1][:, rs, j0:j0 + JB].rearrange("s p j -> p s j"), wr=[t_b])
                S.dma(u_[:, a, :].rearrange("p (s j) -> p s j", s=4), u4[:, rs, j0:j0 + JB].rearrange("s p j -> p s j"), wr=[t_u])
            tt("dve", f_, f_, b_, ALU.add, [t_b], [t_f])
            tt("pool", b_, u_, Dc.unsqueeze(2).to_broadcast([128, 4, N]), ALU.mult, [t_u, t_w], [t_b])
            tt("dve", f_, f_, b_, ALU.add, [t_b], [t_f])
            tt("pool", y2, f_, f_, ALU.mult, [t_f], [t_y2])
            ts("dve", y2, y2, 0.044715, 1.0, ALU.mult, ALU.add, [t_y2], [t_y2])
            tt("dve", y2, y2, f_, ALU.mult, [t_f], [t_y2])
            act(y2, y2, AF.Sigmoid, [t_y2], [t_y2], scale=1.5957691216057308)
            tt("dve", f_, f_, y2, ALU.mult, [t_y2], [t_f])
            g_, t_g = gb.next()
            cp("act", g_, f_, [t_f], [t_g])
            o_, t_o = ob.next()
            for m in range(4):
                pk, tk = banks[cnt % 2], tbk[cnt % 2]
                cnt += 1
                for a in range(4):
                    mm(pk[:, :], Wg[:, a, m * 128:(m + 1) * 128], g_[:, a, :], a == 0, a == 3, [t_w, t_g], [tk])
                act(sg, pk[:, :], AF.Sigmoid, [tk, t_w], [t_sg], bias=bg[:, m:m + 1])
                tt("dve", o_[:, m, :].rearrange("p (j s) -> p s j", s=4), f_[:, m, :].rearrange("p (s j) -> p s j", s=4),
                   sg.rearrange("p (s j) -> p s j", s=4), ALU.mult, [t_f, t_sg], [t_o])
            S.dma(mixT[1536:2048, g0:g0 + N].rearrange("(a p) t -> p a t", p=128), o_, rd=[t_o])

    def postnorm(ysrcs, rd_y, xt, t_x, gw, gi, tmp, t_tmp, ss, t_ss, junk, t_junk):
        for nb in range(4):
            act(junk[:, 0:512], ysrcs[nb], AF.Square, rd_y, [t_junk, t_ss], accum=ss[:, nb:nb + 1])
        E("dve", lambda e: e.reduce_sum(ss[:, 4:5], ss[:, 0:4], axis=AX.X), rd=[t_ss], wr=[t_ss])
        ts("dve", ss[:, 5:6], ss[:, 4:5], 1.0 / D, 1e-6, ALU.mult, ALU.add, [t_ss], [t_ss])
        act(ss[:, 6:7], ss[:, 5:6], AF.Sqrt, [t_ss], [t_ss])
        E("dve", lambda e: e.reciprocal(ss[:, 7:8], ss[:, 6:7]), rd=[t_ss], wr=[t_ss])
        for nb in range(4):
            stt(tmp[:, nb * 512:(nb + 1) * 512], ysrcs[nb], ss[:, 7:8], gates[:, gw, gi, nb * 512:(nb + 1) * 512],
                ALU.mult, ALU.mult, list(rd_y) + [t_ss, t_gates], [t_tmp])
        tt("pool", xt, xt, tmp, ALU.add, [t_tmp, t_x], [t_x])

    def phase_C(l, t5, xsrc):
        ar.reset()
        T = 1024
        tok0 = t5 * T
        gi = 0 if t5 == 0 else 1
        mT = ar.bf(16 * T).rearrange("p (k t) -> p k t", k=16)
        t_m = Trk()
        if t5 == 0:
            for k in range(16):
                S.dma(mT[:, k, :], mixT[k * 128:(k + 1) * 128, tok0:tok0 + T], wr=[t_m])
        else:
            hm = ar.f32(2)
            t_hm = Trk()
            S.dma(hm, hmask, wr=[t_hm])
            arot = Rot([ar.bf(T) for _ in range(2)])
            brot = Rot([ar.bf(T) for _ in range(2)])
            cA = NPR * LP + (t5 - 1) * T
            for k in range(16):
                if 6 <= k < 12:
                    S.dma(mT[:, k, :], mixA_h[(k - 6) * 128:(k - 5) * 128, (t5 - 1) * T:t5 * T], wr=[t_m])
                    continue
                a_, t_a = arot.next()
                b_, t_b = brot.next()
                S.dma(a_, mixT[k * 128:(k + 1) * 128, cA:cA + T], wr=[t_a])
                S.dma(b_, mixT[k * 128:(k + 1) * 128, cA + LH:cA + LH + T], wr=[t_b])
                ts("pool", a_, a_, hm[:, 0:1], None, ALU.mult, ALU.bypass, [t_hm], [t_a])
                stt(mT[:, k, :], b_, hm[:, 1:2], a_, ALU.mult, ALU.add, [t_b, t_a, t_hm], [t_m])
        wo = ar.bf(16 * D).rearrange("p (k n) -> p k n", k=16)
        t_wo = Trk()
        for nb in range(4):
            S.dma(wo[:, :, nb * 512:(nb + 1) * 512],
                  WB["wout"][:, nb * 512:(nb + 1) * 512].rearrange("(k p) n -> p k n", p=128), wr=[t_wo])
        xrot = Rot([ar.f32(D) for _ in range(2)])
        tmp = ar.f32(D)
        t_tmp = Trk()
        junk = ar.bf(512)
        t_junk = Trk()
        ssr = Rot([ar.f32(8) for _ in range(2)])
        t_bk = [Trk() for _ in range(8)]
        for sub in range(T // 128):
            bo = (sub % 2) * 4
            for nb in range(4):
                for k in range(16):
                    mm(banks[bo + nb][:, :], mT[:, k, sub * 128:(sub + 1) * 128], wo[:, k, nb * 512:(nb + 1) * 512],
                       k == 0, k == 15, [t_m, t_wo], [t_bk[bo + nb]])
            xt, t_x = xrot.next()
            g0 = tok0 + sub * 128
            S.dma(xt, xsrc[g0:g0 + 128, :], wr=[t_x])
            ss, t_ss = ssr.next()
            postnorm([banks[bo + nb][:, :] for nb in range(4)], [t_bk[bo + nb] for nb in range(4)],
                     xt, t_x, 0, gi, tmp, t_tmp, ss, t_ss, junk, t_junk)
            S.dma(x1[g0:g0 + 128, :], xt, rd=[t_x])

    ffn_keep = {}

    def phase_F1(l, t10):
        ar.reset()
        T = 512
        tok0 = t10 * T
        gi = 0 if t10 < 2 else 1
        actT = ar.bf(44 * T).rearrange("p (k t) -> p k t", k=44)
        ffn_keep["actT"] = actT
        ffn_keep["mark"] = ar.p
        hT = ar.bf(16 * T).rearrange("p (k t) -> p k t", k=16)
        t_h = Trk()
        prenorm(x1, tok0, T, gi, 2, 3, hT, t_h)
        grot = Rot([ar.bf(16 * 256).rearrange("p (k n) -> p k n", k=16) for _ in range(2)])
        urot = Rot([ar.bf(16 * 256).rearrange("p (k n) -> p k n", k=16) for _ in range(2)])
        sgr = Rot([ar.f32(512) for _ in range(2)])
        t_pg = [Trk(), Trk()]
        t_pu = [Trk(), Trk()]
        t_act = Trk()
        cnt = 0
        for blk in range(22):
            wg, t_wg = grot.next()
            wu, t_wu = urot.next()
            S.dma(wg, WB["f1"][:, blk * 256:(blk + 1) * 256].rearrange("(k p) n -> p k n", p=128), wr=[t_wg])
            S.dma(wu, WB["f1"][:, DFF + blk * 256: DFF + (blk + 1) * 256].rearrange("(k p) n -> p k n", p=128),
                  wr=[t_wu])
            for mt in range(2):
                ft = blk * 2 + mt
                pg, tg = banks[cnt % 2], t_pg[cnt % 2]
                pu, tu = banks[2 + cnt % 2], t_pu[cnt % 2]
                for k in range(16):
                    mm(pg[:, :], wg[:, k, mt * 128:(mt + 1) * 128], hT[:, k, :], k == 0, k == 15, [t_wg, t_h], [tg])
                for k in range(16):
                    mm(pu[:, :], wu[:, k, mt * 128:(mt + 1) * 128], hT[:, k, :], k == 0, k == 15, [t_wu, t_h], [tu])
                sg, t_sg = sgr.next()
                act(sg, pg[:, :], AF.Silu, [tg], [t_sg])
                tt("dve", actT[:, ft, :], sg, pu[:, :], ALU.mult, [t_sg, tu], [t_act])
                cnt += 1

    def phase_F2(l, t10, xdst):
        T = 512
        tok0 = t10 * T
        gi = 0 if t10 < 2 else 1
        actT = ffn_keep["actT"]
        ar.p = ffn_keep["mark"]
        t_act = Trk()
        wrot = Rot([ar.bf(44 * 256).rearrange("p (k n) -> p k n", k=44) for _ in range(2)])
        stash = ar.f32(4 * D).rearrange("p (s d) -> p s d", s=4)
        t_st = [Trk() for _ in range(4)]
        t_pk = [Trk(), Trk()]
        cnt = 0
        for nb in range(8):
            wt, t_w = wrot.next()
            S.dma(wt, WB["f2"][:, nb * 256:(nb + 1) * 256].rearrange("(k p) n -> p k n", p=128), wr=[t_w])
            for sub in range(4):
                pk, tk = banks[cnt % 2], t_pk[cnt % 2]
                for k in range(44):
                    mm(pk[:, 0:256], actT[:, k, sub * 128:(sub + 1) * 128], wt[:, k, :], k == 0, k == 43,
                       [t_act, t_w], [tk])
                cp("act" if cnt % 2 == 0 else "dve", stash[:, sub, nb * 256:(nb + 1) * 256], pk[:, 0:256],
                   [tk], [t_st[sub]])
                cnt += 1
        xrot = Rot([ar.f32(D) for _ in range(1)])
        tmp = ar.f32(D)
        t_tmp = Trk()
        junk = ar.bf(512)
        t_junk = Trk()
        ssr = Rot([ar.f32(8) for _ in range(2)])
        for sub in range(4):
            xt, t_x = xrot.next()
            g0 = tok0 + sub * 128
            S.dma(xt, x1[g0:g0 + 128, :], wr=[t_x])
            ss, t_ss = ssr.next()
            postnorm([stash[:, sub, nb * 512:(nb + 1) * 512] for nb in range(4)], [t_st[sub]],
                     xt, t_x, 1, gi, tmp, t_tmp, ss, t_ss, junk, t_junk)
            S.dma(xdst[g0:g0 + 128, :], xt, rd=[t_x])

    for l in range(NLAYERS):
        layer(l)
    S.finish()
    return nc


_NC = None


def kernel(**inp):
    global _NC
    f = lambda a: np.ascontiguousarray(np.asarray(a, dtype=np.float32))
    xp, xs = f(inp["x_prompt"]), f(inp["x_sample"])
    shared = {}
    for k in ("w_ada", "b_ada", "norm_mix_pre", "norm_mix_post", "norm_ffn_pre", "norm_ffn_post", "w_in", "w_out",
              "ssd_conv_w", "ssd_conv_b", "ssd_D", "ssd_norm", "attn_sink", "s5_log_dt", "s5_C_re", "s5_C_im",
              "s5_D", "s5_w_glu", "s5_b_glu", "w_ffn_in", "w_ffn_out"):
        shared[k] = f(inp[k])
    shared["ssd_dt_bias"] = f(inp["ssd_dt_bias"]).reshape(2, 24)
    shared["ssd_A_log"] = f(inp["ssd_A_log"]).reshape(2, 24)
    shared["s5_A_re"] = f(inp["s5_A_re"]).reshape(2, 2, 2048)
    shared["s5_A_im"] = f(inp["s5_A_im"]).reshape(2, 2, 2048)
    shared["s5_B_re"] = f(inp["s5_B_re"]).reshape(2, 2048, 16)
    shared["s5_B_im"] = f(inp["s5_B_im"]).reshape(2, 2048, 16)
    in_maps = []
    for c in range(8):
        b = c // 2
        m = dict(shared)
        hf = c % 2
        m["xin"] = np.ascontiguousarray(np.concatenate([xp[4 * c:4 * c + 4].reshape(NPR * LP, D),
                                                        xs[b, hf * LH:(hf + 1) * LH]], axis=0))
        hm = np.zeros((128, 2), np.float32)
        hm[:, hf] = 1.0
        m["hmask"] = hm
        m["cvec"] = np.ascontiguousarray(np.stack([f(inp["c_ctx"]), f(inp["c"])[b]], axis=0))
        m["ckin"] = f(inp["cache_k"])[b].reshape(2, 512, 256)
        m["cvin"] = f(inp["cache_v"])[b].reshape(2, 512, 256)
        m["sssd"] = f(inp["state_ssd"])[b]
        m["ss5re"] = f(inp["state_s5_re"])[b].reshape(2, 2, 2048)
        m["ss5im"] = f(inp["state_s5_im"])[b].reshape(2, 2, 2048)
        in_maps.append(m)
    if _NC is None:
        _NC = build()
    res = run_bass_kernel_spmd(_NC, in_maps, core_ids=list(range(8)))
    R = res.results
    y_prompt = np.concatenate([R[c]["yout"][:NPR * LP].reshape(NPR, LP, D) for c in range(8)], axis=0)
    y_sample = np.stack([np.concatenate([R[2 * b]["yout"][NPR * LP:], R[2 * b + 1]["yout"][NPR * LP:]], axis=0)
                         for b in range(4)], axis=0)
    ck = np.concatenate([R[c]["ock"].reshape(NPR, 2, LP, 4, 64) for c in range(8)], axis=0)
    cv = np.concatenate([R[c]["ocv"].reshape(NPR, 2, LP, 4, 64) for c in range(8)], axis=0)
    sd = np.concatenate([R[c]["ossd"] for c in range(8)], axis=0)
    s5r = np.concatenate([R[c]["os5re"].reshape(NPR, 2, 2, 32, 64) for c in range(8)], axis=0)
    s5i = np.concatenate([R[c]["os5im"].reshape(NPR, 2, 2, 32, 64) for c in range(8)], axis=0)
    return (y_prompt.astype(np.float32), y_sample.astype(np.float32), ck.astype(np.float32), cv.astype(np.float32),
            sd.astype(np.float32), s5r.astype(np.float32), s5i.astype(np.float32))
```

```python
import contextlib
import math
import numpy as np
import concourse.bass as bass
import concourse.mybir as mybir
from concourse.bass_utils import run_bass_kernel_spmd

F32 = mybir.dt.float32
BF16 = mybir.dt.bfloat16
I32 = mybir.dt.int32
ALU = mybir.AluOpType
AF = mybir.ActivationFunctionType
AX = mybir.AxisListType

NPR, LP, LS = 4, 256, 4096
NT = NPR * LP + LS
LH = LS // 2
NTD = NPR * LP + LH
D = 2048
DIN = 3608
DFF = 5632
NEG = -30000.0
TWO_PI = 2.0 * math.pi


class Tok:
    __slots__ = ("sem", "val", "eng")

    def __init__(self, sem, val=None, eng=None):
        self.sem = sem
        self.val = val
        self.eng = eng


class Trk:
    __slots__ = ("w", "r")

    def __init__(self):
        self.w = None
        self.r = []


class Op:
    __slots__ = ("fn", "waits", "signal", "sem", "inc")

    def __init__(self, fn, waits):
        self.fn = fn
        self.waits = waits
        self.signal = False
        self.sem = None
        self.inc = 1


ENGS = ("pe", "act", "dve", "pool", "sp")


class Sched:
    def __init__(self, nc, n_dma_sems=32):
        self.nc = nc
        self.stack = contextlib.ExitStack()
        self.ops = {e: [] for e in ENGS}
        self.sem = {e: self.stack.enter_context(nc.semaphore("s_" + e)) for e in ENGS}
        self.cnt = {e: 0 for e in ENGS}
        self.pending = {e: [] for e in ENGS}
        self.lastc = {e: None for e in ENGS}
        self.known = {e: {} for e in ENGS}
        self.dsems = [self.stack.enter_context(nc.semaphore("s_d%d" % i)) for i in range(n_dma_sems)]
        self.dcnt = [0] * n_dma_sems
        self.dlast = [None] * n_dma_sems
        self.dnext = 0
        self.dnext_sw = 0
        self.nalloc = 0
        self.bgsems = [self.stack.enter_context(nc.semaphore("s_bg%d" % i)) for i in range(28)]
        self.bgnext = 0
        self.csem = self.stack.enter_context(nc.semaphore("s_cc"))
        self.ccnt = 0
        self.ctoks = []

    def sb(self, shape, dtype, name=None):
        self.nalloc += 1
        return self.stack.enter_context(self.nc.sbuf_tensor(name or ("sb%d" % self.nalloc), list(shape), dtype))

    def ps(self, shape, dtype, name=None):
        self.nalloc += 1
        return self.stack.enter_context(self.nc.psum_tensor(name or ("ps%d" % self.nalloc), list(shape), dtype))

    def _resolve_eng(self, e):
        if not self.pending[e]:
            return
        last = self.lastc[e]
        assert last is not None and not last.signal
        last.signal = True
        last.sem = self.sem[e]
        self.cnt[e] += 1
        v = self.cnt[e]
        for t in self.pending[e]:
            t.val = v
        self.pending[e] = []

    def _need(self, eng, toks):
        kn = self.known[eng]
        best = {}
        for t in toks:
            if t is None:
                continue
            if t.val is None:
                self._resolve_eng(t.eng)
            key = id(t.sem)
            if kn.get(key, 0) >= t.val:
                continue
            kn[key] = t.val
            best[key] = (t.sem, t.val)
        return list(best.values())

    def _deps(self, rd, wr, skip_pe=False):
        toks = []
        for t in rd:
            toks.append(t.w)
        for t in wr:
            toks.append(t.w)
            toks.extend(t.r)
        if skip_pe:
            toks = [t for t in toks if t is not None and t.eng != "pe"]
        return toks

    def _commit(self, tok, rd, wr):
        for t in rd:
            t.r = [x for x in t.r if not (x.eng is not None and x.eng == tok.eng)] + [tok]
        for t in wr:
            t.w = tok
            t.r = []

    def op(self, eng, fn, rd=(), wr=(), pe_acc=False):
        toks = self._deps(rd, wr, skip_pe=(eng == "pe" and pe_acc))
        waits = self._need(eng, toks)
        o = Op(fn, waits)
        self.ops[eng].append(o)
        self.lastc[eng] = o
        tok = Tok(self.sem[eng], None, eng)
        self.pending[eng].append(tok)
        self._commit(tok, rd, wr)
        return tok

    def dma(self, out, in_, rd=(), wr=(), eng="sp", **kw):
        self._resolve_eng(eng)
        nh = len(self.dsems) // 2
        if eng == "pool":
            i = nh + self.dnext_sw
            self.dnext_sw = (self.dnext_sw + 1) % (len(self.dsems) - nh)
        else:
            i = self.dnext
            self.dnext = (self.dnext + 1) % nh
        toks = self._deps(rd, wr)
        toks.append(self.dlast[i])
        waits = self._need(eng, toks)
        o = Op(lambda e: e.dma_start(out=out, in_=in_, **kw), waits)
        o.signal = True
        o.sem = self.dsems[i]
        o.inc = 16
        self.ops[eng].append(o)
        self.dcnt[i] += 16
        tok = Tok(self.dsems[i], self.dcnt[i], None)
        self.dlast[i] = tok
        self._commit(tok, rd, wr)
        return tok

    def dma_bg(self, out, in_, eng="pool", **kw):
        self._resolve_eng(eng)
        sem = self.bgsems[self.bgnext]
        self.bgnext += 1
        o = Op(lambda e: e.dma_start(out=out, in_=in_, **kw), [])
        o.signal = True
        o.sem = sem
        o.inc = 16
        self.ops[eng].append(o)
        return Tok(sem, 16, None)

    def coll(self, kind, op, groups, src, dst):
        eng = "pool"
        self._resolve_eng(eng)
        o = Op(lambda e: e.collective_compute(kind, op, replica_groups=groups, ins=[src], outs=[dst]), [])
        o.signal = True
        o.sem = self.csem
        o.inc = None
        self.ops[eng].append(o)
        self.ccnt += 1
        tok = Tok(self.csem, self.ccnt, None)
        self.ctoks.append(tok)
        return tok

    def wait_toks(self, toks):
        for e in ENGS:
            waits = self._need(e, toks)
            if waits:
                self.ops[e].append(Op(None, waits))

    def barrier(self):
        for e in ENGS:
            self._resolve_eng(e)
        for e in ENGS:
            toks = [Tok(self.sem[o], self.cnt[o], o) for o in ENGS if o != e and self.cnt[o] > 0]
            toks += [t for t in self.dlast if t is not None]
            toks += self.ctoks[-16:]
            waits = self._need(e, toks)
            if waits:
                self.ops[e].append(Op(None, waits))

    def finish(self):
        self.barrier()
        nc = self.nc

        def run(eng_obj, ops):
            for o in ops:
                for s, v in o.waits:
                    eng_obj.wait_ge(s, v)
                if o.fn is None:
                    continue
                ins = o.fn(eng_obj)
                if o.signal:
                    if o.inc is None:
                        ins.then_inc(o.sem)
                    else:
                        ins.then_inc(o.sem, o.inc)

        with nc.Block() as block:
            @block.tensor
            def _(e):
                run(e, self.ops["pe"])

            @block.scalar
            def _(e):
                run(e, self.ops["act"])

            @block.vector
            def _(e):
                run(e, self.ops["dve"])

            @block.gpsimd
            def _(e):
                run(e, self.ops["pool"])

            @block.sync
            def _(e):
                run(e, self.ops["sp"])
        self.stack.close()


class Arena:
    def __init__(self, t, nwords):
        self.t = t
        self.n = nwords
        self.p = 0

    def reset(self):
        self.p = 0

    def f32(self, n):
        a = self.t[:, self.p:self.p + n]
        self.p += n
        assert self.p <= self.n, ("arena overflow", self.p)
        return a

    def bf(self, n):
        w = (n + 1) // 2
        a = self.t[:, self.p:self.p + w].bitcast(BF16)
        self.p += w
        assert self.p <= self.n, ("arena overflow", self.p)
        return a[:, 0:n]

    def i32(self, n):
        return self.f32(n).bitcast(I32)


class Rot:
    def __init__(self, aps):
        self.items = [(a, Trk()) for a in aps]
        self.i = 0

    def next(self):
        it = self.items[self.i]
        self.i = (self.i + 1) % len(self.items)
        return it


MIX = ("conv", "ssd", "attn", "s5")
DEBUG = False
NLAYERS = 2


def build():
    nc = bass.Bass("TRN2", target_bir_lowering=False)

    def din(name, shape, dt=F32):
        return nc.dram_tensor(name, list(shape), dt, kind="ExternalInput").ap()

    def dout(name, shape, dt=F32):
        return nc.dram_tensor(name, list(shape), dt, kind="ExternalOutput").ap()

    def dscr(name, shape, dt=F32):
        kind = "ExternalOutput" if (DEBUG and name in ("mixT", "projT", "tokm")) else "Internal"
        return nc.dram_tensor(name, list(shape), dt, kind=kind).ap()

    xin = din("xin", [NTD, D])
    hmask = din("hmask", [128, 2])
    cvec = din("cvec", [2, D])
    ckin = din("ckin", [2, 512, 256])
    cvin = din("cvin", [2, 512, 256])
    sssd = din("sssd", [2, 2, 12, 64, 64])
    ss5re = din("ss5re", [2, 2, 2048])
    ss5im = din("ss5im", [2, 2, 2048])
    w_ada = din("w_ada", [2, D, 6 * D])
    b_ada = din("b_ada", [2, 6 * D])
    n_mix_pre = din("norm_mix_pre", [2, D])
    n_mix_post = din("norm_mix_post", [2, D])
    n_ffn_pre = din("norm_ffn_pre", [2, D])
    n_ffn_post = din("norm_ffn_post", [2, D])
    w_in = din("w_in", [2, D, DIN])
    w_out = din("w_out", [2, D, D])
    conv_w = din("ssd_conv_w", [2, 3, 1024])
    conv_b = din("ssd_conv_b", [2, 1024])
    dt_bias = din("ssd_dt_bias", [2, 24])
    a_log = din("ssd_A_log", [2, 24])
    ssd_D = din("ssd_D", [2, 12])
    ssd_norm = din("ssd_norm", [2, 768])
    attn_sink = din("attn_sink", [2, 12])
    s5_A_re = din("s5_A_re", [2, 2, 2048])
    s5_A_im = din("s5_A_im", [2, 2, 2048])
    s5_log_dt = din("s5_log_dt", [2, 2, 32])
    s5_B_re = din("s5_B_re", [2, 2048, 16])
    s5_B_im = din("s5_B_im", [2, 2048, 16])
    s5_C_re = din("s5_C_re", [2, 2, 32, 16, 64])
    s5_C_im = din("s5_C_im", [2, 2, 32, 16, 64])
    s5_D = din("s5_D", [2, 512])
    s5_w_glu = din("s5_w_glu", [2, 512, 512])
    s5_b_glu = din("s5_b_glu", [2, 512])
    w_ffn_in = din("w_ffn_in", [2, D, 2 * DFF])
    w_ffn_out = din("w_ffn_out", [2, DFF, D])
    yout = dout("yout", [NTD, D])
    ock = dout("ock", [NPR, 2, LP, 256])
    ocv = dout("ocv", [NPR, 2, LP, 256])
    ossd = dout("ossd", [NPR, 2, 2, 12, 64, 64])
    os5re = dout("os5re", [NPR, 2, 2, 2048])
    os5im = dout("os5im", [NPR, 2, 2, 2048])
    projT = dscr("projT", [3328, NT], BF16)
    tokm = dscr("tokm", [NT, 536])
    xcT = dscr("xcT", [1024, NT], BF16)
    xtok = dscr("xtok", [NT, 896], BF16)
    yfT = dscr("yfT", [768, NT])
    NJ = NT // 4
    u4 = dscr("u4", [4, 512, NJ], BF16)
    s5x4 = [dscr("s5f4", [4, 512, NJ]), dscr("s5b4", [4, 512, NJ])]
    mixT = dscr("mixT", [2048, NT], BF16)
    x1 = dscr("x1", [NTD, D])
    xmid = dscr("xmid", [NTD, D])
    projT_h = dscr("projT_h", [2816, LH], BF16)
    mixA_h = dscr("mixA_h", [768, LH], BF16)
    tokm_h = dscr("tokm_h", [LH, 536])
    u4_h = dscr("u4_h", [4 * 512, LH // 4], BF16)
    PROJ_RANGES = [(0, 512), (512, 1024), (1024, 1536), (1536, 1792), (2560, 2816)]
    gp = [dscr("g_proj%d" % i, [2 * (r1 - r0), LH], BF16) for i, (r0, r1) in enumerate(PROJ_RANGES)]
    gt = [dscr("g_tokm%d" % i, [2 * 512, 536]) for i in range(LH // 512)]
    gu = [dscr("g_u4%d" % i, [2 * 1024, LH // 4], BF16) for i in range(2)]
    wbs = [dict(ada=dscr("wb_ada%d" % i, [D, 6 * D], BF16), win=dscr("wb_in%d" % i, [D, DIN], BF16),
                wout=dscr("wb_out%d" % i, [D, D], BF16), f1=dscr("wb_f1%d" % i, [D, 2 * DFF], BF16),
                f2=dscr("wb_f2%d" % i, [DFF, D], BF16)) for i in range(2)]
    WB = {}

    S = Sched(nc)
    ARW = 36000
    arena_t = S.sb([128, ARW], F32, "arena")
    ar = Arena(arena_t, ARW)
    banks = [S.ps([128, 512], F32, "bank%d" % i) for i in range(8)]

    def E(eng, fn, rd=(), wr=(), pe_acc=False):
        return S.op(eng, fn, rd=rd, wr=wr, pe_acc=pe_acc)

    def mm(out, lhsT, rhs, start, stop, rd, wr, skip=False):
        return E("pe", lambda e: e.matmul(out, lhsT=lhsT, rhs=rhs, start=start, stop=stop, skip_group_check=skip),
                 rd=rd, wr=wr, pe_acc=(not start))

    def tr(out, in_, ident, rd, wr):
        return E("pe", lambda e: e.transpose(out, in_, ident), rd=rd, wr=wr)

    def act(out, in_, func, rd, wr, scale=1.0, bias=0.0, accum=None, eng="act"):
        if accum is None:
            return E(eng, lambda e: e.activation(out=out, in_=in_, func=func, scale=scale, bias=bias), rd=rd, wr=wr)
        return E(eng, lambda e: e.activation(out=out, in_=in_, func=func, scale=scale, bias=bias, accum_out=accum),
                 rd=rd, wr=wr)

    def tt(eng, out, a, b, op, rd, wr):
        return E(eng, lambda e: e.tensor_tensor(out, a, b, op), rd=rd, wr=wr)

    def ts(eng, out, a, s1, s2, op0, op1, rd, wr):
        return E(eng, lambda e: e.tensor_scalar(out, a, s1, s2, op0, op1), rd=rd, wr=wr)

    def stt(out, a, sc, b, op0, op1, rd, wr):
        return E("dve", lambda e: e.scalar_tensor_tensor(out, a, sc, b, op0, op1), rd=rd, wr=wr)

    def cp(eng, out, a, rd, wr):
        if eng == "act":
            return E("act", lambda e: e.copy(out, a), rd=rd, wr=wr)
        return E(eng, lambda e: e.tensor_copy(out, a), rd=rd, wr=wr)

    def memset(eng, out, v, wr):
        return E(eng, lambda e: e.memset(out, v), wr=wr)

    ident_f = S.sb([128, 128], F32, "ident_f")
    ident_b = S.sb([128, 128], BF16, "ident_b")
    ones_f = S.sb([128, 128], F32, "ones_f")
    ones_b = S.sb([128, 128], BF16, "ones_b")
    U_f = S.sb([128, 128], F32, "U_f")
    L_f = S.sb([128, 128], F32, "L_f")
    nm_fwd = S.sb([128, 128], F32, "nm_fwd")
    nm_bwd = S.sb([128, 128], F32, "nm_bwd")
    perm_b = S.sb([128, 128], BF16, "perm_b")
    negones_f = S.sb([128, 128], F32, "negones_f")
    nm3 = [S.sb([128, 384], F32, "nm3_%d" % i) for i in range(2)]
    nm3b = [S.sb([128, 384], BF16, "nm3b_%d" % i) for i in range(2)]
    nmb = [S.sb([128, 128], BF16, "nmb_%d" % i) for i in range(2)]
    modc = S.sb([128, 4, 16, 2], F32, "modc")
    gates = S.sb([128, 2, 2, D], F32, "gates")
    tC = Trk()
    t_modc = Trk()
    t_gates = Trk()
    iot = S.sb([128, 128], F32, "iot")
    E("pool", lambda e: e.iota(iot[:], pattern=[[1, 128]], base=0, channel_multiplier=-1,
                               allow_small_or_imprecise_dtypes=True), wr=[tC])
    ts("dve", ident_f[:], iot[:], 0.0, None, ALU.is_equal, ALU.bypass, [tC], [tC])
    cp("dve", ident_b[:], ident_f[:], [tC], [tC])
    memset("dve", ones_f[:], 1.0, [tC])
    memset("dve", ones_b[:], 1.0, [tC])
    memset("dve", negones_f[:], -1.0, [tC])
    ts("dve", U_f[:], iot[:], 0.0, None, ALU.is_ge, ALU.bypass, [tC], [tC])
    ts("dve", L_f[:], iot[:], 0.0, None, ALU.is_le, ALU.bypass, [tC], [tC])
    ts("dve", nm_fwd[:], iot[:], 0.0, NEG, ALU.is_lt, ALU.mult, [tC], [tC])
    ts("dve", nm_bwd[:], iot[:], 0.0, NEG, ALU.is_gt, ALU.mult, [tC], [tC])
    for i in range(3):
        cp("dve", nm3[0][:, i * 128:(i + 1) * 128], nm_fwd[:], [tC], [tC])
        cp("dve", nm3[1][:, i * 128:(i + 1) * 128], nm_bwd[:], [tC], [tC])
        cp("dve", nm3b[0][:, i * 128:(i + 1) * 128], nm_fwd[:], [tC], [tC])
        cp("dve", nm3b[1][:, i * 128:(i + 1) * 128], nm_bwd[:], [tC], [tC])
    cp("dve", nmb[0][:], nm_fwd[:], [tC], [tC])
    cp("dve", nmb[1][:], nm_bwd[:], [tC], [tC])
    pa = S.sb([128, 128], F32, "pa")
    pidx = S.sb([128, 1], I32, "pidx")
    pf = S.sb([128, 4], F32, "pf")
    E("pool", lambda e: e.iota(pidx[:], pattern=[[0, 1]], base=0, channel_multiplier=1), wr=[tC])
    pidx2 = S.sb([128, 1], I32, "pidx2")
    E("dve", lambda e: e.tensor_single_scalar(pidx2[:], pidx[:], 32, ALU.bitwise_and), rd=[tC], wr=[tC])
    cp("dve", pf[:, 0:1], pidx2[:], [tC], [tC])
    ts("dve", pf[:, 1:2], pf[:, 0:1], -2.0, 32.0, ALU.mult, ALU.add, [tC], [tC])
    ts("dve", pa[:], iot[:], pf[:, 1:2], None, ALU.is_equal, ALU.bypass, [tC], [tC])
    cp("dve", perm_b[:], pa[:], [tC], [tC])

    def convert(dst, src, rows, cols):
        toks = []
        step = 1024
        for r0 in range(0, rows, step):
            r1 = min(rows, r0 + step)
            toks.append(S.dma_bg(dst[r0:r1, :], src[r0:r1, :], eng="pool"))
        return toks

    conv_toks = []
    for l_ in range(NLAYERS):
        conv_toks.append(dict(
            ada=convert(wbs[l_]["ada"], w_ada[l_], D, 6 * D),
            win=convert(wbs[l_]["win"], w_in[l_], D, DIN),
            wout=convert(wbs[l_]["wout"], w_out[l_], D, D),
            f1=convert(wbs[l_]["f1"], w_ffn_in[l_], D, 2 * DFF),
            f2=convert(wbs[l_]["f2"], w_ffn_out[l_], DFF, D)))

    def need_w(l, name):
        S.wait_toks(conv_toks[l][name])
        WB[name] = wbs[l][name]

    PAIRS = [[0, 1], [2, 3], [4, 5], [6, 7]]

    def exchange_jobs():
        jobs = []
        for i, (r0, r1) in enumerate(PROJ_RANGES):
            jobs.append((projT_h[r0:r1, :], gp[i], ("p", r0, r1)))
        for i, t0 in enumerate(range(0, LH, 512)):
            jobs.append((tokm_h[t0:t0 + 512, :], gt[i], ("t", t0, t0 + 512)))
        for i in range(2):
            jobs.append((u4_h[i * 1024:(i + 1) * 1024, :], gu[i], ("u", i, None)))
        return jobs

    def exchange_issue():
        for (src, dst, _) in exchange_jobs():
            S.coll("AllGather", ALU.bypass, PAIRS, src, dst)

    def exchange_finish():
        for (src, dst, (kind, a0, a1)) in exchange_jobs():
            for h in range(2):
                c0 = NPR * LP + h * LH
                if kind == "p":
                    n = a1 - a0
                    S.dma(projT[a0:a1, c0:c0 + LH], dst[h * n:(h + 1) * n, :])
                elif kind == "t":
                    S.dma(tokm[c0 + a0:c0 + a1, :], dst[h * 512:(h + 1) * 512, :])
                else:
                    for ss in range(2):
                        s_ = a0 * 2 + ss
                        S.dma(u4[s_, :, c0 // 4: c0 // 4 + LH // 4],
                              dst[h * 1024 + ss * 512: h * 1024 + (ss + 1) * 512, :])
        S.barrier()

    def layer(l):
        need_w(l, "ada")
        x_src = xin if l == 0 else xmid
        x_dst = xmid if l == 0 else yout
        phase_mod(l)
        S.barrier()
        need_w(l, "win")
        for t3 in range(3):
            phase_A(l, t3, x_src)
            S.barrier()
        exchange_issue()
        S.barrier()
        exchange_finish()
        if "conv" in MIX:
            phase_conv(l)
            S.barrier()
        if "ssd" in MIX:
            phase_ssd(l)
            S.barrier()
        if "attn" in MIX:
            phase_attn(l)
            S.barrier()
        if "s5" in MIX:
            for d_ in range(2):
                phase_s5h(l, d_)
                S.barrier()
            phase_s5fin(l)
            S.barrier()
        need_w(l, "wout")
        for t3 in range(3):
            phase_C(l, t3, x_src)
            S.barrier()
        need_w(l, "f1")
        need_w(l, "f2")
        for t10 in range(6):
            phase_F1(l, t10)
            S.barrier()
            phase_F2(l, t10, x_dst)
            S.barrier()

    def phase_mod(l):
        ar.reset()
        cT = ar.f32(32).rearrange("p (k r) -> p k r", r=2)
        sT = ar.bf(32).rearrange("p (k r) -> p k r", r=2)
        t_c = Trk()
        for r in range(2):
            S.dma(cT[:, :, r], cvec[r].rearrange("(k p) -> p k", p=128), wr=[t_c], allow_slow_non_contiguous=True)
        act(sT, cT, AF.Silu, [t_c], [t_c])
        npre = ar.f32(32).rearrange("p (v k) -> p v k", v=2)
        t_np = Trk()
        S.dma(npre[:, 0, :], n_mix_pre[l].rearrange("(k p) -> p k", p=128), wr=[t_np], allow_slow_non_contiguous=True)
        S.dma(npre[:, 1, :], n_ffn_pre[l].rearrange("(k p) -> p k", p=128), wr=[t_np], allow_slow_non_contiguous=True)
        npost = ar.f32(2 * D).rearrange("p (v d) -> p v d", v=2)
        t_npo = Trk()
        S.dma(npost[:, 0, :], n_mix_post[l].partition_broadcast(128), wr=[t_npo])
        S.dma(npost[:, 1, :], n_ffn_post[l].partition_broadcast(128), wr=[t_npo])
        sel = ar.f32(256).rearrange("p (r m) -> p r m", r=2)
        esel = ar.f32(2)
        t_sel = Trk()
        for r in range(2):
            ts("dve", sel[:, r, :], ones_f[:], ident_f[:, r:r + 1], None, ALU.mult, ALU.bypass, [tC], [t_sel])
        cp("dve", esel, ident_f[:, 0:2], [tC], [t_sel])
        wrot = Rot([ar.bf(16 * 512).rearrange("p (k n) -> p k n", k=16) for _ in range(2)])
        brot = Rot([ar.f32(512) for _ in range(2)])
        mrot = Rot([ar.f32(512) for _ in range(2)])
        t_b0, t_b1, t_b2 = Trk(), Trk(), Trk()
        colsb = ar.f32(8)
        t_colsb = Trk()
        for blk in range(24):
            vec = blk // 4
            wt, t_w = wrot.next()
            S.dma(wt, WB["ada"][:, blk * 512:(blk + 1) * 512].rearrange("(k p) n -> p k n", p=128), wr=[t_w])
            bt, t_b = brot.next()
            S.dma(bt[0:2, :], b_ada[l, blk * 512:(blk + 1) * 512].partition_broadcast(2), wr=[t_b])
            for k in range(16):
                mm(banks[0][0:2, :], sT[:, k, :], wt[:, k, :], k == 0, k == 15, [t_c, t_w], [t_b0])
            mt, t_m = mrot.next()
            tt("dve", mt[0:2, :], banks[0][0:2, :], bt[0:2, :], ALU.add, [t_b0, t_b], [t_m])
            if vec in (2, 5):
                which = 0 if vec == 2 else 1
                for r in range(2):
                    mm(banks[1][:, :], sel[0:2, r, :], mt[0:2, :], True, True, [t_sel, t_m], [t_b1])
                    tt("dve", gates[:, which, r, (blk % 4) * 512:(blk % 4 + 1) * 512], banks[1][:, :],
                       npost[:, which, (blk % 4) * 512:(blk % 4 + 1) * 512], ALU.mult, [t_b1, t_npo], [t_gates])
            else:
                vi = {0: 1, 1: 0, 3: 3, 4: 2}[vec]
                for j in range(4):
                    for r in range(2):
                        mm(banks[2][:, j * 2 + r:j * 2 + r + 1], mt[0:2, j * 128:(j + 1) * 128], esel[0:2, r:r + 1],
                           True, True, [t_m, t_sel], [t_b2])
                kc0 = (blk % 4) * 4
                cp("dve", modc[:, vi, kc0:kc0 + 4, :], banks[2][:, 0:8].rearrange("p (j r) -> p j r", r=2),
                   [t_b2], [t_modc])
        for r in range(2):
            for (vi, v) in ((0, 0), (2, 1)):
                stt(modc[:, vi, :, r], modc[:, vi, :, r], 1.0, npre[:, v, :], ALU.add, ALU.mult,
                    [t_modc, t_np], [t_modc])

    def prenorm(xsrc, tok0, T, gi, vA, vB, hT, t_h):
        nsub = T // 128
        xrot = Rot([ar.f32(D) for _ in range(2)])
        nrot = Rot([ar.bf(D) for _ in range(2)])
        junk = ar.bf(D)
        t_junk = Trk()
        ssr = Rot([ar.f32(4) for _ in range(2)])
        t_tp = [Trk(), Trk()]
        for sub in range(nsub):
            xt, t_x = xrot.next()
            S.dma(xt, xsrc[tok0 + sub * 128: tok0 + (sub + 1) * 128, :], wr=[t_x])
            ss, t_ss = ssr.next()
            act(junk, xt, AF.Square, [t_x], [t_junk, t_ss], accum=ss[:, 0:1])
            ts("dve", ss[:, 1:2], ss[:, 0:1], 1.0 / D, 1e-6, ALU.mult, ALU.add, [t_ss], [t_ss])
            act(ss[:, 2:3], ss[:, 1:2], AF.Sqrt, [t_ss], [t_ss])
            E("dve", lambda e, o=ss[:, 3:4], i=ss[:, 2:3]: e.reciprocal(o, i), rd=[t_ss], wr=[t_ss])
            xn, t_n = nrot.next()
            ts("dve", xn, xt, ss[:, 3:4], None, ALU.mult, ALU.bypass, [t_x, t_ss], [t_n])
            for q4 in range(4):
                pb = banks[6 + (q4 % 2)][:, 0:256].bitcast(BF16).rearrange("p (a b) -> p a b", a=4)
                tp = t_tp[q4 % 2]
                for a in range(4):
                    kc = q4 * 4 + a
                    tr(pb[:, a, :], xn[:, kc * 128:(kc + 1) * 128], ident_b[:], [t_n, tC], [tp])
                cp("act" if q4 % 2 == 0 else "dve", hT[:, q4 * 4:(q4 + 1) * 4, sub * 128:(sub + 1) * 128], pb,
                   [tp], [t_h])
        for kc in range(16):
            act(hT[:, kc, :], hT[:, kc, :], AF.Identity, [t_h, t_modc], [t_h],
                scale=modc[:, vA, kc, gi:gi + 1], bias=modc[:, vB, kc, gi:gi + 1])

    FM_BLOCKS = [(0, 512, 0), (512, 256, 512), (768, 512, 768), (1280, 512, 1280),
                 (1816, 512, 1792), (2328, 256, 2304), (2584, 256, 2560), (3096, 512, 2816)]

    def phase_A(l, t5, xsrc):
        ar.reset()
        T = 1024
        tok0 = t5 * T
        gi = 0 if t5 == 0 else 1
        if t5 == 0:
            pj_dst, tk_dst, dcol = projT, tokm, 0
            u4v = u4
        else:
            pj_dst, tk_dst, dcol = projT_h, tokm_h, (t5 - 1) * T
            u4v = u4_h.rearrange("(s f) j -> s f j", s=4)
        hT = ar.bf(16 * T).rearrange("p (k t) -> p k t", k=16)
        t_h = Trk()
        prenorm(xsrc, tok0, T, gi, 0, 1, hT, t_h)
        wtm = ar.bf(16 * 536).rearrange("p (k n) -> p k n", k=16)
        t_wtm = Trk()
        for (c0, n, d0) in ((1792, 24, 0), (2584, 256, 24), (2840, 256, 280)):
            S.dma(wtm[:, :, d0:d0 + n], WB["win"][:, c0:c0 + n].rearrange("(k p) n -> p k n", p=128), wr=[t_wtm])
        trot = Rot([ar.f32(536) for _ in range(2)])
        t_pa, t_pb = [Trk(), Trk()], [Trk(), Trk()]
        for sub in range(T // 128):
            pa_, pb_ = banks[sub % 2], banks[2 + sub % 2]
            ta, tb = t_pa[sub % 2], t_pb[sub % 2]
            for k in range(16):
                mm(pa_[:, :], hT[:, k, sub * 128:(sub + 1) * 128], wtm[:, k, 24:536], k == 0, k == 15,
                   [t_h, t_wtm], [ta])
            for k in range(16):
                mm(pb_[:, 0:24], hT[:, k, sub * 128:(sub + 1) * 128], wtm[:, k, 0:24], k == 0, k == 15,
                   [t_h, t_wtm], [tb])
            st, t_st = trot.next()
            cp("act", st[:, 24:536], pa_[:, :], [ta], [t_st])
            cp("dve", st[:, 0:24], pb_[:, 0:24], [tb], [t_st])
            g0 = dcol + sub * 128
            S.dma(tk_dst[g0:g0 + 128, :], st, rd=[t_st])
            if t5 == 0:
                sq, so = sub // 2, (sub % 2) * 128
                S.dma(ock[sq, l, so:so + 128, :], st[:, 24:280], rd=[t_st])
                S.dma(ocv[sq, l, so:so + 128, :], st[:, 280:536], rd=[t_st])
        wrot = Rot([ar.bf(16 * 512).rearrange("p (k n) -> p k n", k=16) for _ in range(2)])
        srot = Rot([ar.bf(T) for _ in range(2)])
        t_pm = [Trk() for _ in range(2)]
        cnt = 0
        for (c0, n, d0) in FM_BLOCKS:
            wt, t_w = wrot.next()
            S.dma(wt[:, :, 0:n], WB["win"][:, c0:c0 + n].rearrange("(k p) n -> p k n", p=128), wr=[t_w])
            if d0 == 2816:
                hT4 = hT.rearrange("p k (j s) -> p k s j", s=4)
                for mt in range(4):
                    for s_ in range(4):
                        stg, t_sg = srot.next()
                        pm, tpm = banks[4 + cnt % 2], t_pm[cnt % 2]
                        for k in range(16):
                            mm(pm[:, 0:256], wt[:, k, mt * 128:(mt + 1) * 128], hT4[:, k, s_, :],
                               k == 0, k == 15, [t_w, t_h], [tpm])
                        cp("act" if cnt % 2 == 0 else "dve", stg[:, 0:256], pm[:, 0:256], [tpm], [t_sg])
                        cnt += 1
                        S.dma(u4v[s_, mt * 128:(mt + 1) * 128, dcol // 4: dcol // 4 + 256], stg[:, 0:256], rd=[t_sg])
                continue
            for mt in range(n // 128):
                stg, t_sg = srot.next()
                for nn in range(T // 512):
                    pm, tpm = banks[4 + cnt % 2], t_pm[cnt % 2]
                    for k in range(16):
                        mm(pm[:, :], wt[:, k, mt * 128:(mt + 1) * 128], hT[:, k, nn * 512:(nn + 1) * 512],
                           k == 0, k == 15, [t_w, t_h], [tpm])
                    cp("act" if cnt % 2 == 0 else "dve", stg[:, nn * 512:(nn + 1) * 512], pm[:, :], [tpm], [t_sg])
                    cnt += 1
                S.dma(pj_dst[d0 + mt * 128: d0 + (mt + 1) * 128, dcol:dcol + T], stg, rd=[t_sg])

    SEQS = [(i * LP, LP, False, i) for i in range(NPR)] + [(NPR * LP, LS, True, None)]

    def phase_conv(l):
        ar.reset()
        cw = ar.f32(24).rearrange("p (j k) -> p j k", k=3)
        cb = ar.f32(8)
        t_cw = Trk()
        for k in range(3):
            S.dma(cw[:, :, k], conv_w[l, k].rearrange("(j p) -> p j", p=128), wr=[t_cw], allow_slow_non_contiguous=True)
        S.dma(cb, conv_b[l].rearrange("(j p) -> p j", p=128), wr=[t_cw], allow_slow_non_contiguous=True)
        xrot = Rot([ar.bf(8 * 130).rearrange("p (j t) -> p j t", j=8) for _ in range(2)])
        acc = ar.f32(1024).rearrange("p (j t) -> p j t", j=8)
        acc2 = ar.f32(1024).rearrange("p (j t) -> p j t", j=8)
        t_acc, t_acc2 = Trk(), Trk()
        crot = Rot([ar.bf(1024).rearrange("p (j t) -> p j t", j=8) for _ in range(2)])
        srot = Rot([ar.bf(896) for _ in range(2)])
        t_ps = [Trk(), Trk()]
        ci = 0
        for (off, L, lat, si) in SEQS:
            for c in range(L // 128):
                t0 = c * 128
                g0 = off + t0
                xi, t_xi = xrot.next()
                lo, hi = 0, 130
                if t0 == 0:
                    memset("pool", xi[:, :, 0:1], 0.0, [t_xi])
                    lo = 1
                if t0 + 128 == L:
                    memset("pool", xi[:, :, 129:130], 0.0, [t_xi])
                    hi = 129
                S.dma(xi[:, :, lo:hi], projT[768:1792, g0 - 1 + lo: g0 - 1 + hi].rearrange("(j p) t -> p j t", p=128),
                      wr=[t_xi])
                wb = lambda k: cw[:, :, k:k + 1].to_broadcast([128, 8, 128])
                tt("dve", acc, xi[:, :, 1:129], wb(1), ALU.mult, [t_xi, t_cw], [t_acc])
                tt("pool", acc2, xi[:, :, 0:128], wb(0), ALU.mult, [t_xi, t_cw], [t_acc2])
                tt("dve", acc, acc, acc2, ALU.add, [t_acc2], [t_acc])
                tt("pool", acc2, xi[:, :, 2:130], wb(2), ALU.mult, [t_xi, t_cw], [t_acc2])
                tt("dve", acc, acc, acc2, ALU.add, [t_acc2], [t_acc])
                tt("dve", acc, acc, cb.unsqueeze(2).to_broadcast([128, 8, 128]), ALU.add, [t_cw], [t_acc])
                xc, t_xc = crot.next()
                act(xc, acc, AF.Silu, [t_acc], [t_xc])
                S.dma(xcT[:, g0:g0 + 128].rearrange("(j p) t -> p j t", p=128), xc, rd=[t_xc])
                pb = banks[ci % 2][:, 0:448].bitcast(BF16)
                tp = t_ps[ci % 2]
                for j in range(7):
                    tr(pb[:, j * 128:(j + 1) * 128], xc[:, j, :], ident_b[:], [t_xc, tC], [tp])
                st, t_st = srot.next()
                cp("act", st, pb, [tp], [t_st])
                S.dma(xtok[g0:g0 + 128, :], st, rd=[t_st])
                ci += 1

    def phase_ssd(l):
        ar.reset()
        dtb = ar.f32(24)
        aneg = ar.f32(24)
        Dc = ar.f32(6)
        nw = ar.f32(6)
        t_par = Trk()
        S.dma(dtb, dt_bias[l].partition_broadcast(128), wr=[t_par])
        S.dma(aneg, a_log[l].partition_broadcast(128), wr=[t_par])
        act(aneg, aneg, AF.Exp, [t_par], [t_par])
        ts("dve", aneg, aneg, -1.0, None, ALU.mult, ALU.bypass, [t_par], [t_par])
        dv = ssd_D[l].rearrange("(j t) -> t j", t=2)
        for hh in range(2):
            S.dma(Dc[hh * 64:(hh + 1) * 64, :], dv[hh].partition_broadcast(64), wr=[t_par], allow_slow_non_contiguous=True)
        S.dma(nw, ssd_norm[l].rearrange("(j p) -> p j", p=128), wr=[t_par], allow_slow_non_contiguous=True)
        ST = ar.f32(6 * 64).rearrange("p (h q) -> p h q", h=6)
        STb = ar.bf(6 * 64).rearrange("p (h q) -> p h q", h=6)
        t_ST = Trk()
        stio = ar.f32(12 * 64).rearrange("p (h n) -> p h n", h=12)
        t_stio = Trk()
        dtr = Rot([ar.f32(24) for _ in range(2)])
        xkr = Rot([ar.bf(896) for _ in range(2)])
        bcr = Rot([ar.bf(256).rearrange("p (w t) -> p w t", w=2) for _ in range(2)])
        sm = Rot([ar.f32(96) for _ in range(2)])
        xdtr = Rot([ar.bf(1536).rearrange("p (w h q) -> p w h q", w=2, h=12) for _ in range(2)])
        GTr = Rot([ar.f32(256).rearrange("p (g t) -> p g t", g=2) for _ in range(2)])
        rhr = Rot([ar.f32(384).rearrange("p (i l) -> p i l", i=3) for _ in range(3)])
        e2r = Rot([ar.f32(384) for _ in range(2)])
        dcr = Rot([ar.f32(384) for _ in range(2)])
        scr = Rot([ar.bf(384).rearrange("p (i l) -> p i l", i=3) for _ in range(3)])
        csr = Rot([ar.bf(384).rearrange("p (i l) -> p i l", i=3) for _ in range(2)])
        yst = Rot([ar.f32(768).rearrange("p (j t) -> p j t", j=6) for _ in range(2)])
        yfl = ar.f32(768).rearrange("p (j t) -> p j t", j=6)
        xsl = ar.bf(768).rearrange("p (j t) -> p j t", j=6)
        zl = ar.bf(768).rearrange("p (j t) -> p j t", j=6)
        zs = ar.f32(768).rearrange("p (j t) -> p j t", j=6)
        ysq = ar.bf(768).rearrange("p (j t) -> p j t", j=6)
        rsd = ar.f32(128)
        yo = ar.bf(768).rearrange("p (j t) -> p j t", j=6)
        t_fin = Trk()
        t_yo = Trk()
        pending_fin = []
        tb = {k: Trk() for k in ("c", "tot", "gt", "cr0", "cr1", "y0", "y1", "pst", "ss")}
        tb["tr"] = tb["ss"]
        for d in range(2):
            Tm = U_f if d == 0 else L_f
            nmk = nm_fwd if d == 0 else nm_bwd
            for (off, L, lat, si) in SEQS:
                if lat:
                    S.dma(stio[0:64, :, :], sssd[l, d].rearrange("h p n -> p h n"), wr=[t_stio])
                    for h in range(12):
                        g = h // 6
                        E("pe", lambda e, o=banks[7][g * 64:(g + 1) * 64, (h % 6) * 64:(h % 6 + 1) * 64],
                          i=stio[0:64, h, :]: e.matmul(o, lhsT=i, rhs=ident_f[0:64, 0:64], start=True, stop=True),
                          rd=[t_stio, tC], wr=[tb["tr"]])
                        if h % 6 == 5:
                            cp("dve", ST[g * 64:(g + 1) * 64, :, :],
                               banks[7][g * 64:(g + 1) * 64, 0:384].rearrange("p (h q) -> p h q", h=6), [tb["tr"]], [t_ST])
                else:
                    memset("dve", ST, 0.0, [t_ST])
                cp("act", STb, ST, [t_ST], [t_ST])
                nch = L // 128
                order = range(nch) if d == 0 else range(nch - 1, -1, -1)
                for c in order:
                    g0 = off + c * 128
                    dtt, t_dt = dtr.next()
                    S.dma(dtt, tokm[g0:g0 + 128, 0:24], wr=[t_dt])
                    xk, t_xk = xkr.next()
                    S.dma(xk, xtok[g0:g0 + 128, :], wr=[t_xk])
                    bc, t_bc = bcr.next()
                    S.dma(bc, xcT[768:1024, g0:g0 + 128].rearrange("(w p) t -> p w t", p=128), wr=[t_bc])
                    s_, t_s = sm.next()
                    dt_ = s_[:, 0:24]
                    da = s_[:, 24:48]
                    cneg = s_[:, 48:60]
                    w_ = s_[:, 60:72]
                    edec = s_[:, 72:84]
                    tt("dve", dt_, dtt, dtb, ALU.add, [t_dt, t_par], [t_s])
                    act(dt_, dt_, AF.Exp, [t_s], [t_s])
                    act(dt_, dt_, AF.Ln, [t_s], [t_s], bias=1.0)
                    tt("dve", da, dt_, aneg, ALU.mult, [t_s, t_par], [t_s])
                    dad = da[:, d * 12:(d + 1) * 12]
                    mm(banks[0][:, 0:12], Tm[:], dad, True, True, [tC, t_s], [tb["c"]])
                    mm(banks[0][:, 16:28], ones_f[:], dad, True, True, [tC, t_s], [tb["tot"]])
                    ts("dve", cneg, banks[0][:, 0:12], -1.0, None, ALU.mult, ALU.bypass, [tb["c"]], [t_s])
                    tt("dve", w_, banks[0][:, 16:28], cneg, ALU.add, [tb["tot"], t_s], [t_s])
                    act(w_, w_, AF.Exp, [t_s], [t_s])
                    act(edec, banks[0][:, 16:28], AF.Exp, [tb["tot"]], [t_s])
                    xd, t_xd = xdtr.next()
                    xs3 = xk[:, 0:768].rearrange("p (h q) -> p h q", h=12)
                    tt("dve", xd[:, 0, :, :], xs3, dt_[:, d * 12:(d + 1) * 12].unsqueeze(2).to_broadcast([128, 12, 64]),
                       ALU.mult, [t_xk, t_s], [t_xd])
                    GT, t_GT = GTr.next()
                    for g in range(2):
                        mm(banks[1][:, g * 128:(g + 1) * 128], bc[g * 64:(g + 1) * 64, 0, :], bc[g * 64:(g + 1) * 64, 1, :],
                           True, True, [t_bc], [tb["gt"]])
                    cp("act", GT, banks[1][:, 0:256].rearrange("p (g t) -> p g t", g=2), [tb["gt"]], [t_GT])
                    ys, t_ys = yst.next()
                    def stageA(hb):
                        g = hb // 2
                        gs = slice(g * 64, (g + 1) * 64)
                        hd0 = d * 12 + hb * 3
                        rh, t_rh = rhr.next()
                        tt("pool", rh, Tm[:].unsqueeze(1).to_broadcast([128, 3, 128]),
                           da[:, hd0:hd0 + 3].unsqueeze(2).to_broadcast([128, 3, 128]), ALU.mult, [tC, t_s], [t_rh])
                        crb, tcr = banks[2 + hb % 2], tb["cr%d" % (hb % 2)]
                        mm(crb[:, 0:384], ones_f[:], rh.rearrange("p i l -> p (i l)"), True, True, [tC, t_rh], [tcr])
                        e2, t_e2 = e2r.next()
                        act(e2[gs, :], crb[gs, 0:384], AF.Exp, [tcr], [t_e2])
                        for i in range(3):
                            mm(crb[:, i * 128:(i + 1) * 128], rh[:, i, :], negones_f[:], False, False, [t_rh, tC], [tcr], skip=True)
                        mm(crb[:, 0:384], ident_b[:], nm3b[d][:], False, True, [tC], [tcr], skip=True)
                        dc, t_dc = dcr.next()
                        act(dc, crb[:, 0:384], AF.Exp, [tcr], [t_dc])
                        sc, t_sc = scr.next()
                        tt("dve", sc, dc.rearrange("p (i l) -> p i l", i=3), GT[:, g:g + 1, :].to_broadcast([128, 3, 128]),
                           ALU.mult, [t_dc, t_GT], [t_sc])
                        cs, t_cs = csr.next()
                        tt("pool", cs[gs, :, :], e2[gs, :].rearrange("p (i l) -> p i l", i=3),
                           bc[gs, 1:2, :].to_broadcast([64, 3, 128]), ALU.mult, [t_bc, t_e2], [t_cs])
                        return (hb, sc, t_sc, cs, t_cs)

                    def stageB(cxb):
                        hb, sc, t_sc, cs, t_cs = cxb
                        g = hb // 2
                        gs = slice(g * 64, (g + 1) * 64)
                        for i in range(3):
                            h = hb * 3 + i
                            yb, tyb = banks[4 + (h // 2) % 2], tb["y%d" % ((h // 2) % 2)]
                            yo_ = yb[(h % 2) * 64:(h % 2 + 1) * 64, 0:128]
                            mm(yo_, xd[:, 0, h, :], sc[:, i, :], True, False, [t_xd, t_sc], [tyb])
                            mm(yo_, STb[gs, h % 6, :], cs[gs, i, :], False, True, [t_ST, t_cs], [tyb])
                            mm(banks[6][gs, (h % 6) * 64:(h % 6 + 1) * 64], xk[:, 768 + g * 64: 768 + (g + 1) * 64],
                               xd[:, 1, h, :], True, True, [t_xk, t_xd], [tb["pst"]])
                            stt(ST[gs, h % 6, :], ST[gs, h % 6, :], edec[gs, h:h + 1],
                                banks[6][gs, (h % 6) * 64:(h % 6 + 1) * 64], ALU.mult, ALU.add,
                                [t_s, tb["pst"]], [t_ST])
                            if h % 6 == 5:
                                cp("act", STb[gs, :, :], ST[gs, :, :], [t_ST], [t_ST])
                            if h % 2 == 1:
                                cp("act", ys[:, h // 2, :], yb[:, 0:128], [tyb], [t_ys])

                    prevb = stageA(0)
                    tt("pool", xd[:, 1, :, :], xd[:, 0, :, :], w_.unsqueeze(2).to_broadcast([128, 12, 64]), ALU.mult,
                       [t_s], [t_xd])
                    while pending_fin:
                        pending_fin.pop(0)()
                    for hb in range(1, 4):
                        nxt = stageA(hb)
                        stageB(prevb)
                        prevb = nxt
                    stageB(prevb)
                    if d == 0:
                        S.dma(yfT[:, g0:g0 + 128].rearrange("(j p) t -> p j t", p=128), ys, rd=[t_ys])
                    else:
                        def fin_chunk(g0=g0, ys=ys, t_ys=t_ys):
                            S.dma(yfl, yfT[:, g0:g0 + 128].rearrange("(j p) t -> p j t", p=128), wr=[t_fin])
                            S.dma(xsl, xcT[0:768, g0:g0 + 128].rearrange("(j p) t -> p j t", p=128), wr=[t_fin])
                            S.dma(zl, projT[0:768, g0:g0 + 128].rearrange("(j p) t -> p j t", p=128), wr=[t_fin])
                            tt("dve", ys, ys, yfl, ALU.add, [t_fin], [t_ys])
                            tt("pool", yfl, xsl, Dc.unsqueeze(2).to_broadcast([128, 6, 128]), ALU.mult, [t_par], [t_fin])
                            tt("dve", ys, ys, yfl, ALU.add, [t_fin], [t_ys])
                            act(zs, zl, AF.Silu, [t_fin], [t_fin])
                            tt("dve", ys, ys, zs, ALU.mult, [t_fin], [t_ys])
                            tt("pool", ysq, ys, ys, ALU.mult, [t_ys], [t_fin])
                            for j in range(6):
                                mm(banks[7][:, 0:128], ones_b[:], ysq[:, j, :], j == 0, j == 5, [tC, t_fin], [tb["ss"]])
                            ts("dve", rsd, banks[7][:, 0:128], 1.0 / 768, 1e-6, ALU.mult, ALU.add, [tb["ss"]], [t_fin])
                            act(rsd, rsd, AF.Sqrt, [t_fin], [t_fin])
                            E("dve", lambda e: e.reciprocal(rsd, rsd), rd=[t_fin], wr=[t_fin])
                            tt("dve", ys, ys, rsd.unsqueeze(1).to_broadcast([128, 6, 128]), ALU.mult, [t_fin], [t_ys])
                            tt("dve", yo, ys, nw.unsqueeze(2).to_broadcast([128, 6, 128]), ALU.mult, [t_ys, t_par], [t_yo])
                            S.dma(mixT[0:768, g0:g0 + 128].rearrange("(j p) t -> p j t", p=128), yo, rd=[t_yo])
                        pending_fin.append(fin_chunk)
                while pending_fin:
                    pending_fin.pop(0)()
                if not lat:
                    for h in range(12):
                        g = h // 6
                        E("pe", lambda e, o=banks[7][0:64, (h % 6) * 64:(h % 6 + 1) * 64],
                          i=ST[g * 64:(g + 1) * 64, h % 6, :], idn=ident_f[g * 64:(g + 1) * 64, g * 64:(g + 1) * 64]: e.transpose(o, i, idn),
                          rd=[t_ST, tC], wr=[tb["tr"]])
                        if h % 6 == 5:
                            cp("dve", stio[0:64, g * 6:(g + 1) * 6, :],
                               banks[7][0:64, 0:384].rearrange("p (h q) -> p h q", h=6), [tb["tr"]], [t_stio])
                    S.dma(ossd[si, l, d].rearrange("h p n -> p h n"), stio[0:64, :, :], rd=[t_stio])

    def phase_attn(l):
        ar.reset()
        esink = ar.f32(12)
        t_es = Trk()
        S.dma(esink[0:64, :], attn_sink[l].partition_broadcast(64), wr=[t_es])
        act(esink[0:64, :], esink[0:64, :], AF.Exp, [t_es], [t_es])
        t1 = ar.f32(4096)
        t2 = ar.f32(4096)
        t3 = ar.f32(4096)
        pc = ar.f32(8)
        pci = ar.i32(4)
        t_rt = Trk()
        E("pool", lambda e: e.iota(t1[0:64, :].rearrange("p (r c) -> p r c", r=64), pattern=[[1, 64], [0, 64]], base=0,
                                   channel_multiplier=0, allow_small_or_imprecise_dtypes=True), wr=[t_rt])
        E("pool", lambda e: e.iota(t2[0:64, :].rearrange("p (r c) -> p r c", r=64), pattern=[[0, 64], [1, 64]], base=0,
                                   channel_multiplier=0, allow_small_or_imprecise_dtypes=True), wr=[t_rt])
        E("dve", lambda e: e.tensor_single_scalar(pci[0:64, 0:1], pidx[0:64, :], 16, ALU.bitwise_and), rd=[tC], wr=[t_rt])
        E("dve", lambda e: e.tensor_single_scalar(pci[0:64, 1:2], pidx[0:64, :], 15, ALU.bitwise_and), rd=[tC], wr=[t_rt])
        cp("dve", pc[0:64, 0:2], pci[0:64, 0:2], [t_rt], [t_rt])
        ts("dve", pc[0:64, 2:3], pc[0:64, 0:1], 0.0, None, ALU.is_equal, ALU.bypass, [t_rt], [t_rt])
        act(pc[0:64, 3:4], pc[0:64, 1:2], AF.Exp, [t_rt], [t_rt], scale=-math.log(10000.0) / 16.0)
        ts("dve", pc[0:64, 4:5], pf[0:64, 0:1], 1.0 / 16.0, -1.0, ALU.mult, ALU.add, [tC], [t_rt])
        tt("dve", t1[0:64, :], t1[0:64, :], t2[0:64, :], ALU.subtract, [t_rt], [t_rt])
        stt(t1[0:64, :], t1[0:64, :], pc[0:64, 2:3], t2[0:64, :], ALU.mult, ALU.add, [t_rt], [t_rt])
        ts("dve", t1[0:64, :], t1[0:64, :], pc[0:64, 3:4], None, ALU.mult, ALU.bypass, [t_rt], [t_rt])
        t2i = t2.bitcast(I32)
        ts("dve", t2i[0:64, :], t1[0:64, :], 1.0 / TWO_PI, None, ALU.mult, ALU.bypass, [t_rt], [t_rt])
        cp("dve", t3[0:64, :], t2i[0:64, :], [t_rt], [t_rt])
        stt(t1[0:64, :], t3[0:64, :], -TWO_PI, t1[0:64, :], ALU.mult, ALU.add, [t_rt], [t_rt])
        ts("dve", t1[0:64, :], t1[0:64, :], math.pi, -math.pi, ALU.min, ALU.max, [t_rt], [t_rt])
        act(t2[0:64, :], t1[0:64, :], AF.Sin, [t_rt], [t_rt])
        act(t3[0:64, :], t1[0:64, :], AF.Abs, [t_rt], [t_rt])
        act(t3[0:64, :], t3[0:64, :], AF.Sin, [t_rt], [t_rt], scale=-1.0, bias=math.pi / 2)
        ts("dve", t2[0:64, :], t2[0:64, :], pc[0:64, 4:5], None, ALU.mult, ALU.bypass, [t_rt], [t_rt])
        cosT, sinT = t3, t2
        kT = ar.bf(4096)
        qT = ar.bf(4096)
        t_kT, t_qT = Trk(), Trk()
        vf = ar.f32(2048).rearrange("p (t q) -> p t q", q=64)
        vb = ar.bf(2048).rearrange("p (t q) -> p t q", q=64)
        t_v = Trk()
        ckf = ar.f32(256).rearrange("p (t q) -> p t q", q=64)
        ckT = ar.bf(512)
        cvf = ar.f32(256).rearrange("p (t q) -> p t q", q=64)
        cvb = ar.bf(256).rearrange("p (t q) -> p t q", q=64)
        t_ck = Trk()
        etr = Rot([ar.bf(512) for _ in range(4)])
        rtmp = Rot([ar.f32(512) for _ in range(2)])
        dnr = Rot([ar.f32(512) for _ in range(2)])
        otr = Rot([ar.bf(512) for _ in range(2)])
        tbk = [Trk() for _ in range(8)]

        def rope(xT, t_x, ncol, cT=None, sT=None):
            cT = cosT if cT is None else cT
            sT = sinT if sT is None else sT
            for b0 in range(0, ncol, 512):
                sl = slice(b0, b0 + 512)
                mm(banks[7][0:64, :], perm_b[0:64, 0:64], xT[0:64, sl], True, True, [tC, t_x], [tbk[7]])
                rt, t_r = rtmp.next()
                tt("dve", rt[0:64, :], banks[7][0:64, :], sT[0:64, sl], ALU.mult, [tbk[7], t_rt], [t_r])
                tt("pool", xT[0:64, sl], xT[0:64, sl], cT[0:64, sl], ALU.mult, [t_rt], [t_x])
                tt("dve", xT[0:64, sl], xT[0:64, sl], rt[0:64, :], ALU.add, [t_r], [t_x])

        hm = ar.f32(2)
        S.dma(hm, hmask, wr=[t_rt])
        cos_h = ar.f32(LH)
        sin_h = ar.f32(LH)
        for (dst_, src_) in ((cos_h, cosT), (sin_h, sinT)):
            ts("pool", dst_[0:64, :], src_[0:64, 0:LH], hm[0:64, 0:1], None, ALU.mult, ALU.bypass, [t_rt], [t_rt])
            stt(dst_[0:64, :], src_[0:64, LH:2 * LH], hm[0:64, 1:2], dst_[0:64, :], ALU.mult, ALU.add, [t_rt], [t_rt])
        kTw = ar.bf(18 * 128)
        vbw = ar.bf(18 * 64).rearrange("p (t q) -> p t q", q=64)
        t_kw, t_vw = Trk(), Trk()

        def finish_o(nps, dps, tn, td, h, n, col0, dst=None):
            dn, t_dn = dnr.next()
            ts("dve", dn[0:64, 0:n], dps[0:64, 0:n], esink[0:64, h:h + 1], None, ALU.add, ALU.bypass, [td, t_es], [t_dn])
            E("dve", lambda e: e.reciprocal(dn[0:64, 0:n], dn[0:64, 0:n]), rd=[t_dn], wr=[t_dn])
            ot, t_ot = otr.next()
            tt("dve", ot[0:64, 0:n], nps[0:64, 0:n], dn[0:64, 0:n], ALU.mult, [tn, t_dn], [t_ot])
            if dst is None:
                S.dma(mixT[768 + h * 64: 768 + (h + 1) * 64, col0:col0 + n], ot[0:64, 0:n], rd=[t_ot])
            else:
                S.dma(dst[h * 64:(h + 1) * 64, col0:col0 + n], ot[0:64, 0:n], rd=[t_ot])

        cnt = 0
        for (off, L, lat, si) in SEQS:
            nt = L // 128
            for j in range(4):
                S.dma(kT[0:64, 0:L], projT[2560 + j * 64: 2560 + (j + 1) * 64, off:off + L], wr=[t_kT])
                S.dma(vf[:, 0:nt, :], tokm[off:off + L, 280 + j * 64: 280 + (j + 1) * 64].rearrange("(t p) q -> p t q", p=128),
                      wr=[t_v])
                cp("act", vb[:, 0:nt, :], vf[:, 0:nt, :], [t_v], [t_v])
                if lat:
                    rope(kT, t_kT, L)
                    S.dma(ckf, ckin[l, :, j * 64:(j + 1) * 64].rearrange("(t p) q -> p t q", p=128), wr=[t_ck])
                    S.dma(cvf, cvin[l, :, j * 64:(j + 1) * 64].rearrange("(t p) q -> p t q", p=128), wr=[t_ck])
                    for t_ in range(4):
                        E("pe", lambda e, o=banks[7][0:64, t_ * 128:(t_ + 1) * 128], i=ckf[:, t_, :]: e.transpose(o, i, ident_f[:]),
                          rd=[t_ck, tC], wr=[tbk[7]])
                    cp("dve", ckT[0:64, :], banks[7][0:64, :], [tbk[7]], [t_ck])
                    cp("act", cvb, cvf, [t_ck], [t_ck])
                    h0c, h1c = hm[0:64, 0:1], hm[0:64, 1:2]
                    ts("pool", kTw[0:64, 0:128], kT[0:64, LH - 128:LH], h1c, None, ALU.mult, ALU.bypass, [t_kT, t_rt], [t_kw])
                    ts("pool", kTw[0:64, 128:128 + LH], kT[0:64, 0:LH], h0c, None, ALU.mult, ALU.bypass, [t_kT, t_rt], [t_kw])
                    stt(kTw[0:64, 128:128 + LH], kT[0:64, LH:2 * LH], h1c, kTw[0:64, 128:128 + LH], ALU.mult, ALU.add,
                        [t_kT, t_rt], [t_kw])
                    ts("pool", kTw[0:64, 128 + LH:256 + LH], kT[0:64, LH:LH + 128], h0c, None, ALU.mult, ALU.bypass, [t_kT, t_rt], [t_kw])
                    ts("pool", vbw[:, 0, :], vb[:, 15, :], hm[:, 1:2], None, ALU.mult, ALU.bypass, [t_v, t_rt], [t_vw])
                    ts("pool", vbw[:, 1:17, :], vb[:, 0:16, :], hm[:, 0:1], None, ALU.mult, ALU.bypass, [t_v, t_rt], [t_vw])
                    stt(vbw[:, 1:17, :], vb[:, 16:32, :], hm[:, 1:2], vbw[:, 1:17, :], ALU.mult, ALU.add, [t_v, t_rt], [t_vw])
                    ts("pool", vbw[:, 17, :], vb[:, 16, :], hm[:, 0:1], None, ALU.mult, ALU.bypass, [t_v, t_rt], [t_vw])
                for gq in range(3):
                    h = j * 3 + gq
                    if lat:
                        S.dma(qT[0:64, 0:LH], projT_h[1792 + h * 64: 1792 + (h + 1) * 64, 0:LH], wr=[t_qT])
                    else:
                        S.dma(qT[0:64, 0:L], projT[1792 + h * 64: 1792 + (h + 1) * 64, off:off + L], wr=[t_qT])
                    if not lat:
                        nps, dps, tn, td = banks[4], banks[5], tbk[4], tbk[5]
                        for kt in range(2):
                            sp_, tsp = banks[cnt % 2], tbk[cnt % 2]
                            cnt += 1
                            mm(sp_[:, 0:256], kT[0:64, kt * 128:(kt + 1) * 128], qT[0:64, 0:256], True, True, [t_kT, t_qT], [tsp])
                            et, t_et = etr.next()
                            act(et[:, 0:256], sp_[:, 0:256], AF.Exp, [tsp], [t_et], scale=0.125)
                            mm(nps[0:64, 0:256], vb[:, kt, :], et[:, 0:256], kt == 0, kt == 1, [t_v, t_et], [tn])
                            mm(dps[0:64, 0:256], ones_b[:, 0:64], et[:, 0:256], kt == 0, kt == 1, [tC, t_et], [td])
                        finish_o(nps, dps, tn, td, h, 256, off)
                    else:
                        rope(qT, t_qT, LH, cos_h, sin_h)
                        for qb in range(LH // 512):
                            pi_ = qb % 2
                            nps, dps, tn, td = banks[4 + pi_], banks[2 + pi_], tbk[4 + pi_], tbk[2 + pi_]
                            qs_ = slice(qb * 512, (qb + 1) * 512)
                            tasks = [("ctx", kt) for kt in range(4)] + [("win", qs) for qs in range(4)]

                            def score(task):
                                nonlocal cnt
                                kind, idx = task
                                if kind == "ctx":
                                    sp_, tsp = banks[cnt % 2], tbk[cnt % 2]
                                    cnt += 1
                                    mm(sp_[:, :], ckT[0:64, idx * 128:(idx + 1) * 128], qT[0:64, qs_], True, True, [t_ck, t_qT], [tsp])
                                    et, t_et = etr.next()
                                    act(et, sp_[:, :], AF.Exp, [tsp], [t_et], scale=0.125)
                                    return (kind, idx, et, t_et, None)
                                n_ = qb * 4 + idx
                                kts = [n_ + dk for dk in (-1, 0, 1)]
                                sp_, tsp = banks[6 + cnt % 2], tbk[6 + cnt % 2]
                                cnt += 1
                                for ii, kt in enumerate(kts):
                                    dk = kt - n_
                                    cols = slice(ii * 128, (ii + 1) * 128)
                                    mm(sp_[:, cols], kTw[0:64, (kt + 1) * 128:(kt + 2) * 128], qT[0:64, n_ * 128:(n_ + 1) * 128],
                                       True, dk == 0, [t_kw, t_qT], [tsp])
                                    if dk != 0:
                                        mm(sp_[:, cols], ident_b[:], (nmb[1] if dk == -1 else nmb[0])[:], False, True, [tC], [tsp])
                                et, t_et = etr.next()
                                nk = len(kts) * 128
                                act(et[:, 0:nk], sp_[:, 0:nk], AF.Exp, [tsp], [t_et], scale=0.125)
                                if n_ == 0:
                                    ts("dve", et[:, 0:128], et[:, 0:128], hm[:, 1:2], None, ALU.mult, ALU.bypass, [t_rt], [t_et])
                                if n_ == LH // 128 - 1:
                                    ts("dve", et[:, 256:384], et[:, 256:384], hm[:, 0:1], None, ALU.mult, ALU.bypass, [t_rt], [t_et])
                                return (kind, idx, et, t_et, kts)

                            def pv(res):
                                kind, idx, et, t_et, kts = res
                                if kind == "ctx":
                                    mm(nps[0:64, :], cvb[:, idx, :], et, idx == 0, False, [t_ck, t_et], [tn])
                                    mm(dps[0:64, :], ones_b[:, 0:64], et, idx == 0, False, [tC, t_et], [td])
                                    return
                                for ii, kt in enumerate(kts):
                                    last = (idx == 3 and ii == len(kts) - 1)
                                    mm(nps[0:64, idx * 128:(idx + 1) * 128], vbw[:, kt + 1, :], et[:, ii * 128:(ii + 1) * 128], False, last,
                                       [t_vw, t_et], [tn])
                                    mm(dps[0:64, idx * 128:(idx + 1) * 128], ones_b[:, 0:64], et[:, ii * 128:(ii + 1) * 128], False, last,
                                       [tC, t_et], [td])

                            prev_r = score(tasks[0])
                            for tk_ in tasks[1:]:
                                nxt_r = score(tk_)
                                pv(prev_r)
                                prev_r = nxt_r
                            pv(prev_r)
                            finish_o(nps, dps, tn, td, h, 512, qb * 512, dst=mixA_h)

    def phase_s5h(l, d):
        ar.reset()
        Q = 128
        t_su = Trk()
        W = [t_su]
        tbk = [Trk() for _ in range(8)]
        ta = ar.f32(16 * 64).rearrange("p (k t) -> p k t", k=16)
        tb_ = ar.f32(16 * 64).rearrange("p (k t) -> p k t", k=16)
        Bre = ar.f32(256).rearrange("p (k c) -> p k c", k=16)
        Bim = ar.f32(256).rearrange("p (k c) -> p k c", k=16)
        b1 = ar.f32(256).rearrange("p (k c) -> p k c", k=16)
        b2 = ar.f32(256).rearrange("p (k c) -> p k c", k=16)
        b3 = ar.f32(256).rearrange("p (k c) -> p k c", k=16)
        b4 = ar.f32(256).rearrange("p (k c) -> p k c", k=16)
        cn = ar.f32(4 * 128).rearrange("p (a q) -> p a q", a=4)
        pads = [ar.bf(16 * 128).rearrange("p (k c) -> p k c", k=16) for _ in range(2)]
        CTf = [ar.f32(16 * 32).rearrange("p (k c) -> p k c", k=16) for _ in range(2)]
        CTb = [ar.bf(16 * 32).rearrange("p (k c) -> p k c", k=16) for _ in range(2)]
        Kall = ar.bf(16 * 4 * 32).rearrange("p (k e c) -> p k e c", k=16, e=4)
        sc_ = ar.f32(16 * 24).rearrange("p (v k) -> p v k", v=24)
        PW = ar.f32(5 * 2 * 16).rearrange("p (e c k) -> p e c k", e=5, c=2)
        kqi = ar.i32(16)
        are, aim, ldt = sc_[:, 0, :], sc_[:, 1, :], sc_[:, 2, :]
        S.dma(are, s5_A_re[l, d].rearrange("(k q) -> q k", q=128), wr=W, allow_slow_non_contiguous=True)
        S.dma(aim, s5_A_im[l, d].rearrange("(k q) -> q k", q=128), wr=W, allow_slow_non_contiguous=True)
        lv = s5_log_dt[l, d].rearrange("(k t) -> t k", t=2)
        for g2 in range(2):
            S.dma(ldt[g2 * 64:(g2 + 1) * 64, :], lv[g2].partition_broadcast(64), wr=W, allow_slow_non_contiguous=True)
        step, a_, th, r_, kq_ = sc_[:, 3, :], sc_[:, 4, :], sc_[:, 5, :], sc_[:, 6, :], sc_[:, 7, :]
        c1, s1, lr, li = sc_[:, 8, :], sc_[:, 9, :], sc_[:, 10, :], sc_[:, 11, :]
        den, kr, ki, tmp1, tmp2 = sc_[:, 12, :], sc_[:, 13, :], sc_[:, 14, :], sc_[:, 15, :], sc_[:, 16, :]
        Cm, Sm, tmp3 = sc_[:, 17, :], sc_[:, 18, :], sc_[:, 19, :]
        r4, c4, s4, rr = sc_[:, 20, :], sc_[:, 21, :], sc_[:, 22, :], sc_[:, 23, :]
        act(step, ldt, AF.Exp, W, W)
        tt("dve", a_, are, step, ALU.mult, W, W)
        tt("dve", th, aim, step, ALU.mult, W, W)
        act(r_, a_, AF.Exp, W, W)
        ts("dve", kqi, th, 1.0 / TWO_PI, None, ALU.mult, ALU.bypass, W, W)
        cp("dve", kq_, kqi, W, W)
        stt(tmp1, kq_, -TWO_PI, th, ALU.mult, ALU.add, W, W)
        ts("dve", tmp1, tmp1, math.pi, -math.pi, ALU.min, ALU.max, W, W)
        act(s1, tmp1, AF.Sin, W, W)
        act(tmp2, tmp1, AF.Abs, W, W)
        act(c1, tmp2, AF.Sin, W, W, scale=-1.0, bias=math.pi / 2)
        tt("dve", lr, r_, c1, ALU.mult, W, W)
        tt("dve", li, r_, s1, ALU.mult, W, W)
        tt("dve", den, are, are, ALU.mult, W, W)
        tt("dve", tmp1, aim, aim, ALU.mult, W, W)
        tt("dve", den, den, tmp1, ALU.add, W, W)
        E("dve", lambda e, o=den: e.reciprocal(o, o), rd=W, wr=W)
        ts("dve", tmp1, lr, -1.0, None, ALU.add, ALU.bypass, W, W)
        tt("dve", kr, tmp1, are, ALU.mult, W, W)
        tt("dve", tmp2, li, aim, ALU.mult, W, W)
        tt("dve", kr, kr, tmp2, ALU.add, W, W)
        tt("dve", kr, kr, den, ALU.mult, W, W)
        tt("dve", ki, li, are, ALU.mult, W, W)
        tt("dve", tmp2, tmp1, aim, ALU.mult, W, W)
        tt("dve", ki, ki, tmp2, ALU.subtract, W, W)
        tt("dve", ki, ki, den, ALU.mult, W, W)

        def cmul16(o_re, o_im, a_re, a_im, b_re, b_im):
            tt("dve", tmp1, a_re, b_re, ALU.mult, W, W)
            tt("dve", tmp2, a_im, b_im, ALU.mult, W, W)
            tt("dve", tmp3, a_re, b_im, ALU.mult, W, W)
            tt("dve", rr, a_im, b_re, ALU.mult, W, W)
            tt("dve", o_re, tmp1, tmp2, ALU.subtract, W, W)
            tt("dve", o_im, tmp3, rr, ALU.add, W, W)

        memset("dve", PW[:, 0, 0, :], 1.0, W)
        memset("dve", PW[:, 0, 1, :], 0.0, W)
        cp("dve", PW[:, 1, 0, :], lr, W, W)
        cp("dve", PW[:, 1, 1, :], li, W, W)
        cmul16(PW[:, 2, 0, :], PW[:, 2, 1, :], lr, li, lr, li)
        cmul16(PW[:, 3, 0, :], PW[:, 3, 1, :], PW[:, 2, 0, :], PW[:, 2, 1, :], lr, li)
        cmul16(PW[:, 4, 0, :], PW[:, 4, 1, :], PW[:, 2, 0, :], PW[:, 2, 1, :], PW[:, 2, 0, :], PW[:, 2, 1, :])
        tt("dve", r4, r_, r_, ALU.mult, W, W)
        tt("dve", r4, r4, r4, ALU.mult, W, W)
        E("dve", lambda e: e.reciprocal(rr, r4), rd=W, wr=W)
        tt("dve", c4, PW[:, 4, 0, :], rr, ALU.mult, W, W)
        tt("dve", s4, PW[:, 4, 1, :], rr, ALU.mult, W, W)
        cosT = ar.f32(16 * Q).rearrange("p (k t) -> p k t", k=16)
        sinT = ar.f32(16 * Q).rearrange("p (k t) -> p k t", k=16)
        dec = ar.f32(16 * Q).rearrange("p (k t) -> p k t", k=16)
        dec64 = ar.f32(16 * 64).rearrange("p (k t) -> p k t", k=16)
        memset("dve", cosT[:, :, 0:1], 1.0, W)
        memset("dve", sinT[:, :, 0:1], 0.0, W)
        cp("dve", Cm, c4, W, W)
        cp("dve", Sm, s4, W, W)
        m = 1
        while m < Q:
            Cb = Cm.unsqueeze(2).to_broadcast([128, 16, m])
            Sb = Sm.unsqueeze(2).to_broadcast([128, 16, m])
            tt("dve", ta[:, :, 0:m], cosT[:, :, 0:m], Cb, ALU.mult, W, W)
            tt("dve", tb_[:, :, 0:m], sinT[:, :, 0:m], Sb, ALU.mult, W, W)
            tt("dve", cosT[:, :, m:2 * m], ta[:, :, 0:m], tb_[:, :, 0:m], ALU.subtract, W, W)
            tt("dve", ta[:, :, 0:m], sinT[:, :, 0:m], Cb, ALU.mult, W, W)
            tt("dve", tb_[:, :, 0:m], cosT[:, :, 0:m], Sb, ALU.mult, W, W)
            tt("dve", sinT[:, :, m:2 * m], ta[:, :, 0:m], tb_[:, :, 0:m], ALU.add, W, W)
            tt("dve", tmp1, Cm, Cm, ALU.mult, W, W)
            tt("dve", tmp2, Sm, Sm, ALU.mult, W, W)
            tt("dve", tmp3, Cm, Sm, ALU.mult, W, W)
            tt("dve", Cm, tmp1, tmp2, ALU.subtract, W, W)
            ts("dve", Sm, tmp3, 2.0, None, ALU.mult, ALU.bypass, W, W)
            m *= 2
        cp("dve", dec, r4.unsqueeze(2).to_broadcast([128, 16, Q]), W, W)
        cp("dve", dec64, r4.unsqueeze(2).to_broadcast([128, 16, 64]), W, W)
        if d == 0:
            memset("dve", dec[:, :, 0:1], 0.0, W)
            memset("dve", dec64[:, :, 0:1], 0.0, W)
        else:
            taf = ta.rearrange("p k t -> p (k t)")
            for T_ in (cosT, sinT):
                for k in range(16):
                    rev = bass.AP(T_.tensor, T_[:, k, Q - 1:Q].offset, [[T_.ap[0][0], 128], [-1, Q]])
                    cp("dve", taf[:, 0:Q], rev, W, W)
                    cp("dve", T_[:, k, :], taf[:, 0:Q], W, W)
            memset("dve", dec[:, :, Q - 1:Q], 0.0, W)
            memset("dve", dec64[:, :, 63:64], 0.0, W)
        S.dma(Bre, s5_B_re[l].rearrange("(k q) c -> q k c", q=128), wr=W)
        S.dma(Bim, s5_B_im[l].rearrange("(k q) c -> q k c", q=128), wr=W)
        krb = kr.unsqueeze(2).to_broadcast([128, 16, 16])
        kib = ki.unsqueeze(2).to_broadcast([128, 16, 16])
        tt("dve", b1, Bre, krb, ALU.mult, W, W)
        tt("dve", b3, Bim, kib, ALU.mult, W, W)
        tt("dve", b1, b1, b3, ALU.subtract, W, W)
        tt("dve", b2, Bim, krb, ALU.mult, W, W)
        tt("dve", b3, Bre, kib, ALU.mult, W, W)
        tt("dve", b2, b2, b3, ALU.add, W, W)
        for pz in pads:
            memset("dve", pz, 0.0, W)
        for s_ in range(4):
            e_ = (3 - s_) if d == 0 else s_
            pr_ = PW[:, e_, 0, :].unsqueeze(2).to_broadcast([128, 16, 16])
            pi_ = PW[:, e_, 1, :].unsqueeze(2).to_broadcast([128, 16, 16])
            tt("dve", b3, b1, pr_, ALU.mult, W, W)
            tt("dve", b4, b2, pi_, ALU.mult, W, W)
            for g2 in range(2):
                hs = slice(g2 * 64, (g2 + 1) * 64)
                tt("dve", pads[0][hs, :, s_ * 32 + g2 * 16: s_ * 32 + (g2 + 1) * 16], b3[hs], b4[hs], ALU.subtract, W, W)
            tt("dve", b3, b2, pr_, ALU.mult, W, W)
            tt("dve", b4, b1, pi_, ALU.mult, W, W)
            for g2 in range(2):
                hs = slice(g2 * 64, (g2 + 1) * 64)
                tt("dve", pads[1][hs, :, s_ * 32 + g2 * 16: s_ * 32 + (g2 + 1) * 16], b3[hs], b4[hs], ALU.add, W, W)
        WT = []
        for pz in pads:
            wt_ = ar.bf(16 * 128).rearrange("p (k q) -> p k q", k=16)
            for k4 in range(0, 16, 4):
                pb = banks[7][:, 0:256].bitcast(BF16)
                for kk in range(4):
                    tr(pb[:, kk * 128:(kk + 1) * 128], pz[:, k4 + kk, :], ident_b[:], W + [tC], [tbk[7]])
                cp("dve", wt_[:, k4:k4 + 4, :], pb.rearrange("p (k q) -> p k q", k=4), [tbk[7]], W)
            WT.append(wt_)
        for ci_, csrc in enumerate((s5_C_re, s5_C_im)):
            memset("dve", CTf[ci_], 0.0, W)
            cv_ = csrc[l, d].rearrange("g c p -> (g c) p").rearrange("(a q) p -> q a p", q=128)
            S.dma(cn[:, :, 0:64], cv_, wr=W)
            S.dma(cn[:, :, 64:128], cv_, wr=W)
            for a in range(4):
                E("pe", lambda e, o=banks[6][:, 0:128], i=cn[:, a, :]: e.transpose(o, i, ident_f[:]), rd=W + [tC], wr=[tbk[6]])
                for g2 in range(2):
                    src = banks[6][g2 * 64:(g2 + 1) * 64, 0:128].rearrange("p (a2 gg c) -> p a2 gg c", a2=4, gg=2)[:, :, g2, :]
                    dst = CTf[ci_][g2 * 64:(g2 + 1) * 64, a * 4:(a + 1) * 4, g2 * 16:(g2 + 1) * 16]
                    cp("dve", dst, src, [tbk[6]], W)
        cp("dve", CTb[0], CTf[0], W, W)
        ts("dve", CTb[1], CTf[1], -1.0, None, ALU.mult, ALU.bypass, W, W)
        CH = [ar.bf(16 * 128).rearrange("p (k q) -> p k q", k=16) for _ in range(2)]
        cta = ta.rearrange("p k t -> p (k t)")[:, 0:512].rearrange("p (k c) -> p k c", k=16)
        ctb = tb_.rearrange("p k t -> p (k t)")[:, 0:512].rearrange("p (k c) -> p k c", k=16)
        for tau in range(4):
            f_ = (tau + 1) if d == 0 else (4 - tau)
            pr_ = PW[:, f_, 0, :].unsqueeze(2).to_broadcast([128, 16, 32])
            pi_ = PW[:, f_, 1, :].unsqueeze(2).to_broadcast([128, 16, 32])
            tt("dve", cta, CTf[0], pr_, ALU.mult, W, W)
            tt("dve", ctb, CTf[1], pi_, ALU.mult, W, W)
            tt("dve", CH[0][:, :, tau * 32:(tau + 1) * 32], cta, ctb, ALU.subtract, W, W)
            tt("dve", cta, CTf[0], pi_, ALU.mult, W, W)
            tt("dve", ctb, CTf[1], pr_, ALU.mult, W, W)
            tt("dve", cta, cta, ctb, ALU.add, W, W)
            ts("dve", CH[1][:, :, tau * 32:(tau + 1) * 32], cta, -1.0, None, ALU.mult, ALU.bypass, W, W)
        for k4 in range(0, 16, 4):
            for kk in range(4):
                k = k4 + kk
                for e_ in range(4):
                    s_blk = (3 - e_) if d == 0 else e_
                    o_ = banks[5][0:32, kk * 128 + e_ * 32: kk * 128 + (e_ + 1) * 32]
                    mm(o_, pads[0][:, k, s_blk * 32:(s_blk + 1) * 32], CTb[0][:, k, :], True, False, W, [tbk[5]])
                    mm(o_, pads[1][:, k, s_blk * 32:(s_blk + 1) * 32], CTb[1][:, k, :], False, True, W, [tbk[5]])
            cp("dve", Kall[0:32, k4:k4 + 4, :, :], banks[5][0:32, :].rearrange("p (k e c) -> p k e c", k=4, e=4), [tbk[5]], W)
        TOE = ar.bf(16 * 128).rearrange("p (k q) -> p k q", k=16)
        t_toe = Trk()
        memset("dve", TOE, 0.0, [t_toe])
        for s_ in range(4):
            for tau in range(4):
                dl = (tau - s_) if d == 0 else (s_ - tau)
                if dl < 0:
                    continue
                S.dma(TOE[s_ * 32:(s_ + 1) * 32, :, tau * 32:(tau + 1) * 32], Kall[0:32, :, dl, :], rd=W, wr=[t_toe])
        urot = Rot([ar.bf(16 * Q).rearrange("p (k t) -> p k t", k=16) for _ in range(2)])
        ginb = Rot([ar.f32(2 * 4 * Q) for _ in range(2)])
        goutb = Rot([ar.f32(2 * 4 * Q) for _ in range(2)])
        tmpA = Rot([ar.f32(4 * Q) for _ in range(2)])
        Hfb = Rot([ar.f32(2 * 4 * Q) for _ in range(2)])
        Hbb = Rot([ar.bf(2 * 4 * (Q + 2)) for _ in range(2)])
        ysr = Rot([ar.f32(4 * Q) for _ in range(2)])
        Hin = ar.f32(32).rearrange("p (c k) -> p c k", c=2)
        carry = ar.f32(64).rearrange("p (c k) -> p c k", c=4)
        ct = ar.f32(64).rearrange("p (c k) -> p c k", c=4)
        t_car = Trk()
        dst4 = s5x4[d]
        bodyno = [0]

        def stage1(j0, n, a, u_, t_u):
            bn = bodyno[0]
            bodyno[0] += 1
            ks = slice(a * 4, (a + 1) * 4)
            bre, bim = banks[(bn % 2) * 2], banks[(bn % 2) * 2 + 1]
            tre, tim = tbk[(bn % 2) * 2], tbk[(bn % 2) * 2 + 1]
            for kq in range(4):
                k = a * 4 + kq
                mm(bre[:, kq * Q:kq * Q + n], WT[0][:, k, :], u_[:, k, 0:n], True, True, W + [t_u], [tre])
                mm(bim[:, kq * Q:kq * Q + n], WT[1][:, k, :], u_[:, k, 0:n], True, True, W + [t_u], [tim])
            br = bre[:, :].rearrange("p (k t) -> p k t", k=4)[:, :, 0:n]
            bi = bim[:, :].rearrange("p (k t) -> p k t", k=4)[:, :, 0:n]
            tsl = slice(0, n) if d == 0 else slice(Q - n, Q)
            cT_, sT_ = cosT[:, ks, tsl], sinT[:, ks, tsl]
            gb, t_gi = ginb.next()
            gi_ = gb[:, 0:2 * 4 * n].rearrange("p (c k t) -> p c k t", c=2, k=4)
            tAb, t_tA = tmpA.next()
            tA = tAb[:, 0:4 * n].rearrange("p (k t) -> p k t", k=4)
            tt("dve", gi_[:, 0, :, :], br, cT_, ALU.mult, [tre] + W, [t_gi])
            tt("dve", tA, bi, sT_, ALU.mult, [tim] + W, [t_tA])
            tt("pool", gi_[:, 0, :, :], gi_[:, 0, :, :], tA, ALU.add, [t_tA], [t_gi])
            tAb, t_tA = tmpA.next()
            tA = tAb[:, 0:4 * n].rearrange("p (k t) -> p k t", k=4)
            tt("dve", gi_[:, 1, :, :], bi, cT_, ALU.mult, [tim] + W, [t_gi])
            tt("dve", tA, br, sT_, ALU.mult, [tre] + W, [t_tA])
            tt("pool", gi_[:, 1, :, :], gi_[:, 1, :, :], tA, ALU.subtract, [t_tA], [t_gi])
            return dict(j0=j0, n=n, a=a, ks=ks, gi=gi_, t_gi=t_gi, bn=bn, cT=cT_, sT=sT_, u=u_, t_u=t_u)

        def stage2(cx):
            j0, n, a, ks, gi_, t_gi, bn, cT_, sT_, u_, t_u = (cx[k] for k in ("j0", "n", "a", "ks", "gi", "t_gi", "bn", "cT", "sT", "u", "t_u"))
            first = 0 if d == 0 else n - 1
            lastp = n - 1 if d == 0 else 0
            tt("dve", ct[:, 0, ks], Hin[:, 0, ks], c4[:, ks], ALU.mult, W + [t_car], [t_car])
            tt("dve", ct[:, 1, ks], Hin[:, 1, ks], s4[:, ks], ALU.mult, W, [t_car])
            tt("dve", ct[:, 2, ks], Hin[:, 0, ks], s4[:, ks], ALU.mult, W, [t_car])
            tt("dve", ct[:, 3, ks], Hin[:, 1, ks], c4[:, ks], ALU.mult, W, [t_car])
            tt("dve", carry[:, 0, ks], ct[:, 0, ks], ct[:, 1, ks], ALU.subtract, [t_car], [t_car])
            tt("dve", carry[:, 1, ks], ct[:, 2, ks], ct[:, 3, ks], ALU.add, [t_car], [t_car])
            for cc in range(2):
                tt("dve", carry[:, 2 + cc, ks], carry[:, cc, ks], r4[:, ks], ALU.mult, W, [t_car])
                tt("dve", gi_[:, cc, :, first], gi_[:, cc, :, first], carry[:, 2 + cc, ks], ALU.add, [t_car], [t_gi])
            gob, t_go = goutb.next()
            go = gob[:, 0:2 * 4 * n].rearrange("p (c k t) -> p c k t", c=2, k=4)
            dsrc = (dec if n == Q else dec64)[:, ks, :]
            for cc in range(2):
                fin = gi_[:, cc, :, :].rearrange("p k t -> p (k t)")
                fo = go[:, cc, :, :].rearrange("p k t -> p (k t)")
                fd = dsrc.rearrange("p k t -> p (k t)")
                if d == 1:
                    def rv(x):
                        return bass.AP(x.tensor, x[:, 4 * n - 1:4 * n].offset, [[x.ap[0][0], 128], [-1, 4 * n]])
                    fin, fo, fd = rv(fin), rv(fo), rv(fd)
                E("dve", lambda e, o=fo, a0=fd, a1=fin: e.tensor_tensor_scan(o, a0, a1, 0.0, ALU.mult, ALU.add),
                  rd=[t_gi] + W, wr=[t_go])
            hfb_, t_hf = Hfb.next()
            Hf = hfb_[:, 0:2 * 4 * n].rearrange("p (c k t) -> p c k t", c=2, k=4)
            tAb, t_tA = tmpA.next()
            tA = tAb[:, 0:4 * n].rearrange("p (k t) -> p k t", k=4)
            tt("dve", Hf[:, 0, :, :], go[:, 0, :, :], cT_, ALU.mult, [t_go] + W, [t_hf])
            tt("pool", tA, go[:, 1, :, :], sT_, ALU.mult, [t_go] + W, [t_tA])
            tt("dve", Hf[:, 0, :, :], Hf[:, 0, :, :], tA, ALU.subtract, [t_tA], [t_hf])
            tAb, t_tA = tmpA.next()
            tA = tAb[:, 0:4 * n].rearrange("p (k t) -> p k t", k=4)
            tt("dve", Hf[:, 1, :, :], go[:, 1, :, :], cT_, ALU.mult, [t_go] + W, [t_hf])
            tt("pool", tA, go[:, 0, :, :], sT_, ALU.mult, [t_go] + W, [t_tA])
            tt("dve", Hf[:, 1, :, :], Hf[:, 1, :, :], tA, ALU.add, [t_tA], [t_hf])
            hbb_, t_hb = Hbb.next()
            Hb = hbb_[:, 0:2 * 4 * (n + 2)].rearrange("p (c k t) -> p c k t", c=2, k=4)
            bcol = 0 if d == 0 else n + 1
            cp("act", Hb[:, :, :, 1:n + 1], Hf, [t_hf], [t_hb])
            cp("dve", Hb[:, :, :, bcol], Hin[:, :, ks], [t_car], [t_hb])
            cp("dve", Hin[:, :, ks], Hf[:, :, :, lastp], [t_hf], [t_car])
            sh = slice(0, n) if d == 0 else slice(2, n + 2)
            yb, tyb = banks[4 + bn % 2], tbk[4 + bn % 2]
            for kq in range(4):
                k = a * 4 + kq
                o_ = yb[:, kq * Q:kq * Q + n]
                mm(o_, CH[0][:, k, :], Hb[:, 0, kq, sh], True, False, W + [t_hb], [tyb])
                mm(o_, CH[1][:, k, :], Hb[:, 1, kq, sh], False, False, W + [t_hb], [tyb])
                mm(o_, TOE[:, k, :], u_[:, k, 0:n], False, True, [t_toe, t_u], [tyb])
            ysb, t_ys = ysr.next()
            ys = ysb[:, 0:4 * n].rearrange("p (k t) -> p k t", k=4)
            cp("act", ys, yb[:, :].rearrange("p (k t) -> p k t", k=4)[:, :, 0:n], [tyb], [t_ys])
            for tau in range(4):
                S.dma(dst4[tau, a * 128:(a + 1) * 128, j0:j0 + n].rearrange("(k i) j -> i k j", i=32),
                      ys[tau * 32:(tau + 1) * 32, :, :], rd=[t_ys])

        for (off, L, lat, si) in SEQS:
            if lat:
                S.dma(Hin[:, 0, :], ss5re[l, d].rearrange("(k q) -> q k", q=128), wr=[t_car], allow_slow_non_contiguous=True)
                S.dma(Hin[:, 1, :], ss5im[l, d].rearrange("(k q) -> q k", q=128), wr=[t_car], allow_slow_non_contiguous=True)
            else:
                memset("dve", Hin, 0.0, [t_car])
            LJ = L // 4
            n = min(Q, LJ)
            nch = LJ // n
            order = range(nch) if d == 0 else range(nch - 1, -1, -1)
            prev = None
            for c in order:
                j0 = off // 4 + c * n
                u_, t_u = urot.next()
                for s_ in range(4):
                    S.dma(u_[s_ * 32:(s_ + 1) * 32, :, 0:n], u4[s_, :, j0:j0 + n].rearrange("(k i) j -> i k j", i=32), wr=[t_u])
                for a in range(4):
                    cx = stage1(j0, n, a, u_, t_u)
                    if prev is not None:
                        stage2(prev)
                    prev = cx
            stage2(prev)
            if not lat:
                S.dma(os5re[si, l, d].rearrange("(k q) -> q k", q=128), Hin[:, 0, :], rd=[t_car], allow_slow_non_contiguous=True)
                S.dma(os5im[si, l, d].rearrange("(k q) -> q k", q=128), Hin[:, 1, :], rd=[t_car], allow_slow_non_contiguous=True)

    def phase_s5fin(l):
        ar.reset()
        Wg = ar.bf(4 * 512).rearrange("p (a o) -> p a o", a=4)
        bg = ar.f32(4)
        Dc = ar.f32(4)
        t_w = Trk()
        S.dma(Wg, s5_w_glu[l].rearrange("(a p) o -> p a o", p=128), wr=[t_w], eng="pool")
        S.dma(bg, s5_b_glu[l].rearrange("(a p) -> p a", p=128), wr=[t_w], allow_slow_non_contiguous=True)
        S.dma(Dc, s5_D[l].rearrange("(a p) -> p a", p=128), wr=[t_w], allow_slow_non_contiguous=True)
        N = 512
        JB = 128
        fr = Rot([ar.f32(4 * N).rearrange("p (a t) -> p a t", a=4) for _ in range(2)])
        br_ = Rot([ar.f32(4 * N).rearrange("p (a t) -> p a t", a=4) for _ in range(2)])
        ur = Rot([ar.bf(4 * N).rearrange("p (a t) -> p a t", a=4) for _ in range(2)])
        y2 = ar.f32(4 * N).rearrange("p (a t) -> p a t", a=4)
        gb = Rot([ar.bf(4 * N).rearrange("p (a t) -> p a t", a=4) for _ in range(2)])
        sg = ar.f32(N)
        ob = Rot([ar.bf(4 * N).rearrange("p (a t) -> p a t", a=4) for _ in range(2)])
        t_y2, t_sg = Trk(), Trk()
        tbk = [Trk(), Trk()]
        cnt = 0
        for c in range(NJ // JB):
            j0 = c * JB
            g0 = 4 * j0
            f_, t_f = fr.next()
            b_, t_b = br_.next()
            u_, t_u = ur.next()
            for a in range(4):
                rs = slice(a * 128, (a + 1) * 128)
                S.dma(f_[:, a, :].rearrange("p (s j) -> p s j", s=4), s5x4[0][:, rs, j0:j0 + JB].rearrange("s p j -> p s j"), wr=[t_f])
                S.dma(b_[:, a, :].rearrange("p (s j) -> p s j", s=4), s5x4[1][:, rs, j0:j0 + JB].rearrange("s p j -> p s j"), wr=[t_b])
                S.dma(u_[:, a, :].rearrange("p (s j) -> p s j", s=4), u4[:, rs, j0:j0 + JB].rearrange("s p j -> p s j"), wr=[t_u])
            tt("dve", f_, f_, b_, ALU.add, [t_b], [t_f])
            tt("pool", b_, u_, Dc.unsqueeze(2).to_broadcast([128, 4, N]), ALU.mult, [t_u, t_w], [t_b])
            tt("dve", f_, f_, b_, ALU.add, [t_b], [t_f])
            tt("pool", y2, f_, f_, ALU.mult, [t_f], [t_y2])
            ts("dve", y2, y2, 0.044715, 1.0, ALU.mult, ALU.add, [t_y2], [t_y2])
            tt("dve", y2, y2, f_, ALU.mult, [t_f], [t_y2])
            act(y2, y2, AF.Sigmoid, [t_y2], [t_y2], scale=1.5957691216057308)
            tt("dve", f_, f_, y2, ALU.mult, [t_y2], [t_f])
            g_, t_g = gb.next()
            cp("act", g_, f_, [t_f], [t_g])
            o_, t_o = ob.next()
            for m in range(4):
                pk, tk = banks[cnt % 2], tbk[cnt % 2]
                cnt += 1
                for a in range(4):
                    mm(pk[:, :], Wg[:, a, m * 128:(m + 1) * 128], g_[:, a, :], a == 0, a == 3, [t_w, t_g], [tk])
                act(sg, pk[:, :], AF.Sigmoid, [tk, t_w], [t_sg], bias=bg[:, m:m + 1])
                tt("dve", o_[:, m, :].rearrange("p (j s) -> p s j", s=4), f_[:, m, :].rearrange("p (s j) -> p s j", s=4),
                   sg.rearrange("p (s j) -> p s j", s=4), ALU.mult, [t_f, t_sg], [t_o])
            S.dma(mixT[1536:2048, g0:g0 + N].rearrange("(a p) t -> p a t", p=128), o_, rd=[t_o])

    def postnorm(ysrcs, rd_y, xt, t_x, gw, gi, tmp, t_tmp, ss, t_ss, junk, t_junk):
        for nb in range(4):
            act(junk[:, 0:512], ysrcs[nb], AF.Square, rd_y, [t_junk, t_ss], accum=ss[:, nb:nb + 1])
        E("dve", lambda e: e.reduce_sum(ss[:, 4:5], ss[:, 0:4], axis=AX.X), rd=[t_ss], wr=[t_ss])
        ts("dve", ss[:, 5:6], ss[:, 4:5], 1.0 / D, 1e-6, ALU.mult, ALU.add, [t_ss], [t_ss])
        act(ss[:, 6:7], ss[:, 5:6], AF.Sqrt, [t_ss], [t_ss])
        E("dve", lambda e: e.reciprocal(ss[:, 7:8], ss[:, 6:7]), rd=[t_ss], wr=[t_ss])
        for nb in range(4):
            stt(tmp[:, nb * 512:(nb + 1) * 512], ysrcs[nb], ss[:, 7:8], gates[:, gw, gi, nb * 512:(nb + 1) * 512],
                ALU.mult, ALU.mult, list(rd_y) + [t_ss, t_gates], [t_tmp])
        tt("pool", xt, xt, tmp, ALU.add, [t_tmp, t_x], [t_x])

    def phase_C(l, t5, xsrc):
        ar.reset()
        T = 1024
        tok0 = t5 * T
        gi = 0 if t5 == 0 else 1
        mT = ar.bf(16 * T).rearrange("p (k t) -> p k t", k=16)
        t_m = Trk()
        if t5 == 0:
            for k in range(16):
                S.dma(mT[:, k, :], mixT[k * 128:(k + 1) * 128, tok0:tok0 + T], wr=[t_m])
        else:
            hm = ar.f32(2)
            t_hm = Trk()
            S.dma(hm, hmask, wr=[t_hm])
            arot = Rot([ar.bf(T) for _ in range(2)])
            brot = Rot([ar.bf(T) for _ in range(2)])
            cA = NPR * LP + (t5 - 1) * T
            for k in range(16):
                if 6 <= k < 12:
                    S.dma(mT[:, k, :], mixA_h[(k - 6) * 128:(k - 5) * 128, (t5 - 1) * T:t5 * T], wr=[t_m])
                    continue
                a_, t_a = arot.next()
                b_, t_b = brot.next()
                S.dma(a_, mixT[k * 128:(k + 1) * 128, cA:cA + T], wr=[t_a])
                S.dma(b_, mixT[k * 128:(k + 1) * 128, cA + LH:cA + LH + T], wr=[t_b])
                ts("pool", a_, a_, hm[:, 0:1], None, ALU.mult, ALU.bypass, [t_hm], [t_a])
                stt(mT[:, k, :], b_, hm[:, 1:2], a_, ALU.mult, ALU.add, [t_b, t_a, t_hm], [t_m])
        wo = ar.bf(16 * D).rearrange("p (k n) -> p k n", k=16)
        t_wo = Trk()
        for nb in range(4):
            S.dma(wo[:, :, nb * 512:(nb + 1) * 512],
                  WB["wout"][:, nb * 512:(nb + 1) * 512].rearrange("(k p) n -> p k n", p=128), wr=[t_wo])
        xrot = Rot([ar.f32(D) for _ in range(2)])
        tmp = ar.f32(D)
        t_tmp = Trk()
        junk = ar.bf(512)
        t_junk = Trk()
        ssr = Rot([ar.f32(8) for _ in range(2)])
        t_bk = [Trk() for _ in range(8)]
        for sub in range(T // 128):
            bo = (sub % 2) * 4
            for nb in range(4):
                for k in range(16):
                    mm(banks[bo + nb][:, :], mT[:, k, sub * 128:(sub + 1) * 128], wo[:, k, nb * 512:(nb + 1) * 512],
                       k == 0, k == 15, [t_m, t_wo], [t_bk[bo + nb]])
            xt, t_x = xrot.next()
            g0 = tok0 + sub * 128
            S.dma(xt, xsrc[g0:g0 + 128, :], wr=[t_x])
            ss, t_ss = ssr.next()
            postnorm([banks[bo + nb][:, :] for nb in range(4)], [t_bk[bo + nb] for nb in range(4)],
                     xt, t_x, 0, gi, tmp, t_tmp, ss, t_ss, junk, t_junk)
            S.dma(x1[g0:g0 + 128, :], xt, rd=[t_x])

    ffn_keep = {}

    def phase_F1(l, t10):
        ar.reset()
        T = 512
        tok0 = t10 * T
        gi = 0 if t10 < 2 else 1
        actT = ar.bf(44 * T).rearrange("p (k t) -> p k t", k=44)
        ffn_keep["actT"] = actT
        ffn_keep["mark"] = ar.p
        hT = ar.bf(16 * T).rearrange("p (k t) -> p k t", k=16)
        t_h = Trk()
        prenorm(x1, tok0, T, gi, 2, 3, hT, t_h)
        grot = Rot([ar.bf(16 * 256).rearrange("p (k n) -> p k n", k=16) for _ in range(2)])
        urot = Rot([ar.bf(16 * 256).rearrange("p (k n) -> p k n", k=16) for _ in range(2)])
        sgr = Rot([ar.f32(512) for _ in range(2)])
        t_pg = [Trk(), Trk()]
        t_pu = [Trk(), Trk()]
        t_act = Trk()
        cnt = 0
        for blk in range(22):
            wg, t_wg = grot.next()
            wu, t_wu = urot.next()
            S.dma(wg, WB["f1"][:, blk * 256:(blk + 1) * 256].rearrange("(k p) n -> p k n", p=128), wr=[t_wg])
            S.dma(wu, WB["f1"][:, DFF + blk * 256: DFF + (blk + 1) * 256].rearrange("(k p) n -> p k n", p=128),
                  wr=[t_wu])
            for mt in range(2):
                ft = blk * 2 + mt
                pg, tg = banks[cnt % 2], t_pg[cnt % 2]
                pu, tu = banks[2 + cnt % 2], t_pu[cnt % 2]
                for k in range(16):
                    mm(pg[:, :], wg[:, k, mt * 128:(mt + 1) * 128], hT[:, k, :], k == 0, k == 15, [t_wg, t_h], [tg])
                for k in range(16):
                    mm(pu[:, :], wu[:, k, mt * 128:(mt + 1) * 128], hT[:, k, :], k == 0, k == 15, [t_wu, t_h], [tu])
                sg, t_sg = sgr.next()
                act(sg, pg[:, :], AF.Silu, [tg], [t_sg])
                tt("dve", actT[:, ft, :], sg, pu[:, :], ALU.mult, [t_sg, tu], [t_act])
                cnt += 1

    def phase_F2(l, t10, xdst):
        T = 512
        tok0 = t10 * T
        gi = 0 if t10 < 2 else 1
        actT = ffn_keep["actT"]
        ar.p = ffn_keep["mark"]
        t_act = Trk()
        wrot = Rot([ar.bf(44 * 256).rearrange("p (k n) -> p k n", k=44) for _ in range(2)])
        stash = ar.f32(4 * D).rearrange("p (s d) -> p s d", s=4)
        t_st = [Trk() for _ in range(4)]
        t_pk = [Trk(), Trk()]
        cnt = 0
        for nb in range(8):
            wt, t_w = wrot.next()
            S.dma(wt, WB["f2"][:, nb * 256:(nb + 1) * 256].rearrange("(k p) n -> p k n", p=128), wr=[t_w])
            for sub in range(4):
                pk, tk = banks[cnt % 2], t_pk[cnt % 2]
                for k in range(44):
                    mm(pk[:, 0:256], actT[:, k, sub * 128:(sub + 1) * 128], wt[:, k, :], k == 0, k == 43,
                       [t_act, t_w], [tk])
                cp("act" if cnt % 2 == 0 else "dve", stash[:, sub, nb * 256:(nb + 1) * 256], pk[:, 0:256],
                   [tk], [t_st[sub]])
                cnt += 1
        xrot = Rot([ar.f32(D) for _ in range(1)])
        tmp = ar.f32(D)
        t_tmp = Trk()
        junk = ar.bf(512)
        t_junk = Trk()
        ssr = Rot([ar.f32(8) for _ in range(2)])
        for sub in range(4):
            xt, t_x = xrot.next()
            g0 = tok0 + sub * 128
            S.dma(xt, x1[g0:g0 + 128, :], wr=[t_x])
            ss, t_ss = ssr.next()
            postnorm([stash[:, sub, nb * 512:(nb + 1) * 512] for nb in range(4)], [t_st[sub]],
                     xt, t_x, 1, gi, tmp, t_tmp, ss, t_ss, junk, t_junk)
            S.dma(xdst[g0:g0 + 128, :], xt, rd=[t_x])

    for l in range(NLAYERS):
        layer(l)
    S.finish()
    return nc


_NC = None


def kernel(**inp):
    global _NC
    f = lambda a: np.ascontiguousarray(np.asarray(a, dtype=np.float32))
    xp, xs = f(inp["x_prompt"]), f(inp["x_sample"])
    shared = {}
    for k in ("w_ada", "b_ada", "norm_mix_pre", "norm_mix_post", "norm_ffn_pre", "norm_ffn_post", "w_in", "w_out",
              "ssd_conv_w", "ssd_conv_b", "ssd_D", "ssd_norm", "attn_sink", "s5_log_dt", "s5_C_re", "s5_C_im",
              "s5_D", "s5_w_glu", "s5_b_glu", "w_ffn_in", "w_ffn_out"):
        shared[k] = f(inp[k])
    shared["ssd_dt_bias"] = f(inp["ssd_dt_bias"]).reshape(2, 24)
    shared["ssd_A_log"] = f(inp["ssd_A_log"]).reshape(2, 24)
    shared["s5_A_re"] = f(inp["s5_A_re"]).reshape(2, 2, 2048)
    shared["s5_A_im"] = f(inp["s5_A_im"]).reshape(2, 2, 2048)
    shared["s5_B_re"] = f(inp["s5_B_re"]).reshape(2, 2048, 16)
    shared["s5_B_im"] = f(inp["s5_B_im"]).reshape(2, 2048, 16)
    in_maps = []
    for c in range(8):
        b = c // 2
        m = dict(shared)
        hf = c % 2
        m["xin"] = np.ascontiguousarray(np.concatenate([xp[4 * c:4 * c + 4].reshape(NPR * LP, D),
                                                        xs[b, hf * LH:(hf + 1) * LH]], axis=0))
        hm = np.zeros((128, 2), np.float32)
        hm[:, hf] = 1.0
        m["hmask"] = hm
        m["cvec"] = np.ascontiguousarray(np.stack([f(inp["c_ctx"]), f(inp["c"])[b]], axis=0))
        m["ckin"] = f(inp["cache_k"])[b].reshape(2, 512, 256)
        m["cvin"] = f(inp["cache_v"])[b].reshape(2, 512, 256)
        m["sssd"] = f(inp["state_ssd"])[b]
        m["ss5re"] = f(inp["state_s5_re"])[b].reshape(2, 2, 2048)
        m["ss5im"] = f(inp["state_s5_im"])[b].reshape(2, 2, 2048)
        in_maps.append(m)
    if _NC is None:
        _NC = build()
    res = run_bass_kernel_spmd(_NC, in_maps, core_ids=list(range(8)))
    R = res.results
    y_prompt = np.concatenate([R[c]["yout"][:NPR * LP].reshape(NPR, LP, D) for c in range(8)], axis=0)
    y_sample = np.stack([np.concatenate([R[2 * b]["yout"][NPR * LP:], R[2 * b + 1]["yout"][NPR * LP:]], axis=0)
                         for b in range(4)], axis=0)
    ck = np.concatenate([R[c]["ock"].reshape(NPR, 2, LP, 4, 64) for c in range(8)], axis=0)
    cv = np.concatenate([R[c]["ocv"].reshape(NPR, 2, LP, 4, 64) for c in range(8)], axis=0)
    sd = np.concatenate([R[c]["ossd"] for c in range(8)], axis=0)
    s5r = np.concatenate([R[c]["os5re"].reshape(NPR, 2, 2, 32, 64) for c in range(8)], axis=0)
    s5i = np.concatenate([R[c]["os5im"].reshape(NPR, 2, 2, 32, 64) for c in range(8)], axis=0)
    return (y_prompt.astype(np.float32), y_sample.astype(np.float32), ck.astype(np.float32), cv.astype(np.float32),
            sd.astype(np.float32), s5r.astype(np.float32), s5i.astype(np.float32))
```

```python
import contextlib
import math
import numpy as np
import concourse.bass as bass
import concourse.mybir as mybir
from concourse.bass_utils import run_bass_kernel_spmd

F32 = mybir.dt.float32
BF16 = mybir.dt.bfloat16
I32 = mybir.dt.int32
ALU = mybir.AluOpType
AF = mybir.ActivationFunctionType
AX = mybir.AxisListType

NPR, LP, LS = 4, 256, 4096
NT = NPR * LP + LS
LH = LS // 2
NTD = NPR * LP + LH
D = 2048
DIN = 3608
DFF = 5632
NEG = -30000.0
TWO_PI = 2.0 * math.pi


class Tok:
    __slots__ = ("sem", "val", "eng")

    def __init__(self, sem, val=None, eng=None):
        self.sem = sem
        self.val = val
        self.eng = eng


class Trk:
    __slots__ = ("w", "r")

    def __init__(self):
        self.w = None
        self.r = []


class Op:
    __slots__ = ("fn", "waits", "signal", "sem", "inc")

    def __init__(self, fn, waits):
        self.fn = fn
        self.waits = waits
        self.signal = False
        self.sem = None
        self.inc = 1


ENGS = ("pe", "act", "dve", "pool", "sp")


class Sched:
    def __init__(self, nc, n_dma_sems=32):
        self.nc = nc
        self.stack = contextlib.ExitStack()
        self.ops = {e: [] for e in ENGS}
        self.sem = {e: self.stack.enter_context(nc.semaphore("s_" + e)) for e in ENGS}
        self.cnt = {e: 0 for e in ENGS}
        self.pending = {e: [] for e in ENGS}
        self.lastc = {e: None for e in ENGS}
        self.known = {e: {} for e in ENGS}
        self.dsems = [self.stack.enter_context(nc.semaphore("s_d%d" % i)) for i in range(n_dma_sems)]
        self.dcnt = [0] * n_dma_sems
        self.dlast = [None] * n_dma_sems
        self.dnext = 0
        self.dnext_sw = 0
        self.nalloc = 0
        self.bgsems = [self.stack.enter_context(nc.semaphore("s_bg%d" % i)) for i in range(28)]
        self.bgnext = 0
        self.csem = self.stack.enter_context(nc.semaphore("s_cc"))
        self.ccnt = 0
        self.ctoks = []

    def sb(self, shape, dtype, name=None):
        self.nalloc += 1
        return self.stack.enter_context(self.nc.sbuf_tensor(name or ("sb%d" % self.nalloc), list(shape), dtype))

    def ps(self, shape, dtype, name=None):
        self.nalloc += 1
        return self.stack.enter_context(self.nc.psum_tensor(name or ("ps%d" % self.nalloc), list(shape), dtype))

    def _resolve_eng(self, e):
        if not self.pending[e]:
            return
        last = self.lastc[e]
        assert last is not None and not last.signal
        last.signal = True
        last.sem = self.sem[e]
        self.cnt[e] += 1
        v = self.cnt[e]
        for t in self.pending[e]:
            t.val = v
        self.pending[e] = []

    def _need(self, eng, toks):
        kn = self.known[eng]
        best = {}
        for t in toks:
            if t is None:
                continue
            if t.val is None:
                self._resolve_eng(t.eng)
            key = id(t.sem)
            if kn.get(key, 0) >= t.val:
                continue
            kn[key] = t.val
            best[key] = (t.sem, t.val)
        return list(best.values())

    def _deps(self, rd, wr, skip_pe=False):
        toks = []
        for t in rd:
            toks.append(t.w)
        for t in wr:
            toks.append(t.w)
            toks.extend(t.r)
        if skip_pe:
            toks = [t for t in toks if t is not None and t.eng != "pe"]
        return toks

    def _commit(self, tok, rd, wr):
        for t in rd:
            t.r = [x for x in t.r if not (x.eng is not None and x.eng == tok.eng)] + [tok]
        for t in wr:
            t.w = tok
            t.r = []

    def op(self, eng, fn, rd=(), wr=(), pe_acc=False):
        toks = self._deps(rd, wr, skip_pe=(eng == "pe" and pe_acc))
        waits = self._need(eng, toks)
        o = Op(fn, waits)
        self.ops[eng].append(o)
        self.lastc[eng] = o
        tok = Tok(self.sem[eng], None, eng)
        self.pending[eng].append(tok)
        self._commit(tok, rd, wr)
        return tok

    def dma(self, out, in_, rd=(), wr=(), eng="sp", **kw):
        self._resolve_eng(eng)
        nh = len(self.dsems) // 2
        if eng == "pool":
            i = nh + self.dnext_sw
            self.dnext_sw = (self.dnext_sw + 1) % (len(self.dsems) - nh)
        else:
            i = self.dnext
            self.dnext = (self.dnext + 1) % nh
        toks = self._deps(rd, wr)
        toks.append(self.dlast[i])
        waits = self._need(eng, toks)
        o = Op(lambda e: e.dma_start(out=out, in_=in_, **kw), waits)
        o.signal = True
        o.sem = self.dsems[i]
        o.inc = 16
        self.ops[eng].append(o)
        self.dcnt[i] += 16
        tok = Tok(self.dsems[i], self.dcnt[i], None)
        self.dlast[i] = tok
        self._commit(tok, rd, wr)
        return tok

    def dma_bg(self, out, in_, eng="pool", **kw):
        self._resolve_eng(eng)
        sem = self.bgsems[self.bgnext]
        self.bgnext += 1
        o = Op(lambda e: e.dma_start(out=out, in_=in_, **kw), [])
        o.signal = True
        o.sem = sem
        o.inc = 16
        self.ops[eng].append(o)
        return Tok(sem, 16, None)

    def coll(self, kind, op, groups, src, dst):
        eng = "pool"
        self._resolve_eng(eng)
        o = Op(lambda e: e.collective_compute(kind, op, replica_groups=groups, ins=[src], outs=[dst]), [])
        o.signal = True
        o.sem = self.csem
        o.inc = None
        self.ops[eng].append(o)
        self.ccnt += 1
        tok = Tok(self.csem, self.ccnt, None)
        self.ctoks.append(tok)
        return tok

    def wait_toks(self, toks):
        for e in ENGS:
            waits = self._need(e, toks)
            if waits:
                self.ops[e].append(Op(None, waits))

    def barrier(self):
        for e in ENGS:
            self._resolve_eng(e)
        for e in ENGS:
            toks = [Tok(self.sem[o], self.cnt[o], o) for o in ENGS if o != e and self.cnt[o] > 0]
            toks += [t for t in self.dlast if t is not None]
            toks += self.ctoks[-16:]
            waits = self._need(e, toks)
            if waits:
                self.ops[e].append(Op(None, waits))

    def finish(self):
        self.barrier()
        nc = self.nc

        def run(eng_obj, ops):
            for o in ops:
                for s, v in o.waits:
                    eng_obj.wait_ge(s, v)
                if o.fn is None:
                    continue
                ins = o.fn(eng_obj)
                if o.signal:
                    if o.inc is None:
                        ins.then_inc(o.sem)
                    else:
                        ins.then_inc(o.sem, o.inc)

        with nc.Block() as block:
            @block.tensor
            def _(e):
                run(e, self.ops["pe"])

            @block.scalar
            def _(e):
                run(e, self.ops["act"])

            @block.vector
            def _(e):
                run(e, self.ops["dve"])

            @block.gpsimd
            def _(e):
                run(e, self.ops["pool"])

            @block.sync
            def _(e):
                run(e, self.ops["sp"])
        self.stack.close()


class Arena:
    def __init__(self, t, nwords):
        self.t = t
        self.n = nwords
        self.p = 0

    def reset(self):
        self.p = 0

    def f32(self, n):
        a = self.t[:, self.p:self.p + n]
        self.p += n
        assert self.p <= self.n, ("arena overflow", self.p)
        return a

    def bf(self, n):
        w = (n + 1) // 2
        a = self.t[:, self.p:self.p + w].bitcast(BF16)
        self.p += w
        assert self.p <= self.n, ("arena overflow", self.p)
        return a[:, 0:n]

    def i32(self, n):
        return self.f32(n).bitcast(I32)


class Rot:
    def __init__(self, aps):
        self.items = [(a, Trk()) for a in aps]
        self.i = 0

    def next(self):
        it = self.items[self.i]
        self.i = (self.i + 1) % len(self.items)
        return it


MIX = ("conv", "ssd", "attn", "s5")
DEBUG = False
NLAYERS = 2


def build():
    nc = bass.Bass("TRN2", target_bir_lowering=False)

    def din(name, shape, dt=F32):
        return nc.dram_tensor(name, list(shape), dt, kind="ExternalInput").ap()

    def dout(name, shape, dt=F32):
        return nc.dram_tensor(name, list(shape), dt, kind="ExternalOutput").ap()

    def dscr(name, shape, dt=F32):
        kind = "ExternalOutput" if (DEBUG and name in ("mixT", "projT", "tokm")) else "Internal"
        return nc.dram_tensor(name, list(shape), dt, kind=kind).ap()

    xin = din("xin", [NTD, D])
    hmask = din("hmask", [128, 2])
    cvec = din("cvec", [2, D])
    ckin = din("ckin", [2, 512, 256])
    cvin = din("cvin", [2, 512, 256])
    sssd = din("sssd", [2, 2, 12, 64, 64])
    ss5re = din("ss5re", [2, 2, 2048])
    ss5im = din("ss5im", [2, 2, 2048])
    w_ada = din("w_ada", [2, D, 6 * D])
    b_ada = din("b_ada", [2, 6 * D])
    n_mix_pre = din("norm_mix_pre", [2, D])
    n_mix_post = din("norm_mix_post", [2, D])
    n_ffn_pre = din("norm_ffn_pre", [2, D])
    n_ffn_post = din("norm_ffn_post", [2, D])
    w_in = din("w_in", [2, D, DIN])
    w_out = din("w_out", [2, D, D])
    conv_w = din("ssd_conv_w", [2, 3, 1024])
    conv_b = din("ssd_conv_b", [2, 1024])
    dt_bias = din("ssd_dt_bias", [2, 24])
    a_log = din("ssd_A_log", [2, 24])
    ssd_D = din("ssd_D", [2, 12])
    ssd_norm = din("ssd_norm", [2, 768])
    attn_sink = din("attn_sink", [2, 12])
    s5_A_re = din("s5_A_re", [2, 2, 2048])
    s5_A_im = din("s5_A_im", [2, 2, 2048])
    s5_log_dt = din("s5_log_dt", [2, 2, 32])
    s5_B_re = din("s5_B_re", [2, 2048, 16])
    s5_B_im = din("s5_B_im", [2, 2048, 16])
    s5_C_re = din("s5_C_re", [2, 2, 32, 16, 64])
    s5_C_im = din("s5_C_im", [2, 2, 32, 16, 64])
    s5_D = din("s5_D", [2, 512])
    s5_w_glu = din("s5_w_glu", [2, 512, 512])
    s5_b_glu = din("s5_b_glu", [2, 512])
    w_ffn_in = din("w_ffn_in", [2, D, 2 * DFF])
    w_ffn_out = din("w_ffn_out", [2, DFF, D])
    yout = dout("yout", [NTD, D])
    ock = dout("ock", [NPR, 2, LP, 256])
    ocv = dout("ocv", [NPR, 2, LP, 256])
    ossd = dout("ossd", [NPR, 2, 2, 12, 64, 64])
    os5re = dout("os5re", [NPR, 2, 2, 2048])
    os5im = dout("os5im", [NPR, 2, 2, 2048])
    projT = dscr("projT", [3328, NT], BF16)
    tokm = dscr("tokm", [NT, 536])
    xcT = dscr("xcT", [1024, NT], BF16)
    xtok = dscr("xtok", [NT, 896], BF16)
    yfT = dscr("yfT", [768, NT])
    NJ = NT // 4
    u4 = dscr("u4", [4, 512, NJ], BF16)
    s5x4 = [dscr("s5f4", [4, 512, NJ]), dscr("s5b4", [4, 512, NJ])]
    mixT = dscr("mixT", [2048, NT], BF16)
    x1 = dscr("x1", [NTD, D])
    xmid = dscr("xmid", [NTD, D])
    projT_h = dscr("projT_h", [2816, LH], BF16)
    mixA_h = dscr("mixA_h", [768, LH], BF16)
    tokm_h = dscr("tokm_h", [LH, 536])
    u4_h = dscr("u4_h", [4 * 512, LH // 4], BF16)
    PROJ_RANGES = [(0, 512), (512, 1024), (1024, 1536), (1536, 1792), (2560, 2816)]
    gp = [dscr("g_proj%d" % i, [2 * (r1 - r0), LH], BF16) for i, (r0, r1) in enumerate(PROJ_RANGES)]
    gt = [dscr("g_tokm%d" % i, [2 * 512, 536]) for i in range(LH // 512)]
    gu = [dscr("g_u4%d" % i, [2 * 1024, LH // 4], BF16) for i in range(2)]
    wbs = [dict(ada=dscr("wb_ada%d" % i, [D, 6 * D], BF16), win=dscr("wb_in%d" % i, [D, DIN], BF16),
                wout=dscr("wb_out%d" % i, [D, D], BF16), f1=dscr("wb_f1%d" % i, [D, 2 * DFF], BF16),
                f2=dscr("wb_f2%d" % i, [DFF, D], BF16)) for i in range(2)]
    WB = {}

    S = Sched(nc)
    ARW = 36000
    arena_t = S.sb([128, ARW], F32, "arena")
    ar = Arena(arena_t, ARW)
    banks = [S.ps([128, 512], F32, "bank%d" % i) for i in range(8)]

    def E(eng, fn, rd=(), wr=(), pe_acc=False):
        return S.op(eng, fn, rd=rd, wr=wr, pe_acc=pe_acc)

    def mm(out, lhsT, rhs, start, stop, rd, wr, skip=False):
        return E("pe", lambda e: e.matmul(out, lhsT=lhsT, rhs=rhs, start=start, stop=stop, skip_group_check=skip),
                 rd=rd, wr=wr, pe_acc=(not start))

    def tr(out, in_, ident, rd, wr):
        return E("pe", lambda e: e.transpose(out, in_, ident), rd=rd, wr=wr)

    def act(out, in_, func, rd, wr, scale=1.0, bias=0.0, accum=None, eng="act"):
        if accum is None:
            return E(eng, lambda e: e.activation(out=out, in_=in_, func=func, scale=scale, bias=bias), rd=rd, wr=wr)
        return E(eng, lambda e: e.activation(out=out, in_=in_, func=func, scale=scale, bias=bias, accum_out=accum),
                 rd=rd, wr=wr)

    def tt(eng, out, a, b, op, rd, wr):
        return E(eng, lambda e: e.tensor_tensor(out, a, b, op), rd=rd, wr=wr)

    def ts(eng, out, a, s1, s2, op0, op1, rd, wr):
        return E(eng, lambda e: e.tensor_scalar(out, a, s1, s2, op0, op1), rd=rd, wr=wr)

    def stt(out, a, sc, b, op0, op1, rd, wr):
        return E("dve", lambda e: e.scalar_tensor_tensor(out, a, sc, b, op0, op1), rd=rd, wr=wr)

    def cp(eng, out, a, rd, wr):
        if eng == "act":
            return E("act", lambda e: e.copy(out, a), rd=rd, wr=wr)
        return E(eng, lambda e: e.tensor_copy(out, a), rd=rd, wr=wr)

    def memset(eng, out, v, wr):
        return E(eng, lambda e: e.memset(out, v), wr=wr)

    ident_f = S.sb([128, 128], F32, "ident_f")
    ident_b = S.sb([128, 128], BF16, "ident_b")
    ones_f = S.sb([128, 128], F32, "ones_f")
    ones_b = S.sb([128, 128], BF16, "ones_b")
    U_f = S.sb([128, 128], F32, "U_f")
    L_f = S.sb([128, 128], F32, "L_f")
    nm_fwd = S.sb([128, 128], F32, "nm_fwd")
    nm_bwd = S.sb([128, 128], F32, "nm_bwd")
    perm_b = S.sb([128, 128], BF16, "perm_b")
    negones_f = S.sb([128, 128], F32, "negones_f")
    nm3 = [S.sb([128, 384], F32, "nm3_%d" % i) for i in range(2)]
    nm3b = [S.sb([128, 384], BF16, "nm3b_%d" % i) for i in range(2)]
    nmb = [S.sb([128, 128], BF16, "nmb_%d" % i) for i in range(2)]
    modc = S.sb([128, 4, 16, 2], F32, "modc")
    gates = S.sb([128, 2, 2, D], F32, "gates")
    tC = Trk()
    t_modc = Trk()
    t_gates = Trk()
    iot = S.sb([128, 128], F32, "iot")
    E("pool", lambda e: e.iota(iot[:], pattern=[[1, 128]], base=0, channel_multiplier=-1,
                               allow_small_or_imprecise_dtypes=True), wr=[tC])
    ts("dve", ident_f[:], iot[:], 0.0, None, ALU.is_equal, ALU.bypass, [tC], [tC])
    cp("dve", ident_b[:], ident_f[:], [tC], [tC])
    memset("dve", ones_f[:], 1.0, [tC])
    memset("dve", ones_b[:], 1.0, [tC])
    memset("dve", negones_f[:], -1.0, [tC])
    ts("dve", U_f[:], iot[:], 0.0, None, ALU.is_ge, ALU.bypass, [tC], [tC])
    ts("dve", L_f[:], iot[:], 0.0, None, ALU.is_le, ALU.bypass, [tC], [tC])
    ts("dve", nm_fwd[:], iot[:], 0.0, NEG, ALU.is_lt, ALU.mult, [tC], [tC])
    ts("dve", nm_bwd[:], iot[:], 0.0, NEG, ALU.is_gt, ALU.mult, [tC], [tC])
    for i in range(3):
        cp("dve", nm3[0][:, i * 128:(i + 1) * 128], nm_fwd[:], [tC], [tC])
        cp("dve", nm3[1][:, i * 128:(i + 1) * 128], nm_bwd[:], [tC], [tC])
        cp("dve", nm3b[0][:, i * 128:(i + 1) * 128], nm_fwd[:], [tC], [tC])
        cp("dve", nm3b[1][:, i * 128:(i + 1) * 128], nm_bwd[:], [tC], [tC])
    cp("dve", nmb[0][:], nm_fwd[:], [tC], [tC])
    cp("dve", nmb[1][:], nm_bwd[:], [tC], [tC])
    pa = S.sb([128, 128], F32, "pa")
    pidx = S.sb([128, 1], I32, "pidx")
    pf = S.sb([128, 4], F32, "pf")
    E("pool", lambda e: e.iota(pidx[:], pattern=[[0, 1]], base=0, channel_multiplier=1), wr=[tC])
    pidx2 = S.sb([128, 1], I32, "pidx2")
    E("dve", lambda e: e.tensor_single_scalar(pidx2[:], pidx[:], 32, ALU.bitwise_and), rd=[tC], wr=[tC])
    cp("dve", pf[:, 0:1], pidx2[:], [tC], [tC])
    ts("dve", pf[:, 1:2], pf[:, 0:1], -2.0, 32.0, ALU.mult, ALU.add, [tC], [tC])
    ts("dve", pa[:], iot[:], pf[:, 1:2], None, ALU.is_equal, ALU.bypass, [tC], [tC])
    cp("dve", perm_b[:], pa[:], [tC], [tC])

    def convert(dst, src, rows, cols):
        toks = []
        step = 1024
        for r0 in range(0, rows, step):
            r1 = min(rows, r0 + step)
            toks.append(S.dma_bg(dst[r0:r1, :], src[r0:r1, :], eng="pool"))
        return toks

    conv_toks = []
    for l_ in range(NLAYERS):
        conv_toks.append(dict(
            ada=convert(wbs[l_]["ada"], w_ada[l_], D, 6 * D),
            win=convert(wbs[l_]["win"], w_in[l_], D, DIN),
            wout=convert(wbs[l_]["wout"], w_out[l_], D, D),
            f1=convert(wbs[l_]["f1"], w_ffn_in[l_], D, 2 * DFF),
            f2=convert(wbs[l_]["f2"], w_ffn_out[l_], DFF, D)))

    def need_w(l, name):
        S.wait_toks(conv_toks[l][name])
        WB[name] = wbs[l][name]

    PAIRS = [[0, 1], [2, 3], [4, 5], [6, 7]]

    def exchange_jobs():
        jobs = []
        for i, (r0, r1) in enumerate(PROJ_RANGES):
            jobs.append((projT_h[r0:r1, :], gp[i], ("p", r0, r1)))
        for i, t0 in enumerate(range(0, LH, 512)):
            jobs.append((tokm_h[t0:t0 + 512, :], gt[i], ("t", t0, t0 + 512)))
        for i in range(2):
            jobs.append((u4_h[i * 1024:(i + 1) * 1024, :], gu[i], ("u", i, None)))
        return jobs

    def exchange_issue():
        for (src, dst, _) in exchange_jobs():
            S.coll("AllGather", ALU.bypass, PAIRS, src, dst)

    def exchange_finish():
        for (src, dst, (kind, a0, a1)) in exchange_jobs():
            for h in range(2):
                c0 = NPR * LP + h * LH
                if kind == "p":
                    n = a1 - a0
                    S.dma(projT[a0:a1, c0:c0 + LH], dst[h * n:(h + 1) * n, :])
                elif kind == "t":
                    S.dma(tokm[c0 + a0:c0 + a1, :], dst[h * 512:(h + 1) * 512, :])
                else:
                    for ss in range(2):
                        s_ = a0 * 2 + ss
                        S.dma(u4[s_, :, c0 // 4: c0 // 4 + LH // 4],
                              dst[h * 1024 + ss * 512: h * 1024 + (ss + 1) * 512, :])
        S.barrier()

    def layer(l):
        need_w(l, "ada")
        x_src = xin if l == 0 else xmid
        x_dst = xmid if l == 0 else yout
        phase_mod(l)
        S.barrier()
        need_w(l, "win")
        for t3 in range(3):
            phase_A(l, t3, x_src)
            S.barrier()
        exchange_issue()
        S.barrier()
        exchange_finish()
        if "conv" in MIX:
            phase_conv(l)
            S.barrier()
        if "ssd" in MIX:
            phase_ssd(l)
            S.barrier()
        if "attn" in MIX:
            phase_attn(l)
            S.barrier()
        if "s5" in MIX:
            for d_ in range(2):
                phase_s5h(l, d_)
                S.barrier()
            phase_s5fin(l)
            S.barrier()
        need_w(l, "wout")
        for t3 in range(3):
            phase_C(l, t3, x_src)
            S.barrier()
        need_w(l, "f1")
        need_w(l, "f2")
        for t10 in range(6):
            phase_F1(l, t10)
            S.barrier()
            phase_F2(l, t10, x_dst)
            S.barrier()

    def phase_mod(l):
        ar.reset()
        cT = ar.f32(32).rearrange("p (k r) -> p k r", r=2)
        sT = ar.bf(32).rearrange("p (k r) -> p k r", r=2)
        t_c = Trk()
        for r in range(2):
            S.dma(cT[:, :, r], cvec[r].rearrange("(k p) -> p k", p=128), wr=[t_c], allow_slow_non_contiguous=True)
        act(sT, cT, AF.Silu, [t_c], [t_c])
        npre = ar.f32(32).rearrange("p (v k) -> p v k", v=2)
        t_np = Trk()
        S.dma(npre[:, 0, :], n_mix_pre[l].rearrange("(k p) -> p k", p=128), wr=[t_np], allow_slow_non_contiguous=True)
        S.dma(npre[:, 1, :], n_ffn_pre[l].rearrange("(k p) -> p k", p=128), wr=[t_np], allow_slow_non_contiguous=True)
        npost = ar.f32(2 * D).rearrange("p (v d) -> p v d", v=2)
        t_npo = Trk()
        S.dma(npost[:, 0, :], n_mix_post[l].partition_broadcast(128), wr=[t_npo])
        S.dma(npost[:, 1, :], n_ffn_post[l].partition_broadcast(128), wr=[t_npo])
        sel = ar.f32(256).rearrange("p (r m) -> p r m", r=2)
        esel = ar.f32(2)
        t_sel = Trk()
        for r in range(2):
            ts("dve", sel[:, r, :], ones_f[:], ident_f[:, r:r + 1], None, ALU.mult, ALU.bypass, [tC], [t_sel])
        cp("dve", esel, ident_f[:, 0:2], [tC], [t_sel])
        wrot = Rot([ar.bf(16 * 512).rearrange("p (k n) -> p k n", k=16) for _ in range(2)])
        brot = Rot([ar.f32(512) for _ in range(2)])
        mrot = Rot([ar.f32(512) for _ in range(2)])
        t_b0, t_b1, t_b2 = Trk(), Trk(), Trk()
        colsb = ar.f32(8)
        t_colsb = Trk()
        for blk in range(24):
            vec = blk // 4
            wt, t_w = wrot.next()
            S.dma(wt, WB["ada"][:, blk * 512:(blk + 1) * 512].rearrange("(k p) n -> p k n", p=128), wr=[t_w])
            bt, t_b = brot.next()
            S.dma(bt[0:2, :], b_ada[l, blk * 512:(blk + 1) * 512].partition_broadcast(2), wr=[t_b])
            for k in range(16):
                mm(banks[0][0:2, :], sT[:, k, :], wt[:, k, :], k == 0, k == 15, [t_c, t_w], [t_b0])
            mt, t_m = mrot.next()
            tt("dve", mt[0:2, :], banks[0][0:2, :], bt[0:2, :], ALU.add, [t_b0, t_b], [t_m])
            if vec in (2, 5):
                which = 0 if vec == 2 else 1
                for r in range(2):
                    mm(banks[1][:, :], sel[0:2, r, :], mt[0:2, :], True, True, [t_sel, t_m], [t_b1])
                    tt("dve", gates[:, which, r, (blk % 4) * 512:(blk % 4 + 1) * 512], banks[1][:, :],
                       npost[:, which, (blk % 4) * 512:(blk % 4 + 1) * 512], ALU.mult, [t_b1, t_npo], [t_gates])
            else:
                vi = {0: 1, 1: 0, 3: 3, 4: 2}[vec]
                for j in range(4):
                    for r in range(2):
                        mm(banks[2][:, j * 2 + r:j * 2 + r + 1], mt[0:2, j * 128:(j + 1) * 128], esel[0:2, r:r + 1],
                           True, True, [t_m, t_sel], [t_b2])
                kc0 = (blk % 4) * 4
                cp("dve", modc[:, vi, kc0:kc0 + 4, :], banks[2][:, 0:8].rearrange("p (j r) -> p j r", r=2),
                   [t_b2], [t_modc])
        for r in range(2):
            for (vi, v) in ((0, 0), (2, 1)):
                stt(modc[:, vi, :, r], modc[:, vi, :, r], 1.0, npre[:, v, :], ALU.add, ALU.mult,
                    [t_modc, t_np], [t_modc])

    def prenorm(xsrc, tok0, T, gi, vA, vB, hT, t_h):
        nsub = T // 128
        xrot = Rot([ar.f32(D) for _ in range(2)])
        nrot = Rot([ar.bf(D) for _ in range(2)])
        junk = ar.bf(D)
        t_junk = Trk()
        ssr = Rot([ar.f32(4) for _ in range(2)])
        t_tp = [Trk(), Trk()]
        for sub in range(nsub):
            xt, t_x = xrot.next()
            S.dma(xt, xsrc[tok0 + sub * 128: tok0 + (sub + 1) * 128, :], wr=[t_x])
            ss, t_ss = ssr.next()
            act(junk, xt, AF.Square, [t_x], [t_junk, t_ss], accum=ss[:, 0:1])
            ts("dve", ss[:, 1:2], ss[:, 0:1], 1.0 / D, 1e-6, ALU.mult, ALU.add, [t_ss], [t_ss])
            act(ss[:, 2:3], ss[:, 1:2], AF.Sqrt, [t_ss], [t_ss])
            E("dve", lambda e, o=ss[:, 3:4], i=ss[:, 2:3]: e.reciprocal(o, i), rd=[t_ss], wr=[t_ss])
            xn, t_n = nrot.next()
            ts("dve", xn, xt, ss[:, 3:4], None, ALU.mult, ALU.bypass, [t_x, t_ss], [t_n])
            for q4 in range(4):
                pb = banks[6 + (q4 % 2)][:, 0:256].bitcast(BF16).rearrange("p (a b) -> p a b", a=4)
                tp = t_tp[q4 % 2]
                for a in range(4):
                    kc = q4 * 4 + a
                    tr(pb[:, a, :], xn[:, kc * 128:(kc + 1) * 128], ident_b[:], [t_n, tC], [tp])
                cp("act" if q4 % 2 == 0 else "dve", hT[:, q4 * 4:(q4 + 1) * 4, sub * 128:(sub + 1) * 128], pb,
                   [tp], [t_h])
        for kc in range(16):
            act(hT[:, kc, :], hT[:, kc, :], AF.Identity, [t_h, t_modc], [t_h],
                scale=modc[:, vA, kc, gi:gi + 1], bias=modc[:, vB, kc, gi:gi + 1])

    FM_BLOCKS = [(0, 512, 0), (512, 256, 512), (768, 512, 768), (1280, 512, 1280),
                 (1816, 512, 1792), (2328, 256, 2304), (2584, 256, 2560), (3096, 512, 2816)]

    def phase_A(l, t5, xsrc):
        ar.reset()
        T = 1024
        tok0 = t5 * T
        gi = 0 if t5 == 0 else 1
        if t5 == 0:
            pj_dst, tk_dst, dcol = projT, tokm, 0
            u4v = u4
        else:
            pj_dst, tk_dst, dcol = projT_h, tokm_h, (t5 - 1) * T
            u4v = u4_h.rearrange("(s f) j -> s f j", s=4)
        hT = ar.bf(16 * T).rearrange("p (k t) -> p k t", k=16)
        t_h = Trk()
        prenorm(xsrc, tok0, T, gi, 0, 1, hT, t_h)
        wtm = ar.bf(16 * 536).rearrange("p (k n) -> p k n", k=16)
        t_wtm = Trk()
        for (c0, n, d0) in ((1792, 24, 0), (2584, 256, 24), (2840, 256, 280)):
            S.dma(wtm[:, :, d0:d0 + n], WB["win"][:, c0:c0 + n].rearrange("(k p) n -> p k n", p=128), wr=[t_wtm])
        trot = Rot([ar.f32(536) for _ in range(2)])
        t_pa, t_pb = [Trk(), Trk()], [Trk(), Trk()]
        for sub in range(T // 128):
            pa_, pb_ = banks[sub % 2], banks[2 + sub % 2]
            ta, tb = t_pa[sub % 2], t_pb[sub % 2]
            for k in range(16):
                mm(pa_[:, :], hT[:, k, sub * 128:(sub + 1) * 128], wtm[:, k, 24:536], k == 0, k == 15,
                   [t_h, t_wtm], [ta])
            for k in range(16):
                mm(pb_[:, 0:24], hT[:, k, sub * 128:(sub + 1) * 128], wtm[:, k, 0:24], k == 0, k == 15,
                   [t_h, t_wtm], [tb])
            st, t_st = trot.next()
            cp("act", st[:, 24:536], pa_[:, :], [ta], [t_st])
            cp("dve", st[:, 0:24], pb_[:, 0:24], [tb], [t_st])
            g0 = dcol + sub * 128
            S.dma(tk_dst[g0:g0 + 128, :], st, rd=[t_st])
            if t5 == 0:
                sq, so = sub // 2, (sub % 2) * 128
                S.dma(ock[sq, l, so:so + 128, :], st[:, 24:280], rd=[t_st])
                S.dma(ocv[sq, l, so:so + 128, :], st[:, 280:536], rd=[t_st])
        wrot = Rot([ar.bf(16 * 512).rearrange("p (k n) -> p k n", k=16) for _ in range(2)])
        srot = Rot([ar.bf(T) for _ in range(2)])
        t_pm = [Trk() for _ in range(2)]
        cnt = 0
        for (c0, n, d0) in FM_BLOCKS:
            wt, t_w = wrot.next()
            S.dma(wt[:, :, 0:n], WB["win"][:, c0:c0 + n].rearrange("(k p) n -> p k n", p=128), wr=[t_w])
            if d0 == 2816:
                hT4 = hT.rearrange("p k (j s) -> p k s j", s=4)
                for mt in range(4):
                    for s_ in range(4):
                        stg, t_sg = srot.next()
                        pm, tpm = banks[4 + cnt % 2], t_pm[cnt % 2]
                        for k in range(16):
                            mm(pm[:, 0:256], wt[:, k, mt * 128:(mt + 1) * 128], hT4[:, k, s_, :],
                               k == 0, k == 15, [t_w, t_h], [tpm])
                        cp("act" if cnt % 2 == 0 else "dve", stg[:, 0:256], pm[:, 0:256], [tpm], [t_sg])
                        cnt += 1
                        S.dma(u4v[s_, mt * 128:(mt + 1) * 128, dcol // 4: dcol // 4 + 256], stg[:, 0:256], rd=[t_sg])
                continue
            for mt in range(n // 128):
                stg, t_sg = srot.next()
                for nn in range(T // 512):
                    pm, tpm = banks[4 + cnt % 2], t_pm[cnt % 2]
                    for k in range(16):
                        mm(pm[:, :], wt[:, k, mt * 128:(mt + 1) * 128], hT[:, k, nn * 512:(nn + 1) * 512],
                           k == 0, k == 15, [t_w, t_h], [tpm])
                    cp("act" if cnt % 2 == 0 else "dve", stg[:, nn * 512:(nn + 1) * 512], pm[:, :], [tpm], [t_sg])
                    cnt += 1
                S.dma(pj_dst[d0 + mt * 128: d0 + (mt + 1) * 128, dcol:dcol + T], stg, rd=[t_sg])

    SEQS = [(i * LP, LP, False, i) for i in range(NPR)] + [(NPR * LP, LS, True, None)]

    def phase_conv(l):
        ar.reset()
        cw = ar.f32(24).rearrange("p (j k) -> p j k", k=3)
        cb = ar.f32(8)
        t_cw = Trk()
        for k in range(3):
            S.dma(cw[:, :, k], conv_w[l, k].rearrange("(j p) -> p j", p=128), wr=[t_cw], allow_slow_non_contiguous=True)
        S.dma(cb, conv_b[l].rearrange("(j p) -> p j", p=128), wr=[t_cw], allow_slow_non_contiguous=True)
        xrot = Rot([ar.bf(8 * 130).rearrange("p (j t) -> p j t", j=8) for _ in range(2)])
        acc = ar.f32(1024).rearrange("p (j t) -> p j t", j=8)
        acc2 = ar.f32(1024).rearrange("p (j t) -> p j t", j=8)
        t_acc, t_acc2 = Trk(), Trk()
        crot = Rot([ar.bf(1024).rearrange("p (j t) -> p j t", j=8) for _ in range(2)])
        srot = Rot([ar.bf(896) for _ in range(2)])
        t_ps = [Trk(), Trk()]
        ci = 0
        for (off, L, lat, si) in SEQS:
            for c in range(L // 128):
                t0 = c * 128
                g0 = off + t0
                xi, t_xi = xrot.next()
                lo, hi = 0, 130
                if t0 == 0:
                    memset("pool", xi[:, :, 0:1], 0.0, [t_xi])
                    lo = 1
                if t0 + 128 == L:
                    memset("pool", xi[:, :, 129:130], 0.0, [t_xi])
                    hi = 129
                S.dma(xi[:, :, lo:hi], projT[768:1792, g0 - 1 + lo: g0 - 1 + hi].rearrange("(j p) t -> p j t", p=128),
                      wr=[t_xi])
                wb = lambda k: cw[:, :, k:k + 1].to_broadcast([128, 8, 128])
                tt("dve", acc, xi[:, :, 1:129], wb(1), ALU.mult, [t_xi, t_cw], [t_acc])
                tt("pool", acc2, xi[:, :, 0:128], wb(0), ALU.mult, [t_xi, t_cw], [t_acc2])
                tt("dve", acc, acc, acc2, ALU.add, [t_acc2], [t_acc])
                tt("pool", acc2, xi[:, :, 2:130], wb(2), ALU.mult, [t_xi, t_cw], [t_acc2])
                tt("dve", acc, acc, acc2, ALU.add, [t_acc2], [t_acc])
                tt("dve", acc, acc, cb.unsqueeze(2).to_broadcast([128, 8, 128]), ALU.add, [t_cw], [t_acc])
                xc, t_xc = crot.next()
                act(xc, acc, AF.Silu, [t_acc], [t_xc])
                S.dma(xcT[:, g0:g0 + 128].rearrange("(j p) t -> p j t", p=128), xc, rd=[t_xc])
                pb = banks[ci % 2][:, 0:448].bitcast(BF16)
                tp = t_ps[ci % 2]
                for j in range(7):
                    tr(pb[:, j * 128:(j + 1) * 128], xc[:, j, :], ident_b[:], [t_xc, tC], [tp])
                st, t_st = srot.next()
                cp("act", st, pb, [tp], [t_st])
                S.dma(xtok[g0:g0 + 128, :], st, rd=[t_st])
                ci += 1

    def phase_ssd(l):
        ar.reset()
        dtb = ar.f32(24)
        aneg = ar.f32(24)
        Dc = ar.f32(6)
        nw = ar.f32(6)
        t_par = Trk()
        S.dma(dtb, dt_bias[l].partition_broadcast(128), wr=[t_par])
        S.dma(aneg, a_log[l].partition_broadcast(128), wr=[t_par])
        act(aneg, aneg, AF.Exp, [t_par], [t_par])
        ts("dve", aneg, aneg, -1.0, None, ALU.mult, ALU.bypass, [t_par], [t_par])
        dv = ssd_D[l].rearrange("(j t) -> t j", t=2)
        for hh in range(2):
            S.dma(Dc[hh * 64:(hh + 1) * 64, :], dv[hh].partition_broadcast(64), wr=[t_par], allow_slow_non_contiguous=True)
        S.dma(nw, ssd_norm[l].rearrange("(j p) -> p j", p=128), wr=[t_par], allow_slow_non_contiguous=True)
        ST = ar.f32(6 * 64).rearrange("p (h q) -> p h q", h=6)
        STb = ar.bf(6 * 64).rearrange("p (h q) -> p h q", h=6)
        t_ST = Trk()
        stio = ar.f32(12 * 64).rearrange("p (h n) -> p h n", h=12)
        t_stio = Trk()
        dtr = Rot([ar.f32(24) for _ in range(2)])
        xkr = Rot([ar.bf(896) for _ in range(2)])
        bcr = Rot([ar.bf(256).rearrange("p (w t) -> p w t", w=2) for _ in range(2)])
        sm = Rot([ar.f32(96) for _ in range(2)])
        xdtr = Rot([ar.bf(1536).rearrange("p (w h q) -> p w h q", w=2, h=12) for _ in range(2)])
        GTr = Rot([ar.f32(256).rearrange("p (g t) -> p g t", g=2) for _ in range(2)])
        rhr = Rot([ar.f32(384).rearrange("p (i l) -> p i l", i=3) for _ in range(3)])
        e2r = Rot([ar.f32(384) for _ in range(2)])
        dcr = Rot([ar.f32(384) for _ in range(2)])
        scr = Rot([ar.bf(384).rearrange("p (i l) -> p i l", i=3) for _ in range(3)])
        csr = Rot([ar.bf(384).rearrange("p (i l) -> p i l", i=3) for _ in range(2)])
        yst = Rot([ar.f32(768).rearrange("p (j t) -> p j t", j=6) for _ in range(2)])
        yfl = ar.f32(768).rearrange("p (j t) -> p j t", j=6)
        xsl = ar.bf(768).rearrange("p (j t) -> p j t", j=6)
        zl = ar.bf(768).rearrange("p (j t) -> p j t", j=6)
        zs = ar.f32(768).rearrange("p (j t) -> p j t", j=6)
        ysq = ar.bf(768).rearrange("p (j t) -> p j t", j=6)
        rsd = ar.f32(128)
        yo = ar.bf(768).rearrange("p (j t) -> p j t", j=6)
        t_fin = Trk()
        t_yo = Trk()
        pending_fin = []
        tb = {k: Trk() for k in ("c", "tot", "gt", "cr0", "cr1", "y0", "y1", "pst", "ss")}
        tb["tr"] = tb["ss"]
        for d in range(2):
            Tm = U_f if d == 0 else L_f
            nmk = nm_fwd if d == 0 else nm_bwd
            for (off, L, lat, si) in SEQS:
                if lat:
                    S.dma(stio[0:64, :, :], sssd[l, d].rearrange("h p n -> p h n"), wr=[t_stio])
                    for h in range(12):
                        g = h // 6
                        E("pe", lambda e, o=banks[7][g * 64:(g + 1) * 64, (h % 6) * 64:(h % 6 + 1) * 64],
                          i=stio[0:64, h, :]: e.matmul(o, lhsT=i, rhs=ident_f[0:64, 0:64], start=True, stop=True),
                          rd=[t_stio, tC], wr=[tb["tr"]])
                        if h % 6 == 5:
                            cp("dve", ST[g * 64:(g + 1) * 64, :, :],
                               banks[7][g * 64:(g + 1) * 64, 0:384].rearrange("p (h q) -> p h q", h=6), [tb["tr"]], [t_ST])
                else:
                    memset("dve", ST, 0.0, [t_ST])
                cp("act", STb, ST, [t_ST], [t_ST])
                nch = L // 128
                order = range(nch) if d == 0 else range(nch - 1, -1, -1)
                for c in order:
                    g0 = off + c * 128
                    dtt, t_dt = dtr.next()
                    S.dma(dtt, tokm[g0:g0 + 128, 0:24], wr=[t_dt])
                    xk, t_xk = xkr.next()
                    S.dma(xk, xtok[g0:g0 + 128, :], wr=[t_xk])
                    bc, t_bc = bcr.next()
                    S.dma(bc, xcT[768:1024, g0:g0 + 128].rearrange("(w p) t -> p w t", p=128), wr=[t_bc])
                    s_, t_s = sm.next()
                    dt_ = s_[:, 0:24]
                    da = s_[:, 24:48]
                    cneg = s_[:, 48:60]
                    w_ = s_[:, 60:72]
                    edec = s_[:, 72:84]
                    tt("dve", dt_, dtt, dtb, ALU.add, [t_dt, t_par], [t_s])
                    act(dt_, dt_, AF.Exp, [t_s], [t_s])
                    act(dt_, dt_, AF.Ln, [t_s], [t_s], bias=1.0)
                    tt("dve", da, dt_, aneg, ALU.mult, [t_s, t_par], [t_s])
                    dad = da[:, d * 12:(d + 1) * 12]
                    mm(banks[0][:, 0:12], Tm[:], dad, True, True, [tC, t_s], [tb["c"]])
                    mm(banks[0][:, 16:28], ones_f[:], dad, True, True, [tC, t_s], [tb["tot"]])
                    ts("dve", cneg, banks[0][:, 0:12], -1.0, None, ALU.mult, ALU.bypass, [tb["c"]], [t_s])
                    tt("dve", w_, banks[0][:, 16:28], cneg, ALU.add, [tb["tot"], t_s], [t_s])
                    act(w_, w_, AF.Exp, [t_s], [t_s])
                    act(edec, banks[0][:, 16:28], AF.Exp, [tb["tot"]], [t_s])
                    xd, t_xd = xdtr.next()
                    xs3 = xk[:, 0:768].rearrange("p (h q) -> p h q", h=12)
                    tt("dve", xd[:, 0, :, :], xs3, dt_[:, d * 12:(d + 1) * 12].unsqueeze(2).to_broadcast([128, 12, 64]),
                       ALU.mult, [t_xk, t_s], [t_xd])
                    GT, t_GT = GTr.next()
                    for g in range(2):
                        mm(banks[1][:, g * 128:(g + 1) * 128], bc[g * 64:(g + 1) * 64, 0, :], bc[g * 64:(g + 1) * 64, 1, :],
                           True, True, [t_bc], [tb["gt"]])
                    cp("act", GT, banks[1][:, 0:256].rearrange("p (g t) -> p g t", g=2), [tb["gt"]], [t_GT])
                    ys, t_ys = yst.next()
                    def stageA(hb):
                        g = hb // 2
                        gs = slice(g * 64, (g + 1) * 64)
                        hd0 = d * 12 + hb * 3
                        rh, t_rh = rhr.next()
                        tt("pool", rh, Tm[:].unsqueeze(1).to_broadcast([128, 3, 128]),
                           da[:, hd0:hd0 + 3].unsqueeze(2).to_broadcast([128, 3, 128]), ALU.mult, [tC, t_s], [t_rh])
                        crb, tcr = banks[2 + hb % 2], tb["cr%d" % (hb % 2)]
                        mm(crb[:, 0:384], ones_f[:], rh.rearrange("p i l -> p (i l)"), True, True, [tC, t_rh], [tcr])
                        e2, t_e2 = e2r.next()
                        act(e2[gs, :], crb[gs, 0:384], AF.Exp, [tcr], [t_e2])
                        for i in range(3):
                            mm(crb[:, i * 128:(i + 1) * 128], rh[:, i, :], negones_f[:], False, False, [t_rh, tC], [tcr], skip=True)
                        mm(crb[:, 0:384], ident_b[:], nm3b[d][:], False, True, [tC], [tcr], skip=True)
                        dc, t_dc = dcr.next()
                        act(dc, crb[:, 0:384], AF.Exp, [tcr], [t_dc])
                        sc, t_sc = scr.next()
                        tt("dve", sc, dc.rearrange("p (i l) -> p i l", i=3), GT[:, g:g + 1, :].to_broadcast([128, 3, 128]),
                           ALU.mult, [t_dc, t_GT], [t_sc])
                        cs, t_cs = csr.next()
                        tt("pool", cs[gs, :, :], e2[gs, :].rearrange("p (i l) -> p i l", i=3),
                           bc[gs, 1:2, :].to_broadcast([64, 3, 128]), ALU.mult, [t_bc, t_e2], [t_cs])
                        return (hb, sc, t_sc, cs, t_cs)

                    def stageB(cxb):
                        hb, sc, t_sc, cs, t_cs = cxb
                        g = hb // 2
                        gs = slice(g * 64, (g + 1) * 64)
                        for i in range(3):
                            h = hb * 3 + i
                            yb, tyb = banks[4 + (h // 2) % 2], tb["y%d" % ((h // 2) % 2)]
                            yo_ = yb[(h % 2) * 64:(h % 2 + 1) * 64, 0:128]
                            mm(yo_, xd[:, 0, h, :], sc[:, i, :], True, False, [t_xd, t_sc], [tyb])
                            mm(yo_, STb[gs, h % 6, :], cs[gs, i, :], False, True, [t_ST, t_cs], [tyb])
                            mm(banks[6][gs, (h % 6) * 64:(h % 6 + 1) * 64], xk[:, 768 + g * 64: 768 + (g + 1) * 64],
                               xd[:, 1, h, :], True, True, [t_xk, t_xd], [tb["pst"]])
                            stt(ST[gs, h % 6, :], ST[gs, h % 6, :], edec[gs, h:h + 1],
                                banks[6][gs, (h % 6) * 64:(h % 6 + 1) * 64], ALU.mult, ALU.add,
                                [t_s, tb["pst"]], [t_ST])
                            if h % 6 == 5:
                                cp("act", STb[gs, :, :], ST[gs, :, :], [t_ST], [t_ST])
                            if h % 2 == 1:
                                cp("act", ys[:, h // 2, :], yb[:, 0:128], [tyb], [t_ys])

                    prevb = stageA(0)
                    tt("pool", xd[:, 1, :, :], xd[:, 0, :, :], w_.unsqueeze(2).to_broadcast([128, 12, 64]), ALU.mult,
                       [t_s], [t_xd])
                    while pending_fin:
                        pending_fin.pop(0)()
                    for hb in range(1, 4):
                        nxt = stageA(hb)
                        stageB(prevb)
                        prevb = nxt
                    stageB(prevb)
                    if d == 0:
                        S.dma(yfT[:, g0:g0 + 128].rearrange("(j p) t -> p j t", p=128), ys, rd=[t_ys], eng="act")
                    else:
                        def fin_chunk(g0=g0, ys=ys, t_ys=t_ys):
                            S.dma(yfl, yfT[:, g0:g0 + 128].rearrange("(j p) t -> p j t", p=128), wr=[t_fin])
                            S.dma(xsl, xcT[0:768, g0:g0 + 128].rearrange("(j p) t -> p j t", p=128), wr=[t_fin])
                            S.dma(zl, projT[0:768, g0:g0 + 128].rearrange("(j p) t -> p j t", p=128), wr=[t_fin])
                            tt("dve", ys, ys, yfl, ALU.add, [t_fin], [t_ys])
                            tt("pool", yfl, xsl, Dc.unsqueeze(2).to_broadcast([128, 6, 128]), ALU.mult, [t_par], [t_fin])
                            tt("dve", ys, ys, yfl, ALU.add, [t_fin], [t_ys])
                            act(zs, zl, AF.Silu, [t_fin], [t_fin])
                            tt("dve", ys, ys, zs, ALU.mult, [t_fin], [t_ys])
                            tt("pool", ysq, ys, ys, ALU.mult, [t_ys], [t_fin])
                            for j in range(6):
                                mm(banks[7][:, 0:128], ones_b[:], ysq[:, j, :], j == 0, j == 5, [tC, t_fin], [tb["ss"]])
                            ts("dve", rsd, banks[7][:, 0:128], 1.0 / 768, 1e-6, ALU.mult, ALU.add, [tb["ss"]], [t_fin])
                            act(rsd, rsd, AF.Sqrt, [t_fin], [t_fin])
                            E("dve", lambda e: e.reciprocal(rsd, rsd), rd=[t_fin], wr=[t_fin])
                            tt("dve", ys, ys, rsd.unsqueeze(1).to_broadcast([128, 6, 128]), ALU.mult, [t_fin], [t_ys])
                            tt("dve", yo, ys, nw.unsqueeze(2).to_broadcast([128, 6, 128]), ALU.mult, [t_ys, t_par], [t_yo])
                            S.dma(mixT[0:768, g0:g0 + 128].rearrange("(j p) t -> p j t", p=128), yo, rd=[t_yo])
                        pending_fin.append(fin_chunk)
                while pending_fin:
                    pending_fin.pop(0)()
                if not lat:
                    for h in range(12):
                        g = h // 6
                        E("pe", lambda e, o=banks[7][0:64, (h % 6) * 64:(h % 6 + 1) * 64],
                          i=ST[g * 64:(g + 1) * 64, h % 6, :], idn=ident_f[g * 64:(g + 1) * 64, g * 64:(g + 1) * 64]: e.transpose(o, i, idn),
                          rd=[t_ST, tC], wr=[tb["tr"]])
                        if h % 6 == 5:
                            cp("dve", stio[0:64, g * 6:(g + 1) * 6, :],
                               banks[7][0:64, 0:384].rearrange("p (h q) -> p h q", h=6), [tb["tr"]], [t_stio])
                    S.dma(ossd[si, l, d].rearrange("h p n -> p h n"), stio[0:64, :, :], rd=[t_stio])

    def phase_attn(l):
        ar.reset()
        esink = ar.f32(12)
        t_es = Trk()
        S.dma(esink[0:64, :], attn_sink[l].partition_broadcast(64), wr=[t_es])
        act(esink[0:64, :], esink[0:64, :], AF.Exp, [t_es], [t_es])
        t1 = ar.f32(4096)
        t2 = ar.f32(4096)
        t3 = ar.f32(4096)
        pc = ar.f32(8)
        pci = ar.i32(4)
        t_rt = Trk()
        E("pool", lambda e: e.iota(t1[0:64, :].rearrange("p (r c) -> p r c", r=64), pattern=[[1, 64], [0, 64]], base=0,
                                   channel_multiplier=0, allow_small_or_imprecise_dtypes=True), wr=[t_rt])
        E("pool", lambda e: e.iota(t2[0:64, :].rearrange("p (r c) -> p r c", r=64), pattern=[[0, 64], [1, 64]], base=0,
                                   channel_multiplier=0, allow_small_or_imprecise_dtypes=True), wr=[t_rt])
        E("dve", lambda e: e.tensor_single_scalar(pci[0:64, 0:1], pidx[0:64, :], 16, ALU.bitwise_and), rd=[tC], wr=[t_rt])
        E("dve", lambda e: e.tensor_single_scalar(pci[0:64, 1:2], pidx[0:64, :], 15, ALU.bitwise_and), rd=[tC], wr=[t_rt])
        cp("dve", pc[0:64, 0:2], pci[0:64, 0:2], [t_rt], [t_rt])
        ts("dve", pc[0:64, 2:3], pc[0:64, 0:1], 0.0, None, ALU.is_equal, ALU.bypass, [t_rt], [t_rt])
        act(pc[0:64, 3:4], pc[0:64, 1:2], AF.Exp, [t_rt], [t_rt], scale=-math.log(10000.0) / 16.0)
        ts("dve", pc[0:64, 4:5], pf[0:64, 0:1], 1.0 / 16.0, -1.0, ALU.mult, ALU.add, [tC], [t_rt])
        tt("dve", t1[0:64, :], t1[0:64, :], t2[0:64, :], ALU.subtract, [t_rt], [t_rt])
        stt(t1[0:64, :], t1[0:64, :], pc[0:64, 2:3], t2[0:64, :], ALU.mult, ALU.add, [t_rt], [t_rt])
        ts("dve", t1[0:64, :], t1[0:64, :], pc[0:64, 3:4], None, ALU.mult, ALU.bypass, [t_rt], [t_rt])
        t2i = t2.bitcast(I32)
        ts("dve", t2i[0:64, :], t1[0:64, :], 1.0 / TWO_PI, None, ALU.mult, ALU.bypass, [t_rt], [t_rt])
        cp("dve", t3[0:64, :], t2i[0:64, :], [t_rt], [t_rt])
        stt(t1[0:64, :], t3[0:64, :], -TWO_PI, t1[0:64, :], ALU.mult, ALU.add, [t_rt], [t_rt])
        ts("dve", t1[0:64, :], t1[0:64, :], math.pi, -math.pi, ALU.min, ALU.max, [t_rt], [t_rt])
        act(t2[0:64, :], t1[0:64, :], AF.Sin, [t_rt], [t_rt])
        act(t3[0:64, :], t1[0:64, :], AF.Abs, [t_rt], [t_rt])
        act(t3[0:64, :], t3[0:64, :], AF.Sin, [t_rt], [t_rt], scale=-1.0, bias=math.pi / 2)
        ts("dve", t2[0:64, :], t2[0:64, :], pc[0:64, 4:5], None, ALU.mult, ALU.bypass, [t_rt], [t_rt])
        cosT, sinT = t3, t2
        kT = ar.bf(4096)
        qT = ar.bf(4096)
        t_kT, t_qT = Trk(), Trk()
        vf = ar.f32(2048).rearrange("p (t q) -> p t q", q=64)
        vb = ar.bf(2048).rearrange("p (t q) -> p t q", q=64)
        t_v = Trk()
        ckf = ar.f32(256).rearrange("p (t q) -> p t q", q=64)
        ckT = ar.bf(512)
        cvf = ar.f32(256).rearrange("p (t q) -> p t q", q=64)
        cvb = ar.bf(256).rearrange("p (t q) -> p t q", q=64)
        t_ck = Trk()
        etr = Rot([ar.bf(512) for _ in range(4)])
        rtmp = Rot([ar.f32(512) for _ in range(2)])
        dnr = Rot([ar.f32(512) for _ in range(2)])
        otr = Rot([ar.bf(512) for _ in range(2)])
        tbk = [Trk() for _ in range(8)]

        def rope(xT, t_x, ncol, cT=None, sT=None):
            cT = cosT if cT is None else cT
            sT = sinT if sT is None else sT
            for b0 in range(0, ncol, 512):
                sl = slice(b0, b0 + 512)
                mm(banks[7][0:64, :], perm_b[0:64, 0:64], xT[0:64, sl], True, True, [tC, t_x], [tbk[7]])
                rt, t_r = rtmp.next()
                tt("dve", rt[0:64, :], banks[7][0:64, :], sT[0:64, sl], ALU.mult, [tbk[7], t_rt], [t_r])
                tt("pool", xT[0:64, sl], xT[0:64, sl], cT[0:64, sl], ALU.mult, [t_rt], [t_x])
                tt("dve", xT[0:64, sl], xT[0:64, sl], rt[0:64, :], ALU.add, [t_r], [t_x])

        hm = ar.f32(2)
        S.dma(hm, hmask, wr=[t_rt])
        cos_h = ar.f32(LH)
        sin_h = ar.f32(LH)
        for (dst_, src_) in ((cos_h, cosT), (sin_h, sinT)):
            ts("pool", dst_[0:64, :], src_[0:64, 0:LH], hm[0:64, 0:1], None, ALU.mult, ALU.bypass, [t_rt], [t_rt])
            stt(dst_[0:64, :], src_[0:64, LH:2 * LH], hm[0:64, 1:2], dst_[0:64, :], ALU.mult, ALU.add, [t_rt], [t_rt])
        kTw = ar.bf(18 * 128)
        vbw = ar.bf(18 * 64).rearrange("p (t q) -> p t q", q=64)
        t_kw, t_vw = Trk(), Trk()

        def finish_o(nps, dps, tn, td, h, n, col0, dst=None):
            dn, t_dn = dnr.next()
            ts("dve", dn[0:64, 0:n], dps[0:64, 0:n], esink[0:64, h:h + 1], None, ALU.add, ALU.bypass, [td, t_es], [t_dn])
            E("dve", lambda e: e.reciprocal(dn[0:64, 0:n], dn[0:64, 0:n]), rd=[t_dn], wr=[t_dn])
            ot, t_ot = otr.next()
            tt("dve", ot[0:64, 0:n], nps[0:64, 0:n], dn[0:64, 0:n], ALU.mult, [tn, t_dn], [t_ot])
            if dst is None:
                S.dma(mixT[768 + h * 64: 768 + (h + 1) * 64, col0:col0 + n], ot[0:64, 0:n], rd=[t_ot])
            else:
                S.dma(dst[h * 64:(h + 1) * 64, col0:col0 + n], ot[0:64, 0:n], rd=[t_ot])

        cnt = 0
        for (off, L, lat, si) in SEQS:
            nt = L // 128
            for j in range(4):
                S.dma(kT[0:64, 0:L], projT[2560 + j * 64: 2560 + (j + 1) * 64, off:off + L], wr=[t_kT])
                S.dma(vf[:, 0:nt, :], tokm[off:off + L, 280 + j * 64: 280 + (j + 1) * 64].rearrange("(t p) q -> p t q", p=128),
                      wr=[t_v])
                cp("act", vb[:, 0:nt, :], vf[:, 0:nt, :], [t_v], [t_v])
                if lat:
                    rope(kT, t_kT, L)
                    S.dma(ckf, ckin[l, :, j * 64:(j + 1) * 64].rearrange("(t p) q -> p t q", p=128), wr=[t_ck])
                    S.dma(cvf, cvin[l, :, j * 64:(j + 1) * 64].rearrange("(t p) q -> p t q", p=128), wr=[t_ck])
                    for t_ in range(4):
                        E("pe", lambda e, o=banks[7][0:64, t_ * 128:(t_ + 1) * 128], i=ckf[:, t_, :]: e.transpose(o, i, ident_f[:]),
                          rd=[t_ck, tC], wr=[tbk[7]])
                    cp("dve", ckT[0:64, :], banks[7][0:64, :], [tbk[7]], [t_ck])
                    cp("act", cvb, cvf, [t_ck], [t_ck])
                    h0c, h1c = hm[0:64, 0:1], hm[0:64, 1:2]
                    ts("pool", kTw[0:64, 0:128], kT[0:64, LH - 128:LH], h1c, None, ALU.mult, ALU.bypass, [t_kT, t_rt], [t_kw])
                    ts("pool", kTw[0:64, 128:128 + LH], kT[0:64, 0:LH], h0c, None, ALU.mult, ALU.bypass, [t_kT, t_rt], [t_kw])
                    stt(kTw[0:64, 128:128 + LH], kT[0:64, LH:2 * LH], h1c, kTw[0:64, 128:128 + LH], ALU.mult, ALU.add,
                        [t_kT, t_rt], [t_kw])
                    ts("pool", kTw[0:64, 128 + LH:256 + LH], kT[0:64, LH:LH + 128], h0c, None, ALU.mult, ALU.bypass, [t_kT, t_rt], [t_kw])
                    ts("pool", vbw[:, 0, :], vb[:, 15, :], hm[:, 1:2], None, ALU.mult, ALU.bypass, [t_v, t_rt], [t_vw])
                    ts("pool", vbw[:, 1:17, :], vb[:, 0:16, :], hm[:, 0:1], None, ALU.mult, ALU.bypass, [t_v, t_rt], [t_vw])
                    stt(vbw[:, 1:17, :], vb[:, 16:32, :], hm[:, 1:2], vbw[:, 1:17, :], ALU.mult, ALU.add, [t_v, t_rt], [t_vw])
                    ts("pool", vbw[:, 17, :], vb[:, 16, :], hm[:, 0:1], None, ALU.mult, ALU.bypass, [t_v, t_rt], [t_vw])
                for gq in range(3):
                    h = j * 3 + gq
                    if lat:
                        S.dma(qT[0:64, 0:LH], projT_h[1792 + h * 64: 1792 + (h + 1) * 64, 0:LH], wr=[t_qT])
                    else:
                        S.dma(qT[0:64, 0:L], projT[1792 + h * 64: 1792 + (h + 1) * 64, off:off + L], wr=[t_qT])
                    if not lat:
                        nps, dps, tn, td = banks[4], banks[5], tbk[4], tbk[5]
                        for kt in range(2):
                            sp_, tsp = banks[cnt % 2], tbk[cnt % 2]
                            cnt += 1
                            mm(sp_[:, 0:256], kT[0:64, kt * 128:(kt + 1) * 128], qT[0:64, 0:256], True, True, [t_kT, t_qT], [tsp])
                            et, t_et = etr.next()
                            act(et[:, 0:256], sp_[:, 0:256], AF.Exp, [tsp], [t_et], scale=0.125)
                            mm(nps[0:64, 0:256], vb[:, kt, :], et[:, 0:256], kt == 0, kt == 1, [t_v, t_et], [tn])
                            mm(dps[0:64, 0:256], ones_b[:, 0:64], et[:, 0:256], kt == 0, kt == 1, [tC, t_et], [td])
                        finish_o(nps, dps, tn, td, h, 256, off)
                    else:
                        rope(qT, t_qT, LH, cos_h, sin_h)
                        for qb in range(LH // 512):
                            pi_ = qb % 2
                            nps, dps, tn, td = banks[4 + pi_], banks[2 + pi_], tbk[4 + pi_], tbk[2 + pi_]
                            qs_ = slice(qb * 512, (qb + 1) * 512)
                            tasks = [("ctx", kt) for kt in range(4)] + [("win", qs) for qs in range(4)]

                            def score(task):
                                nonlocal cnt
                                kind, idx = task
                                if kind == "ctx":
                                    sp_, tsp = banks[cnt % 2], tbk[cnt % 2]
                                    cnt += 1
                                    mm(sp_[:, :], ckT[0:64, idx * 128:(idx + 1) * 128], qT[0:64, qs_], True, True, [t_ck, t_qT], [tsp])
                                    et, t_et = etr.next()
                                    act(et, sp_[:, :], AF.Exp, [tsp], [t_et], scale=0.125)
                                    return (kind, idx, et, t_et, None)
                                n_ = qb * 4 + idx
                                kts = [n_ + dk for dk in (-1, 0, 1)]
                                sp_, tsp = banks[6 + cnt % 2], tbk[6 + cnt % 2]
                                cnt += 1
                                for ii, kt in enumerate(kts):
                                    dk = kt - n_
                                    cols = slice(ii * 128, (ii + 1) * 128)
                                    mm(sp_[:, cols], kTw[0:64, (kt + 1) * 128:(kt + 2) * 128], qT[0:64, n_ * 128:(n_ + 1) * 128],
                                       True, dk == 0, [t_kw, t_qT], [tsp])
                                    if dk != 0:
                                        mm(sp_[:, cols], ident_b[:], (nmb[1] if dk == -1 else nmb[0])[:], False, True, [tC], [tsp])
                                et, t_et = etr.next()
                                nk = len(kts) * 128
                                act(et[:, 0:nk], sp_[:, 0:nk], AF.Exp, [tsp], [t_et], scale=0.125)
                                if n_ == 0:
                                    ts("dve", et[:, 0:128], et[:, 0:128], hm[:, 1:2], None, ALU.mult, ALU.bypass, [t_rt], [t_et])
                                if n_ == LH // 128 - 1:
                                    ts("dve", et[:, 256:384], et[:, 256:384], hm[:, 0:1], None, ALU.mult, ALU.bypass, [t_rt], [t_et])
                                return (kind, idx, et, t_et, kts)

                            def pv(res):
                                kind, idx, et, t_et, kts = res
                                if kind == "ctx":
                                    mm(nps[0:64, :], cvb[:, idx, :], et, idx == 0, False, [t_ck, t_et], [tn])
                                    mm(dps[0:64, :], ones_b[:, 0:64], et, idx == 0, False, [tC, t_et], [td])
                                    return
                                for ii, kt in enumerate(kts):
                                    last = (idx == 3 and ii == len(kts) - 1)
                                    mm(nps[0:64, idx * 128:(idx + 1) * 128], vbw[:, kt + 1, :], et[:, ii * 128:(ii + 1) * 128], False, last,
                                       [t_vw, t_et], [tn])
                                    mm(dps[0:64, idx * 128:(idx + 1) * 128], ones_b[:, 0:64], et[:, ii * 128:(ii + 1) * 128], False, last,
                                       [tC, t_et], [td])

                            prev_r = score(tasks[0])
                            for tk_ in tasks[1:]:
                                nxt_r = score(tk_)
                                pv(prev_r)
                                prev_r = nxt_r
                            pv(prev_r)
                            finish_o(nps, dps, tn, td, h, 512, qb * 512, dst=mixA_h)

    def phase_s5h(l, d):
        ar.reset()
        Q = 128
        t_su = Trk()
        W = [t_su]
        tbk = [Trk() for _ in range(8)]
        ta = ar.f32(16 * 64).rearrange("p (k t) -> p k t", k=16)
        tb_ = ar.f32(16 * 64).rearrange("p (k t) -> p k t", k=16)
        Bre = ar.f32(256).rearrange("p (k c) -> p k c", k=16)
        Bim = ar.f32(256).rearrange("p (k c) -> p k c", k=16)
        b1 = ar.f32(256).rearrange("p (k c) -> p k c", k=16)
        b2 = ar.f32(256).rearrange("p (k c) -> p k c", k=16)
        b3 = ar.f32(256).rearrange("p (k c) -> p k c", k=16)
        b4 = ar.f32(256).rearrange("p (k c) -> p k c", k=16)
        cn = ar.f32(4 * 128).rearrange("p (a q) -> p a q", a=4)
        pads = [ar.bf(16 * 128).rearrange("p (k c) -> p k c", k=16) for _ in range(2)]
        CTf = [ar.f32(16 * 32).rearrange("p (k c) -> p k c", k=16) for _ in range(2)]
        CTb = [ar.bf(16 * 32).rearrange("p (k c) -> p k c", k=16) for _ in range(2)]
        Kall = ar.bf(16 * 4 * 32).rearrange("p (k e c) -> p k e c", k=16, e=4)
        sc_ = ar.f32(16 * 24).rearrange("p (v k) -> p v k", v=24)
        PW = ar.f32(5 * 2 * 16).rearrange("p (e c k) -> p e c k", e=5, c=2)
        kqi = ar.i32(16)
        are, aim, ldt = sc_[:, 0, :], sc_[:, 1, :], sc_[:, 2, :]
        S.dma(are, s5_A_re[l, d].rearrange("(k q) -> q k", q=128), wr=W, allow_slow_non_contiguous=True)
        S.dma(aim, s5_A_im[l, d].rearrange("(k q) -> q k", q=128), wr=W, allow_slow_non_contiguous=True)
        lv = s5_log_dt[l, d].rearrange("(k t) -> t k", t=2)
        for g2 in range(2):
            S.dma(ldt[g2 * 64:(g2 + 1) * 64, :], lv[g2].partition_broadcast(64), wr=W, allow_slow_non_contiguous=True)
        step, a_, th, r_, kq_ = sc_[:, 3, :], sc_[:, 4, :], sc_[:, 5, :], sc_[:, 6, :], sc_[:, 7, :]
        c1, s1, lr, li = sc_[:, 8, :], sc_[:, 9, :], sc_[:, 10, :], sc_[:, 11, :]
        den, kr, ki, tmp1, tmp2 = sc_[:, 12, :], sc_[:, 13, :], sc_[:, 14, :], sc_[:, 15, :], sc_[:, 16, :]
        Cm, Sm, tmp3 = sc_[:, 17, :], sc_[:, 18, :], sc_[:, 19, :]
        r4, c4, s4, rr = sc_[:, 20, :], sc_[:, 21, :], sc_[:, 22, :], sc_[:, 23, :]
        act(step, ldt, AF.Exp, W, W)
        tt("dve", a_, are, step, ALU.mult, W, W)
        tt("dve", th, aim, step, ALU.mult, W, W)
        act(r_, a_, AF.Exp, W, W)
        ts("dve", kqi, th, 1.0 / TWO_PI, None, ALU.mult, ALU.bypass, W, W)
        cp("dve", kq_, kqi, W, W)
        stt(tmp1, kq_, -TWO_PI, th, ALU.mult, ALU.add, W, W)
        ts("dve", tmp1, tmp1, math.pi, -math.pi, ALU.min, ALU.max, W, W)
        act(s1, tmp1, AF.Sin, W, W)
        act(tmp2, tmp1, AF.Abs, W, W)
        act(c1, tmp2, AF.Sin, W, W, scale=-1.0, bias=math.pi / 2)
        tt("dve", lr, r_, c1, ALU.mult, W, W)
        tt("dve", li, r_, s1, ALU.mult, W, W)
        tt("dve", den, are, are, ALU.mult, W, W)
        tt("dve", tmp1, aim, aim, ALU.mult, W, W)
        tt("dve", den, den, tmp1, ALU.add, W, W)
        E("dve", lambda e, o=den: e.reciprocal(o, o), rd=W, wr=W)
        ts("dve", tmp1, lr, -1.0, None, ALU.add, ALU.bypass, W, W)
        tt("dve", kr, tmp1, are, ALU.mult, W, W)
        tt("dve", tmp2, li, aim, ALU.mult, W, W)
        tt("dve", kr, kr, tmp2, ALU.add, W, W)
        tt("dve", kr, kr, den, ALU.mult, W, W)
        tt("dve", ki, li, are, ALU.mult, W, W)
        tt("dve", tmp2, tmp1, aim, ALU.mult, W, W)
        tt("dve", ki, ki, tmp2, ALU.subtract, W, W)
        tt("dve", ki, ki, den, ALU.mult, W, W)

        def cmul16(o_re, o_im, a_re, a_im, b_re, b_im):
            tt("dve", tmp1, a_re, b_re, ALU.mult, W, W)
            tt("dve", tmp2, a_im, b_im, ALU.mult, W, W)
            tt("dve", tmp3, a_re, b_im, ALU.mult, W, W)
            tt("dve", rr, a_im, b_re, ALU.mult, W, W)
            tt("dve", o_re, tmp1, tmp2, ALU.subtract, W, W)
            tt("dve", o_im, tmp3, rr, ALU.add, W, W)

        memset("dve", PW[:, 0, 0, :], 1.0, W)
        memset("dve", PW[:, 0, 1, :], 0.0, W)
        cp("dve", PW[:, 1, 0, :], lr, W, W)
        cp("dve", PW[:, 1, 1, :], li, W, W)
        cmul16(PW[:, 2, 0, :], PW[:, 2, 1, :], lr, li, lr, li)
        cmul16(PW[:, 3, 0, :], PW[:, 3, 1, :], PW[:, 2, 0, :], PW[:, 2, 1, :], lr, li)
        cmul16(PW[:, 4, 0, :], PW[:, 4, 1, :], PW[:, 2, 0, :], PW[:, 2, 1, :], PW[:, 2, 0, :], PW[:, 2, 1, :])
        tt("dve", r4, r_, r_, ALU.mult, W, W)
        tt("dve", r4, r4, r4, ALU.mult, W, W)
        E("dve", lambda e: e.reciprocal(rr, r4), rd=W, wr=W)
        tt("dve", c4, PW[:, 4, 0, :], rr, ALU.mult, W, W)
        tt("dve", s4, PW[:, 4, 1, :], rr, ALU.mult, W, W)
        cosT = ar.f32(16 * Q).rearrange("p (k t) -> p k t", k=16)
        sinT = ar.f32(16 * Q).rearrange("p (k t) -> p k t", k=16)
        dec = ar.f32(16 * Q).rearrange("p (k t) -> p k t", k=16)
        dec64 = ar.f32(16 * 64).rearrange("p (k t) -> p k t", k=16)
        memset("dve", cosT[:, :, 0:1], 1.0, W)
        memset("dve", sinT[:, :, 0:1], 0.0, W)
        cp("dve", Cm, c4, W, W)
        cp("dve", Sm, s4, W, W)
        m = 1
        while m < Q:
            Cb = Cm.unsqueeze(2).to_broadcast([128, 16, m])
            Sb = Sm.unsqueeze(2).to_broadcast([128, 16, m])
            tt("dve", ta[:, :, 0:m], cosT[:, :, 0:m], Cb, ALU.mult, W, W)
            tt("dve", tb_[:, :, 0:m], sinT[:, :, 0:m], Sb, ALU.mult, W, W)
            tt("dve", cosT[:, :, m:2 * m], ta[:, :, 0:m], tb_[:, :, 0:m], ALU.subtract, W, W)
            tt("dve", ta[:, :, 0:m], sinT[:, :, 0:m], Cb, ALU.mult, W, W)
            tt("dve", tb_[:, :, 0:m], cosT[:, :, 0:m], Sb, ALU.mult, W, W)
            tt("dve", sinT[:, :, m:2 * m], ta[:, :, 0:m], tb_[:, :, 0:m], ALU.add, W, W)
            tt("dve", tmp1, Cm, Cm, ALU.mult, W, W)
            tt("dve", tmp2, Sm, Sm, ALU.mult, W, W)
            tt("dve", tmp3, Cm, Sm, ALU.mult, W, W)
            tt("dve", Cm, tmp1, tmp2, ALU.subtract, W, W)
            ts("dve", Sm, tmp3, 2.0, None, ALU.mult, ALU.bypass, W, W)
            m *= 2
        cp("dve", dec, r4.unsqueeze(2).to_broadcast([128, 16, Q]), W, W)
        cp("dve", dec64, r4.unsqueeze(2).to_broadcast([128, 16, 64]), W, W)
        if d == 0:
            memset("dve", dec[:, :, 0:1], 0.0, W)
            memset("dve", dec64[:, :, 0:1], 0.0, W)
        else:
            taf = ta.rearrange("p k t -> p (k t)")
            for T_ in (cosT, sinT):
                for k in range(16):
                    rev = bass.AP(T_.tensor, T_[:, k, Q - 1:Q].offset, [[T_.ap[0][0], 128], [-1, Q]])
                    cp("dve", taf[:, 0:Q], rev, W, W)
                    cp("dve", T_[:, k, :], taf[:, 0:Q], W, W)
            memset("dve", dec[:, :, Q - 1:Q], 0.0, W)
            memset("dve", dec64[:, :, 63:64], 0.0, W)
        S.dma(Bre, s5_B_re[l].rearrange("(k q) c -> q k c", q=128), wr=W)
        S.dma(Bim, s5_B_im[l].rearrange("(k q) c -> q k c", q=128), wr=W)
        krb = kr.unsqueeze(2).to_broadcast([128, 16, 16])
        kib = ki.unsqueeze(2).to_broadcast([128, 16, 16])
        tt("dve", b1, Bre, krb, ALU.mult, W, W)
        tt("dve", b3, Bim, kib, ALU.mult, W, W)
        tt("dve", b1, b1, b3, ALU.subtract, W, W)
        tt("dve", b2, Bim, krb, ALU.mult, W, W)
        tt("dve", b3, Bre, kib, ALU.mult, W, W)
        tt("dve", b2, b2, b3, ALU.add, W, W)
        for pz in pads:
            memset("dve", pz, 0.0, W)
        for s_ in range(4):
            e_ = (3 - s_) if d == 0 else s_
            pr_ = PW[:, e_, 0, :].unsqueeze(2).to_broadcast([128, 16, 16])
            pi_ = PW[:, e_, 1, :].unsqueeze(2).to_broadcast([128, 16, 16])
            tt("dve", b3, b1, pr_, ALU.mult, W, W)
            tt("dve", b4, b2, pi_, ALU.mult, W, W)
            for g2 in range(2):
                hs = slice(g2 * 64, (g2 + 1) * 64)
                tt("dve", pads[0][hs, :, s_ * 32 + g2 * 16: s_ * 32 + (g2 + 1) * 16], b3[hs], b4[hs], ALU.subtract, W, W)
            tt("dve", b3, b2, pr_, ALU.mult, W, W)
            tt("dve", b4, b1, pi_, ALU.mult, W, W)
            for g2 in range(2):
                hs = slice(g2 * 64, (g2 + 1) * 64)
                tt("dve", pads[1][hs, :, s_ * 32 + g2 * 16: s_ * 32 + (g2 + 1) * 16], b3[hs], b4[hs], ALU.add, W, W)
        WT = []
        for pz in pads:
            wt_ = ar.bf(16 * 128).rearrange("p (k q) -> p k q", k=16)
            for k4 in range(0, 16, 4):
                pb = banks[7][:, 0:256].bitcast(BF16)
                for kk in range(4):
                    tr(pb[:, kk * 128:(kk + 1) * 128], pz[:, k4 + kk, :], ident_b[:], W + [tC], [tbk[7]])
                cp("dve", wt_[:, k4:k4 + 4, :], pb.rearrange("p (k q) -> p k q", k=4), [tbk[7]], W)
            WT.append(wt_)
        for ci_, csrc in enumerate((s5_C_re, s5_C_im)):
            memset("dve", CTf[ci_], 0.0, W)
            cv_ = csrc[l, d].rearrange("g c p -> (g c) p").rearrange("(a q) p -> q a p", q=128)
            S.dma(cn[:, :, 0:64], cv_, wr=W)
            S.dma(cn[:, :, 64:128], cv_, wr=W)
            for a in range(4):
                E("pe", lambda e, o=banks[6][:, 0:128], i=cn[:, a, :]: e.transpose(o, i, ident_f[:]), rd=W + [tC], wr=[tbk[6]])
                for g2 in range(2):
                    src = banks[6][g2 * 64:(g2 + 1) * 64, 0:128].rearrange("p (a2 gg c) -> p a2 gg c", a2=4, gg=2)[:, :, g2, :]
                    dst = CTf[ci_][g2 * 64:(g2 + 1) * 64, a * 4:(a + 1) * 4, g2 * 16:(g2 + 1) * 16]
                    cp("dve", dst, src, [tbk[6]], W)
        cp("dve", CTb[0], CTf[0], W, W)
        ts("dve", CTb[1], CTf[1], -1.0, None, ALU.mult, ALU.bypass, W, W)
        CH = [ar.bf(16 * 128).rearrange("p (k q) -> p k q", k=16) for _ in range(2)]
        cta = ta.rearrange("p k t -> p (k t)")[:, 0:512].rearrange("p (k c) -> p k c", k=16)
        ctb = tb_.rearrange("p k t -> p (k t)")[:, 0:512].rearrange("p (k c) -> p k c", k=16)
        for tau in range(4):
            f_ = (tau + 1) if d == 0 else (4 - tau)
            pr_ = PW[:, f_, 0, :].unsqueeze(2).to_broadcast([128, 16, 32])
            pi_ = PW[:, f_, 1, :].unsqueeze(2).to_broadcast([128, 16, 32])
            tt("dve", cta, CTf[0], pr_, ALU.mult, W, W)
            tt("dve", ctb, CTf[1], pi_, ALU.mult, W, W)
            tt("dve", CH[0][:, :, tau * 32:(tau + 1) * 32], cta, ctb, ALU.subtract, W, W)
            tt("dve", cta, CTf[0], pi_, ALU.mult, W, W)
            tt("dve", ctb, CTf[1], pr_, ALU.mult, W, W)
            tt("dve", cta, cta, ctb, ALU.add, W, W)
            ts("dve", CH[1][:, :, tau * 32:(tau + 1) * 32], cta, -1.0, None, ALU.mult, ALU.bypass, W, W)
        for k4 in range(0, 16, 4):
            for kk in range(4):
                k = k4 + kk
                for e_ in range(4):
                    s_blk = (3 - e_) if d == 0 else e_
                    o_ = banks[5][0:32, kk * 128 + e_ * 32: kk * 128 + (e_ + 1) * 32]
                    mm(o_, pads[0][:, k, s_blk * 32:(s_blk + 1) * 32], CTb[0][:, k, :], True, False, W, [tbk[5]])
                    mm(o_, pads[1][:, k, s_blk * 32:(s_blk + 1) * 32], CTb[1][:, k, :], False, True, W, [tbk[5]])
            cp("dve", Kall[0:32, k4:k4 + 4, :, :], banks[5][0:32, :].rearrange("p (k e c) -> p k e c", k=4, e=4), [tbk[5]], W)
        TOE = ar.bf(16 * 128).rearrange("p (k q) -> p k q", k=16)
        t_toe = Trk()
        memset("dve", TOE, 0.0, [t_toe])
        for s_ in range(4):
            for tau in range(4):
                dl = (tau - s_) if d == 0 else (s_ - tau)
                if dl < 0:
                    continue
                S.dma(TOE[s_ * 32:(s_ + 1) * 32, :, tau * 32:(tau + 1) * 32], Kall[0:32, :, dl, :], rd=W, wr=[t_toe])
        urot = Rot([ar.bf(16 * Q).rearrange("p (k t) -> p k t", k=16) for _ in range(2)])
        ginb = Rot([ar.f32(2 * 4 * Q) for _ in range(2)])
        goutb = Rot([ar.f32(2 * 4 * Q) for _ in range(2)])
        tmpA = Rot([ar.f32(4 * Q) for _ in range(2)])
        Hfb = Rot([ar.f32(2 * 4 * Q) for _ in range(2)])
        Hbb = Rot([ar.bf(2 * 4 * (Q + 2)) for _ in range(2)])
        ysr = Rot([ar.f32(4 * Q) for _ in range(2)])
        Hin = ar.f32(32).rearrange("p (c k) -> p c k", c=2)
        carry = ar.f32(64).rearrange("p (c k) -> p c k", c=4)
        ct = ar.f32(64).rearrange("p (c k) -> p c k", c=4)
        t_car = Trk()
        dst4 = s5x4[d]
        bodyno = [0]

        def stage1(j0, n, a, u_, t_u):
            bn = bodyno[0]
            bodyno[0] += 1
            ks = slice(a * 4, (a + 1) * 4)
            bre, bim = banks[(bn % 2) * 2], banks[(bn % 2) * 2 + 1]
            tre, tim = tbk[(bn % 2) * 2], tbk[(bn % 2) * 2 + 1]
            for kq in range(4):
                k = a * 4 + kq
                mm(bre[:, kq * Q:kq * Q + n], WT[0][:, k, :], u_[:, k, 0:n], True, True, W + [t_u], [tre])
                mm(bim[:, kq * Q:kq * Q + n], WT[1][:, k, :], u_[:, k, 0:n], True, True, W + [t_u], [tim])
            br = bre[:, :].rearrange("p (k t) -> p k t", k=4)[:, :, 0:n]
            bi = bim[:, :].rearrange("p (k t) -> p k t", k=4)[:, :, 0:n]
            tsl = slice(0, n) if d == 0 else slice(Q - n, Q)
            cT_, sT_ = cosT[:, ks, tsl], sinT[:, ks, tsl]
            gb, t_gi = ginb.next()
            gi_ = gb[:, 0:2 * 4 * n].rearrange("p (c k t) -> p c k t", c=2, k=4)
            tAb, t_tA = tmpA.next()
            tA = tAb[:, 0:4 * n].rearrange("p (k t) -> p k t", k=4)
            tt("dve", gi_[:, 0, :, :], br, cT_, ALU.mult, [tre] + W, [t_gi])
            tt("dve", tA, bi, sT_, ALU.mult, [tim] + W, [t_tA])
            tt("pool", gi_[:, 0, :, :], gi_[:, 0, :, :], tA, ALU.add, [t_tA], [t_gi])
            tAb, t_tA = tmpA.next()
            tA = tAb[:, 0:4 * n].rearrange("p (k t) -> p k t", k=4)
            tt("dve", gi_[:, 1, :, :], bi, cT_, ALU.mult, [tim] + W, [t_gi])
            tt("dve", tA, br, sT_, ALU.mult, [tre] + W, [t_tA])
            tt("pool", gi_[:, 1, :, :], gi_[:, 1, :, :], tA, ALU.subtract, [t_tA], [t_gi])
            return dict(j0=j0, n=n, a=a, ks=ks, gi=gi_, t_gi=t_gi, bn=bn, cT=cT_, sT=sT_, u=u_, t_u=t_u)

        def stage2(cx):
            j0, n, a, ks, gi_, t_gi, bn, cT_, sT_, u_, t_u = (cx[k] for k in ("j0", "n", "a", "ks", "gi", "t_gi", "bn", "cT", "sT", "u", "t_u"))
            first = 0 if d == 0 else n - 1
            lastp = n - 1 if d == 0 else 0
            tt("dve", ct[:, 0, ks], Hin[:, 0, ks], c4[:, ks], ALU.mult, W + [t_car], [t_car])
            tt("dve", ct[:, 1, ks], Hin[:, 1, ks], s4[:, ks], ALU.mult, W, [t_car])
            tt("dve", ct[:, 2, ks], Hin[:, 0, ks], s4[:, ks], ALU.mult, W, [t_car])
            tt("dve", ct[:, 3, ks], Hin[:, 1, ks], c4[:, ks], ALU.mult, W, [t_car])
            tt("dve", carry[:, 0, ks], ct[:, 0, ks], ct[:, 1, ks], ALU.subtract, [t_car], [t_car])
            tt("dve", carry[:, 1, ks], ct[:, 2, ks], ct[:, 3, ks], ALU.add, [t_car], [t_car])
            for cc in range(2):
                tt("dve", carry[:, 2 + cc, ks], carry[:, cc, ks], r4[:, ks], ALU.mult, W, [t_car])
                tt("dve", gi_[:, cc, :, first], gi_[:, cc, :, first], carry[:, 2 + cc, ks], ALU.add, [t_car], [t_gi])
            gob, t_go = goutb.next()
            go = gob[:, 0:2 * 4 * n].rearrange("p (c k t) -> p c k t", c=2, k=4)
            dsrc = (dec if n == Q else dec64)[:, ks, :]
            for cc in range(2):
                fin = gi_[:, cc, :, :].rearrange("p k t -> p (k t)")
                fo = go[:, cc, :, :].rearrange("p k t -> p (k t)")
                fd = dsrc.rearrange("p k t -> p (k t)")
                if d == 1:
                    def rv(x):
                        return bass.AP(x.tensor, x[:, 4 * n - 1:4 * n].offset, [[x.ap[0][0], 128], [-1, 4 * n]])
                    fin, fo, fd = rv(fin), rv(fo), rv(fd)
                E("dve", lambda e, o=fo, a0=fd, a1=fin: e.tensor_tensor_scan(o, a0, a1, 0.0, ALU.mult, ALU.add),
                  rd=[t_gi] + W, wr=[t_go])
            hfb_, t_hf = Hfb.next()
            Hf = hfb_[:, 0:2 * 4 * n].rearrange("p (c k t) -> p c k t", c=2, k=4)
            tAb, t_tA = tmpA.next()
            tA = tAb[:, 0:4 * n].rearrange("p (k t) -> p k t", k=4)
            tt("dve", Hf[:, 0, :, :], go[:, 0, :, :], cT_, ALU.mult, [t_go] + W, [t_hf])
            tt("pool", tA, go[:, 1, :, :], sT_, ALU.mult, [t_go] + W, [t_tA])
            tt("dve", Hf[:, 0, :, :], Hf[:, 0, :, :], tA, ALU.subtract, [t_tA], [t_hf])
            tAb, t_tA = tmpA.next()
            tA = tAb[:, 0:4 * n].rearrange("p (k t) -> p k t", k=4)
            tt("dve", Hf[:, 1, :, :], go[:, 1, :, :], cT_, ALU.mult, [t_go] + W, [t_hf])
            tt("pool", tA, go[:, 0, :, :], sT_, ALU.mult, [t_go] + W, [t_tA])
            tt("dve", Hf[:, 1, :, :], Hf[:, 1, :, :], tA, ALU.add, [t_tA], [t_hf])
            hbb_, t_hb = Hbb.next()
            Hb = hbb_[:, 0:2 * 4 * (n + 2)].rearrange("p (c k t) -> p c k t", c=2, k=4)
            bcol = 0 if d == 0 else n + 1
            cp("act", Hb[:, :, :, 1:n + 1], Hf, [t_hf], [t_hb])
            cp("dve", Hb[:, :, :, bcol], Hin[:, :, ks], [t_car], [t_hb])
            cp("dve", Hin[:, :, ks], Hf[:, :, :, lastp], [t_hf], [t_car])
            sh = slice(0, n) if d == 0 else slice(2, n + 2)
            yb, tyb = banks[4 + bn % 2], tbk[4 + bn % 2]
            for kq in range(4):
                k = a * 4 + kq
                o_ = yb[:, kq * Q:kq * Q + n]
                mm(o_, CH[0][:, k, :], Hb[:, 0, kq, sh], True, False, W + [t_hb], [tyb])
                mm(o_, CH[1][:, k, :], Hb[:, 1, kq, sh], False, False, W + [t_hb], [tyb])
                mm(o_, TOE[:, k, :], u_[:, k, 0:n], False, True, [t_toe, t_u], [tyb])
            ysb, t_ys = ysr.next()
            ys = ysb[:, 0:4 * n].rearrange("p (k t) -> p k t", k=4)
            cp("act", ys, yb[:, :].rearrange("p (k t) -> p k t", k=4)[:, :, 0:n], [tyb], [t_ys])
            for tau in range(4):
                S.dma(dst4[tau, a * 128:(a + 1) * 128, j0:j0 + n].rearrange("(k i) j -> i k j", i=32),
                      ys[tau * 32:(tau + 1) * 32, :, :], rd=[t_ys], eng="act")

        for (off, L, lat, si) in SEQS:
            if lat:
                S.dma(Hin[:, 0, :], ss5re[l, d].rearrange("(k q) -> q k", q=128), wr=[t_car], allow_slow_non_contiguous=True)
                S.dma(Hin[:, 1, :], ss5im[l, d].rearrange("(k q) -> q k", q=128), wr=[t_car], allow_slow_non_contiguous=True)
            else:
                memset("dve", Hin, 0.0, [t_car])
            LJ = L // 4
            n = min(Q, LJ)
            nch = LJ // n
            order = range(nch) if d == 0 else range(nch - 1, -1, -1)
            prev = None
            for c in order:
                j0 = off // 4 + c * n
                u_, t_u = urot.next()
                for s_ in range(4):
                    S.dma(u_[s_ * 32:(s_ + 1) * 32, :, 0:n], u4[s_, :, j0:j0 + n].rearrange("(k i) j -> i k j", i=32), wr=[t_u])
                for a in range(4):
                    cx = stage1(j0, n, a, u_, t_u)
                    if prev is not None:
                        stage2(prev)
                    prev = cx
            stage2(prev)
            if not lat:
                S.dma(os5re[si, l, d].rearrange("(k q) -> q k", q=128), Hin[:, 0, :], rd=[t_car], allow_slow_non_contiguous=True)
                S.dma(os5im[si, l, d].rearrange("(k q) -> q k", q=128), Hin[:, 1, :], rd=[t_car], allow_slow_non_contiguous=True)

    def phase_s5fin(l):
        ar.reset()
        Wg = ar.bf(4 * 512).rearrange("p (a o) -> p a o", a=4)
        bg = ar.f32(4)
        Dc = ar.f32(4)
        t_w = Trk()
        S.dma(Wg, s5_w_glu[l].rearrange("(a p) o -> p a o", p=128), wr=[t_w], eng="pool")
        S.dma(bg, s5_b_glu[l].rearrange("(a p) -> p a", p=128), wr=[t_w], allow_slow_non_contiguous=True)
        S.dma(Dc, s5_D[l].rearrange("(a p) -> p a", p=128), wr=[t_w], allow_slow_non_contiguous=True)
        N = 512
        JB = 128
        fr = Rot([ar.f32(4 * N).rearrange("p (a t) -> p a t", a=4) for _ in range(2)])
        br_ = Rot([ar.f32(4 * N).rearrange("p (a t) -> p a t", a=4) for _ in range(2)])
        ur = Rot([ar.bf(4 * N).rearrange("p (a t) -> p a t", a=4) for _ in range(2)])
        y2 = ar.f32(4 * N).rearrange("p (a t) -> p a t", a=4)
        gb = Rot([ar.bf(4 * N).rearrange("p (a t) -> p a t", a=4) for _ in range(2)])
        sg = ar.f32(N)
        ob = Rot([ar.bf(4 * N).rearrange("p (a t) -> p a t", a=4) for _ in range(2)])
        t_y2, t_sg = Trk(), Trk()
        tbk = [Trk(), Trk()]
        cnt = 0
        for c in range(NJ // JB):
            j0 = c * JB
            g0 = 4 * j0
            f_, t_f = fr.next()
            b_, t_b = br_.next()
            u_, t_u = ur.next()
            for a in range(4):
                rs = slice(a * 128, (a + 1) * 128)
                S.dma(f_[:, a, :].rearrange("p (s j) -> p s j", s=4), s5x4[0][:, rs, j0:j0 + JB].rearrange("s p j -> p s j"), wr=[t_f])
                S.dma(b_[:, a, :].rearrange("p (s j) -> p s j", s=4), s5x4[1][:, rs, j0:j0 + JB].rearrange("s p j -> p s j"), wr=[t_b])
                S.dma(u_[:, a, :].rearrange("p (s j) -> p s j", s=4), u4[:, rs, j0:j0 + JB].rearrange("s p j -> p s j"), wr=[t_u])
            tt("dve", f_, f_, b_, ALU.add, [t_b], [t_f])
            tt("pool", b_, u_, Dc.unsqueeze(2).to_broadcast([128, 4, N]), ALU.mult, [t_u, t_w], [t_b])
            tt("dve", f_, f_, b_, ALU.add, [t_b], [t_f])
            tt("pool", y2, f_, f_, ALU.mult, [t_f], [t_y2])
            ts("dve", y2, y2, 0.044715, 1.0, ALU.mult, ALU.add, [t_y2], [t_y2])
            tt("dve", y2, y2, f_, ALU.mult, [t_f], [t_y2])
            act(y2, y2, AF.Sigmoid, [t_y2], [t_y2], scale=1.5957691216057308)
            tt("dve", f_, f_, y2, ALU.mult, [t_y2], [t_f])
            g_, t_g = gb.next()
            cp("act", g_, f_, [t_f], [t_g])
            o_, t_o = ob.next()
            for m in range(4):
                pk, tk = banks[cnt % 2], tbk[cnt % 2]
                cnt += 1
                for a in range(4):
                    mm(pk[:, :], Wg[:, a, m * 128:(m + 1) * 128], g_[:, a, :], a == 0, a == 3, [t_w, t_g], [tk])
                act(sg, pk[:, :], AF.Sigmoid, [tk, t_w], [t_sg], bias=bg[:, m:m + 1])
                tt("dve", o_[:, m, :].rearrange("p (j s) -> p s j", s=4), f_[:, m, :].rearrange("p (s j) -> p s j", s=4),
                   sg.rearrange("p (s j) -> p s j", s=4), ALU.mult, [t_f, t_sg], [t_o])
            S.dma(mixT[1536:2048, g0:g0 + N].rearrange("(a p) t -> p a t", p=128), o_, rd=[t_o])

    def postnorm(ysrcs, rd_y, xt, t_x, gw, gi, tmp, t_tmp, ss, t_ss, junk, t_junk):
        for nb in range(4):
            act(junk[:, 0:512], ysrcs[nb], AF.Square, rd_y, [t_junk, t_ss], accum=ss[:, nb:nb + 1])
        E("dve", lambda e: e.reduce_sum(ss[:, 4:5], ss[:, 0:4], axis=AX.X), rd=[t_ss], wr=[t_ss])
        ts("dve", ss[:, 5:6], ss[:, 4:5], 1.0 / D, 1e-6, ALU.mult, ALU.add, [t_ss], [t_ss])
        act(ss[:, 6:7], ss[:, 5:6], AF.Sqrt, [t_ss], [t_ss])
        E("dve", lambda e: e.reciprocal(ss[:, 7:8], ss[:, 6:7]), rd=[t_ss], wr=[t_ss])
        for nb in range(4):
            stt(tmp[:, nb * 512:(nb + 1) * 512], ysrcs[nb], ss[:, 7:8], gates[:, gw, gi, nb * 512:(nb + 1) * 512],
                ALU.mult, ALU.mult, list(rd_y) + [t_ss, t_gates], [t_tmp])
        tt("pool", xt, xt, tmp, ALU.add, [t_tmp, t_x], [t_x])

    def phase_C(l, t5, xsrc):
        ar.reset()
        T = 1024
        tok0 = t5 * T
        gi = 0 if t5 == 0 else 1
        mT = ar.bf(16 * T).rearrange("p (k t) -> p k t", k=16)
        t_m = Trk()
        if t5 == 0:
            for k in range(16):
                S.dma(mT[:, k, :], mixT[k * 128:(k + 1) * 128, tok0:tok0 + T], wr=[t_m])
        else:
            hm = ar.f32(2)
            t_hm = Trk()
            S.dma(hm, hmask, wr=[t_hm])
            arot = Rot([ar.bf(T) for _ in range(2)])
            brot = Rot([ar.bf(T) for _ in range(2)])
            cA = NPR * LP + (t5 - 1) * T
            for k in range(16):
                if 6 <= k < 12:
                    S.dma(mT[:, k, :], mixA_h[(k - 6) * 128:(k - 5) * 128, (t5 - 1) * T:t5 * T], wr=[t_m])
                    continue
                a_, t_a = arot.next()
                b_, t_b = brot.next()
                S.dma(a_, mixT[k * 128:(k + 1) * 128, cA:cA + T], wr=[t_a])
                S.dma(b_, mixT[k * 128:(k + 1) * 128, cA + LH:cA + LH + T], wr=[t_b])
                ts("pool", a_, a_, hm[:, 0:1], None, ALU.mult, ALU.bypass, [t_hm], [t_a])
                stt(mT[:, k, :], b_, hm[:, 1:2], a_, ALU.mult, ALU.add, [t_b, t_a, t_hm], [t_m])
        wo = ar.bf(16 * D).rearrange("p (k n) -> p k n", k=16)
        t_wo = Trk()
        for nb in range(4):
            S.dma(wo[:, :, nb * 512:(nb + 1) * 512],
                  WB["wout"][:, nb * 512:(nb + 1) * 512].rearrange("(k p) n -> p k n", p=128), wr=[t_wo])
        xrot = Rot([ar.f32(D) for _ in range(2)])
        tmp = ar.f32(D)
        t_tmp = Trk()
        junk = ar.bf(512)
        t_junk = Trk()
        ssr = Rot([ar.f32(8) for _ in range(2)])
        t_bk = [Trk() for _ in range(8)]
        for sub in range(T // 128):
            bo = (sub % 2) * 4
            for nb in range(4):
                for k in range(16):
                    mm(banks[bo + nb][:, :], mT[:, k, sub * 128:(sub + 1) * 128], wo[:, k, nb * 512:(nb + 1) * 512],
                       k == 0, k == 15, [t_m, t_wo], [t_bk[bo + nb]])
            xt, t_x = xrot.next()
            g0 = tok0 + sub * 128
            S.dma(xt, xsrc[g0:g0 + 128, :], wr=[t_x])
            ss, t_ss = ssr.next()
            postnorm([banks[bo + nb][:, :] for nb in range(4)], [t_bk[bo + nb] for nb in range(4)],
                     xt, t_x, 0, gi, tmp, t_tmp, ss, t_ss, junk, t_junk)
            S.dma(x1[g0:g0 + 128, :], xt, rd=[t_x])

    ffn_keep = {}

    def phase_F1(l, t10):
        ar.reset()
        T = 512
        tok0 = t10 * T
        gi = 0 if t10 < 2 else 1
        actT = ar.bf(44 * T).rearrange("p (k t) -> p k t", k=44)
        ffn_keep["actT"] = actT
        ffn_keep["mark"] = ar.p
        hT = ar.bf(16 * T).rearrange("p (k t) -> p k t", k=16)
        t_h = Trk()
        prenorm(x1, tok0, T, gi, 2, 3, hT, t_h)
        grot = Rot([ar.bf(16 * 256).rearrange("p (k n) -> p k n", k=16) for _ in range(2)])
        urot = Rot([ar.bf(16 * 256).rearrange("p (k n) -> p k n", k=16) for _ in range(2)])
        sgr = Rot([ar.f32(512) for _ in range(2)])
        t_pg = [Trk(), Trk()]
        t_pu = [Trk(), Trk()]
        t_act = Trk()
        cnt = 0
        for blk in range(22):
            wg, t_wg = grot.next()
            wu, t_wu = urot.next()
            S.dma(wg, WB["f1"][:, blk * 256:(blk + 1) * 256].rearrange("(k p) n -> p k n", p=128), wr=[t_wg])
            S.dma(wu, WB["f1"][:, DFF + blk * 256: DFF + (blk + 1) * 256].rearrange("(k p) n -> p k n", p=128),
                  wr=[t_wu])
            for mt in range(2):
                ft = blk * 2 + mt
                pg, tg = banks[cnt % 2], t_pg[cnt % 2]
                pu, tu = banks[2 + cnt % 2], t_pu[cnt % 2]
                for k in range(16):
                    mm(pg[:, :], wg[:, k, mt * 128:(mt + 1) * 128], hT[:, k, :], k == 0, k == 15, [t_wg, t_h], [tg])
                for k in range(16):
                    mm(pu[:, :], wu[:, k, mt * 128:(mt + 1) * 128], hT[:, k, :], k == 0, k == 15, [t_wu, t_h], [tu])
                sg, t_sg = sgr.next()
                act(sg, pg[:, :], AF.Silu, [tg], [t_sg])
                tt("dve", actT[:, ft, :], sg, pu[:, :], ALU.mult, [t_sg, tu], [t_act])
                cnt += 1

    def phase_F2(l, t10, xdst):
        T = 512
        tok0 = t10 * T
        gi = 0 if t10 < 2 else 1
        actT = ffn_keep["actT"]
        ar.p = ffn_keep["mark"]
        t_act = Trk()
        wrot = Rot([ar.bf(44 * 256).rearrange("p (k n) -> p k n", k=44) for _ in range(2)])
        stash = ar.f32(4 * D).rearrange("p (s d) -> p s d", s=4)
        t_st = [Trk() for _ in range(4)]
        t_pk = [Trk(), Trk()]
        cnt = 0
        for nb in range(8):
            wt, t_w = wrot.next()
            S.dma(wt, WB["f2"][:, nb * 256:(nb + 1) * 256].rearrange("(k p) n -> p k n", p=128), wr=[t_w])
            for sub in range(4):
                pk, tk = banks[cnt % 2], t_pk[cnt % 2]
                for k in range(44):
                    mm(pk[:, 0:256], actT[:, k, sub * 128:(sub + 1) * 128], wt[:, k, :], k == 0, k == 43,
                       [t_act, t_w], [tk])
                cp("act" if cnt % 2 == 0 else "dve", stash[:, sub, nb * 256:(nb + 1) * 256], pk[:, 0:256],
                   [tk], [t_st[sub]])
                cnt += 1
        xrot = Rot([ar.f32(D) for _ in range(1)])
        tmp = ar.f32(D)
        t_tmp = Trk()
        junk = ar.bf(512)
        t_junk = Trk()
        ssr = Rot([ar.f32(8) for _ in range(2)])
        for sub in range(4):
            xt, t_x = xrot.next()
            g0 = tok0 + sub * 128
            S.dma(xt, x1[g0:g0 + 128, :], wr=[t_x])
            ss, t_ss = ssr.next()
            postnorm([stash[:, sub, nb * 512:(nb + 1) * 512] for nb in range(4)], [t_st[sub]],
                     xt, t_x, 1, gi, tmp, t_tmp, ss, t_ss, junk, t_junk)
            S.dma(xdst[g0:g0 + 128, :], xt, rd=[t_x])

    for l in range(NLAYERS):
        layer(l)
    S.finish()
    return nc


_NC = None


def kernel(**inp):
    global _NC
    f = lambda a: np.ascontiguousarray(np.asarray(a, dtype=np.float32))
    xp, xs = f(inp["x_prompt"]), f(inp["x_sample"])
    shared = {}
    for k in ("w_ada", "b_ada", "norm_mix_pre", "norm_mix_post", "norm_ffn_pre", "norm_ffn_post", "w_in", "w_out",
              "ssd_conv_w", "ssd_conv_b", "ssd_D", "ssd_norm", "attn_sink", "s5_log_dt", "s5_C_re", "s5_C_im",
              "s5_D", "s5_w_glu", "s5_b_glu", "w_ffn_in", "w_ffn_out"):
        shared[k] = f(inp[k])
    shared["ssd_dt_bias"] = f(inp["ssd_dt_bias"]).reshape(2, 24)
    shared["ssd_A_log"] = f(inp["ssd_A_log"]).reshape(2, 24)
    shared["s5_A_re"] = f(inp["s5_A_re"]).reshape(2, 2, 2048)
    shared["s5_A_im"] = f(inp["s5_A_im"]).reshape(2, 2, 2048)
    shared["s5_B_re"] = f(inp["s5_B_re"]).reshape(2, 2048, 16)
    shared["s5_B_im"] = f(inp["s5_B_im"]).reshape(2, 2048, 16)
    in_maps = []
    for c in range(8):
        b = c // 2
        m = dict(shared)
        hf = c % 2
        m["xin"] = np.ascontiguousarray(np.concatenate([xp[4 * c:4 * c + 4].reshape(NPR * LP, D),
                                                        xs[b, hf * LH:(hf + 1) * LH]], axis=0))
        hm = np.zeros((128, 2), np.float32)
        hm[:, hf] = 1.0
        m["hmask"] = hm
        m["cvec"] = np.ascontiguousarray(np.stack([f(inp["c_ctx"]), f(inp["c"])[b]], axis=0))
        m["ckin"] = f(inp["cache_k"])[b].reshape(2, 512, 256)
        m["cvin"] = f(inp["cache_v"])[b].reshape(2, 512, 256)
        m["sssd"] = f(inp["state_ssd"])[b]
        m["ss5re"] = f(inp["state_s5_re"])[b].reshape(2, 2, 2048)
        m["ss5im"] = f(inp["state_s5_im"])[b].reshape(2, 2, 2048)
        in_maps.append(m)
    if _NC is None:
        _NC = build()
    res = run_bass_kernel_spmd(_NC, in_maps, core_ids=list(range(8)))
    R = res.results
    y_prompt = np.concatenate([R[c]["yout"][:NPR * LP].reshape(NPR, LP, D) for c in range(8)], axis=0)
    y_sample = np.stack([np.concatenate([R[2 * b]["yout"][NPR * LP:], R[2 * b + 1]["yout"][NPR * LP:]], axis=0)
                         for b in range(4)], axis=0)
    ck = np.concatenate([R[c]["ock"].reshape(NPR, 2, LP, 4, 64) for c in range(8)], axis=0)
    cv = np.concatenate([R[c]["ocv"].reshape(NPR, 2, LP, 4, 64) for c in range(8)], axis=0)
    sd = np.concatenate([R[c]["ossd"] for c in range(8)], axis=0)
    s5r = np.concatenate([R[c]["os5re"].reshape(NPR, 2, 2, 32, 64) for c in range(8)], axis=0)
    s5i = np.concatenate([R[c]["os5im"].reshape(NPR, 2, 2, 32, 64) for c in range(8)], axis=0)
    return (y_prompt.astype(np.float32), y_sample.astype(np.float32), ck.astype(np.float32), cv.astype(np.float32),
            sd.astype(np.float32), s5r.astype(np.float32), s5i.astype(np.float32))
```
